# Optimizing a Trainium2 kernel written in Bass

```python
import math
import jax
import jax.numpy as jnp
from jax import lax
import numpy as np

D_MODEL = 1024
BATCH = 16
SEQ = 2048
DEPTH = 2

HEAD_DIM = 64
N_HEADS = D_MODEL // HEAD_DIM
RET_HEADS = (3 * N_HEADS) // 8
RWKV_HEADS = (3 * N_HEADS) // 8
DIFF_HEADS = N_HEADS - RET_HEADS - RWKV_HEADS
D_RET = RET_HEADS * HEAD_DIM
D_RWKV = RWKV_HEADS * HEAD_DIM
D_DIFF = DIFF_HEADS * HEAD_DIM
DIFF_QK_DIM = HEAD_DIM // 2
PROJ_SIZES = (D_RET,) * 4 + (D_RWKV,) * 4 + (D_DIFF,) * 3
D_IN = sum(PROJ_SIZES)
RET_CHUNK = 128
ATTN_BLOCK = 128
ROPE_BASE = 10000.0
RWKV_DECAY_LORA = 64
RWKV_AAA_LORA = 64
RWKV_GATE_LORA = 128
RWKV_SHIFT_FEATS = 6
D_FF = -(-8 * D_MODEL // (3 * 256)) * 256
REL_BUCKETS = 32
REL_MAX_DIST = 128
NORM_EPS = 1e-6
RET_GN_EPS = 1e-5
RWKV_GN_EPS = 64e-5

kernel_name = 'hybrid_ret_rwkv7_diffattn_encoder'


def rms_norm(x, g):
    xf = x.astype(jnp.float32)
    y = xf * lax.rsqrt(jnp.mean(xf * xf, axis=-1, keepdims=True) + NORM_EPS)
    return (y * g.astype(jnp.float32)).astype(x.dtype)


def head_group_norm(y, w, b, eps):
    H, N = y.shape[-2:]
    mu = jnp.mean(y, axis=-1, keepdims=True)
    var = jnp.mean(jnp.square(y - mu), axis=-1, keepdims=True)
    yn = (y - mu) * lax.rsqrt(var + eps)
    return yn * w.astype(jnp.float32).reshape(H, N) + b.astype(jnp.float32).reshape(H, N)


def rotary(x, pos):
    half = x.shape[-1] // 2
    freqs = ROPE_BASE ** (-jnp.arange(half, dtype=jnp.float32) / half)
    ang = pos.astype(jnp.float32)[:, None] * freqs[None, :]
    cos = jnp.cos(ang)[None, :, None, :]
    sin = jnp.sin(ang)[None, :, None, :]
    x1, x2 = x[..., :half], x[..., half:]
    return jnp.concatenate([x1 * cos - x2 * sin, x1 * sin + x2 * cos], axis=-1)


def retention_one_direction(q, k, v, log_gamma, include_diag):
    B, T, H, D = q.shape
    C = RET_CHUNK
    nc = T // C
    qc = q.reshape(B, nc, C, H, D)
    kc = k.reshape(B, nc, C, H, D)
    vc = v.reshape(B, nc, C, H, D)
    idx = jnp.arange(C, dtype=jnp.float32)
    dist = idx[:, None] - idx[None, :]
    mask = (dist >= 0) if include_diag else (dist > 0)
    decay_intra = jnp.where(mask[None],
                            jnp.exp(jnp.maximum(dist, 0.0)[None] * log_gamma[:, None, None]), 0.0)
    scores = jnp.einsum('bnihd,bnjhd->bnhij', qc, kc) * decay_intra
    intra = jnp.einsum('bnhij,bnjhe->bnihe', scores, vc)
    zeta = jnp.exp((C - 1 - idx)[None, :] * log_gamma[:, None])
    kv = jnp.einsum('bnjhd,hj,bnjhe->nbhde', kc, zeta, vc)
    chunk_decay = jnp.exp(C * log_gamma)[None, :, None, None]

    def step(state, kv_n):
        return chunk_decay * state + kv_n, state

    _, prev_states = lax.scan(step, jnp.zeros((B, H, D, D), jnp.float32), kv)
    xi = jnp.exp((idx + 1.0)[None, :] * log_gamma[:, None])
    cross = jnp.einsum('bnihd,nbhde,hi->bnihe', qc, prev_states, xi)
    return (intra + cross).reshape(B, T, H, D)


def retention_mixer(q, k, v, g, gn_w, gn_b, pos):
    dtype = q.dtype
    B, T, _ = q.shape
    f32 = jnp.float32
    qh = rotary(q.astype(f32).reshape(B, T, RET_HEADS, HEAD_DIM), pos) * HEAD_DIM ** -0.5
    kh = rotary(k.astype(f32).reshape(B, T, RET_HEADS, HEAD_DIM), pos)
    vh = v.astype(f32).reshape(B, T, RET_HEADS, HEAD_DIM)
    log_gamma = jnp.log1p(-jnp.exp2(-5.0 - jnp.arange(RET_HEADS, dtype=f32)))
    y_fwd = retention_one_direction(qh, kh, vh, log_gamma, True)
    y_bwd = jnp.flip(retention_one_direction(jnp.flip(qh, 1), jnp.flip(kh, 1), jnp.flip(vh, 1),
                                             log_gamma, False), 1)
    y = head_group_norm(y_fwd + y_bwd, gn_w, gn_b, RET_GN_EPS).reshape(B, T, D_RET)
    return (jax.nn.silu(g.astype(f32)) * y).astype(dtype)


def bidir_token_shift(f, mu):
    prev = jnp.pad(f[:, :-1], ((0, 0), (1, 0), (0, 0)))
    nxt = jnp.pad(f[:, 1:], ((0, 0), (0, 1), (0, 0)))
    return f + mu[0] * (prev - f) + mu[1] * (nxt - f)


def rwkv7_mixer(r, k, v, z, mu, w0, w1, w2, a0, a1, a2, g1, g2, k_k, k_a, r_k, ln_w, ln_b):
    dtype = r.dtype
    B, T, _ = r.shape
    H, N = RWKV_HEADS, HEAD_DIM
    f32 = jnp.float32
    mu = mu.astype(f32)
    xr = bidir_token_shift(r.astype(f32), mu[0])
    xk = bidir_token_shift(k.astype(f32), mu[1])
    xv = bidir_token_shift(v.astype(f32), mu[2])
    zf = z.astype(f32)
    xw = bidir_token_shift(zf, mu[3])
    xa = bidir_token_shift(zf, mu[4])
    xg = bidir_token_shift(zf, mu[5])
    w_lo = jnp.einsum('dbtr,drc->dbtc', jnp.tanh(jnp.einsum('btc,dcr->dbtr', xw, w1.astype(f32))), w2.astype(f32))
    w_raw = w0.astype(f32)[:, None, None, :] + w_lo
    decay = jnp.exp(-jnp.exp(-jax.nn.softplus(-w_raw) - 0.5))
    a = jax.nn.sigmoid(a0.astype(f32)[:, None, None, :]
                       + jnp.einsum('dbtr,drc->dbtc', jnp.einsum('btc,dcr->dbtr', xa, a1.astype(f32)), a2.astype(f32)))
    gate = jnp.einsum('btr,rc->btc', jax.nn.sigmoid(jnp.einsum('btc,cr->btr', xg, g1.astype(f32))), g2.astype(f32))
    rh = xr.reshape(B, T, H, N)
    kh = xk.reshape(B, T, H, N)
    vh = xv.reshape(B, T, H, N)
    kk = kh * k_k.astype(f32).reshape(H, N)
    kk = kk / jnp.maximum(jnp.linalg.norm(kk, axis=-1, keepdims=True), 1e-12)
    a = a.reshape(2, B, T, H, N)
    decay = decay.reshape(2, B, T, H, N)
    k_dir = kh[None] * (1.0 + (a - 1.0) * k_a.astype(f32).reshape(H, N))

    def orient_pair(t):
        return jnp.stack([t[0], jnp.flip(t[1], 1)])

    def orient_shared(t):
        return jnp.stack([t, jnp.flip(t, 1)])

    kk_o = orient_shared(kk)
    seqs = (orient_pair(decay), orient_pair(k_dir), orient_shared(vh), orient_shared(rh),
            -kk_o, kk_o * orient_pair(a))
    seqs = tuple(jnp.moveaxis(s, 2, 0) for s in seqs)

    def step(S, inp):
        w_t, k_t, v_t, r_t, a_t, b_t = inp
        S = (S * w_t[..., None, :]
             + jnp.einsum('dbhvk,dbhk->dbhv', S, a_t)[..., None] * b_t[..., None, :]
             + v_t[..., :, None] * k_t[..., None, :])
        return S, jnp.einsum('dbhvk,dbhk->dbhv', S, r_t)

    _, ys = lax.scan(step, jnp.zeros((2, B, H, N, N), f32), seqs)
    ys = jnp.moveaxis(ys, 0, 2)
    y = ys[0] + jnp.flip(ys[1], 1)
    y = head_group_norm(y, ln_w, ln_b, RWKV_GN_EPS)
    bonus = jnp.sum(rh * k_dir.sum(0) * r_k.astype(f32), axis=-1, keepdims=True) * vh
    return ((y + bonus).reshape(B, T, D_RWKV) * gate).astype(dtype)


def t5_bucket(rel):
    nb = REL_BUCKETS // 2
    max_exact = nb // 2
    n = jnp.abs(rel)
    nf = jnp.maximum(n, 1).astype(jnp.float32)
    large = max_exact + (jnp.log(nf / max_exact) / math.log(REL_MAX_DIST / max_exact)
                         * (nb - max_exact)).astype(jnp.int32)
    large = jnp.minimum(large, nb - 1)
    return jnp.where(rel > 0, nb, 0) + jnp.where(n < max_exact, n, large)


def diff_attention(q, k, v, lam_vecs, subln_w, rel_bias, lambda_init):
    dtype = q.dtype
    B, T, _ = q.shape
    H, d = DIFF_HEADS, DIFF_QK_DIM
    f32 = jnp.float32
    qh = q.astype(f32).reshape(B, T, H, 2, d) * d ** -0.5
    kh = k.astype(f32).reshape(B, T, H, 2, d)
    vh = v.astype(f32).reshape(B, T, H, 2 * d)
    lv = lam_vecs.astype(f32)
    lam = jnp.exp(jnp.sum(lv[0] * lv[1])) - jnp.exp(jnp.sum(lv[2] * lv[3])) + lambda_init
    nblk = T // ATTN_BLOCK
    q_blocks = jnp.moveaxis(qh.reshape(B, nblk, ATTN_BLOCK, H, 2, d), 1, 0)
    starts = jnp.arange(nblk, dtype=jnp.int32) * ATTN_BLOCK
    k_pos = jnp.arange(T, dtype=jnp.int32)
    table = rel_bias.astype(f32)

    def one_block(args):
        qb, start = args
        q_pos = start + jnp.arange(ATTN_BLOCK, dtype=jnp.int32)
        bias = jnp.moveaxis(table[t5_bucket(k_pos[None, :] - q_pos[:, None])], -1, 0)
        s = jnp.einsum('bqhcd,bkhcd->bchqk', qb, kh) + bias[None, None]
        p = jax.nn.softmax(s, axis=-1)
        attn = p[:, 0] - lam * p[:, 1]
        return jnp.einsum('bhqk,bkhe->bqhe', attn, vh)

    out = lax.map(one_block, (q_blocks, starts))
    out = jnp.moveaxis(out, 0, 1).reshape(B, T, H, 2 * d)
    out = out * lax.rsqrt(jnp.mean(out * out, axis=-1, keepdims=True) + NORM_EPS) * subln_w.astype(f32)
    return (out * (1.0 - lambda_init)).reshape(B, T, D_DIFF).astype(dtype)


def setup_inputs(seed: int = 0) -> dict:
    key = jax.random.key(seed)
    ks = jax.random.split(key, 32)
    f32 = jnp.float32

    def nrm(k, shape, scale):
        return jax.random.normal(k, shape, f32) * scale

    def gain(k, shape):
        return 1.0 + 0.05 * jax.random.normal(k, shape, f32)

    return {
        'x': nrm(ks[0], (BATCH, SEQ, D_MODEL), 1.0),
        'mix_norm_g': gain(ks[1], (DEPTH, D_MODEL)),
        'w_in': nrm(ks[2], (DEPTH, D_MODEL, D_IN), D_MODEL ** -0.5),
        'w_out': nrm(ks[3], (DEPTH, D_RET + D_RWKV + D_DIFF, D_MODEL), D_MODEL ** -0.5),
        'ret_gn_w': gain(ks[4], (DEPTH, D_RET)),
        'ret_gn_b': nrm(ks[5], (DEPTH, D_RET), 0.02),
        'rwkv_mu': jax.random.uniform(ks[6], (DEPTH, RWKV_SHIFT_FEATS, 2, D_RWKV), f32, 0.0, 0.5),
        'rwkv_w0': jax.random.uniform(ks[7], (DEPTH, 2, D_RWKV), f32, -5.0, 0.5),
        'rwkv_w1': nrm(ks[8], (DEPTH, 2, D_RWKV, RWKV_DECAY_LORA), D_RWKV ** -0.5),
        'rwkv_w2': nrm(ks[9], (DEPTH, 2, RWKV_DECAY_LORA, D_RWKV), 0.3 * RWKV_DECAY_LORA ** -0.5),
        'rwkv_a0': nrm(ks[10], (DEPTH, 2, D_RWKV), 0.5),
        'rwkv_a1': nrm(ks[11], (DEPTH, 2, D_RWKV, RWKV_AAA_LORA), D_RWKV ** -0.5),
        'rwkv_a2': nrm(ks[12], (DEPTH, 2, RWKV_AAA_LORA, D_RWKV), 0.3 * RWKV_AAA_LORA ** -0.5),
        'rwkv_g1': nrm(ks[13], (DEPTH, D_RWKV, RWKV_GATE_LORA), D_RWKV ** -0.5),
        'rwkv_g2': nrm(ks[14], (DEPTH, RWKV_GATE_LORA, D_RWKV), RWKV_GATE_LORA ** -0.5),
        'rwkv_k_k': 0.85 + 0.05 * jax.random.normal(ks[15], (DEPTH, D_RWKV), f32),
        'rwkv_k_a': gain(ks[16], (DEPTH, D_RWKV)),
        'rwkv_r_k': nrm(ks[17], (DEPTH, RWKV_HEADS, HEAD_DIM), 0.1),
        'rwkv_ln_w': gain(ks[18], (DEPTH, D_RWKV)),
        'rwkv_ln_b': nrm(ks[19], (DEPTH, D_RWKV), 0.02),
        'diff_lambda': nrm(ks[20], (DEPTH, 4, DIFF_QK_DIM), 0.1),
        'diff_subln_w': gain(ks[21], (DEPTH, 2 * DIFF_QK_DIM)),
        'rel_bias': nrm(ks[22], (REL_BUCKETS, DIFF_HEADS), 0.5),
        'ffn_norm_g': gain(ks[23], (DEPTH, D_MODEL)),
        'w_gate': nrm(ks[24], (DEPTH, D_MODEL, D_FF), D_MODEL ** -0.5),
        'w_up': nrm(ks[25], (DEPTH, D_MODEL, D_FF), D_MODEL ** -0.5),
        'w_down': nrm(ks[26], (DEPTH, D_FF, D_MODEL), D_FF ** -0.5),
        'final_norm_g': gain(ks[27], (D_MODEL,)),
    }


def reference(x, mix_norm_g, w_in, w_out, ret_gn_w, ret_gn_b, rwkv_mu, rwkv_w0, rwkv_w1, rwkv_w2,
              rwkv_a0, rwkv_a1, rwkv_a2, rwkv_g1, rwkv_g2, rwkv_k_k, rwkv_k_a, rwkv_r_k, rwkv_ln_w,
              rwkv_ln_b, diff_lambda, diff_subln_w, rel_bias, ffn_norm_g, w_gate, w_up, w_down,
              final_norm_g):
    T = x.shape[1]
    pos = jnp.arange(T, dtype=jnp.int32)
    split_at = np.cumsum(PROJ_SIZES)[:-1].tolist()
    for l in range(DEPTH):
        h = rms_norm(x, mix_norm_g[l])
        proj = jnp.einsum('btd,de->bte', h, w_in[l])
        (rq, rk, rv, rg, wr, wk, wv, wz, dq, dk, dv) = jnp.split(proj, split_at, axis=-1)
        y_ret = retention_mixer(rq, rk, rv, rg, ret_gn_w[l], ret_gn_b[l], pos)
        y_rwkv = rwkv7_mixer(wr, wk, wv, wz, rwkv_mu[l], rwkv_w0[l], rwkv_w1[l], rwkv_w2[l],
                             rwkv_a0[l], rwkv_a1[l], rwkv_a2[l], rwkv_g1[l], rwkv_g2[l],
                             rwkv_k_k[l], rwkv_k_a[l], rwkv_r_k[l], rwkv_ln_w[l], rwkv_ln_b[l])
        lambda_init = 0.8 - 0.6 * math.exp(-0.3 * l)
        y_diff = diff_attention(dq, dk, dv, diff_lambda[l], diff_subln_w[l], rel_bias, lambda_init)
        mixed = jnp.concatenate([y_ret, y_rwkv, y_diff], axis=-1)
        x = x + jnp.einsum('bte,ed->btd', mixed, w_out[l])
        h = rms_norm(x, ffn_norm_g[l])
        hidden = jax.nn.silu(jnp.einsum('btd,df->btf', h, w_gate[l])) * jnp.einsum('btd,df->btf', h, w_up[l])
        x = x + jnp.einsum('btf,fd->btd', hidden, w_down[l])
    return rms_norm(x, final_norm_g)
```

```python
import contextlib
import numpy as np
import concourse.bass as bass
import concourse.mybir as mybir

F32 = mybir.dt.float32
BF16 = mybir.dt.bfloat16
I32 = mybir.dt.int32
ALU = mybir.AluOpType
AF = mybir.ActivationFunctionType
AX = mybir.AxisListType

EPOCH = 16000
N_DMA_SEMS = 40


class Prog:
    def __init__(self, nc, same_engine_sync=True):
        self.nc = nc
        self.stack = contextlib.ExitStack()
        self.engs = ['pe', 'act', 'dve', 'pool', 'sp']
        self.ops = {e: [] for e in self.engs}
        self.count = {e: 0 for e in self.engs}
        self.csems = {e: [] for e in self.engs}
        self.seen = {e: {} for e in self.engs}
        self.last_write = {}
        self.readers = {}
        self.dma_sems = []
        self.dma_uses = []
        self.dma_rr = 0
        self.same_engine_sync = same_engine_sync
        self.n_inst = 0
        self.out_tokens = []
        self.last_tok = {}
        self.dma_last = {}

    def sem(self, name):
        return self.stack.enter_context(self.nc.semaphore(name))

    def sbuf(self, name, shape, dtype):
        return self.stack.enter_context(self.nc.sbuf_tensor(name, list(shape), dtype))

    def psum(self, name, shape, dtype):
        return self.stack.enter_context(self.nc.psum_tensor(name, list(shape), dtype))

    def _csem(self, e, epoch):
        while len(self.csems[e]) <= epoch:
            self.csems[e].append(self.sem(f"c_{e}_{len(self.csems[e])}"))
        return self.csems[e][epoch]

    def _deps(self, reads, writes):
        deps = []
        for r in reads:
            t = self.last_write.get(r)
            if t is not None:
                deps.append(t)
        for w in writes:
            t = self.last_write.get(w)
            if t is not None:
                deps.append(t)
            for t in self.readers.get(w, {}).values():
                deps.append(t)
        return deps

    def _commit(self, token, reads, writes):
        for r in reads:
            self.readers.setdefault(r, {})[token[0]] = token
        for w in writes:
            self.last_write[w] = token
            self.readers[w] = {}

    def _waits(self, e, deps):
        need = {}
        for (src, sem_id, sem, val) in deps:
            if src == e and (e == 'pe' or not self.same_engine_sync):
                continue
            if self.seen[e].get(sem_id, 0) >= val:
                continue
            if need.get(sem_id, (None, 0))[1] < val:
                need[sem_id] = (sem, val)
        out = []
        for sem_id, (sem, val) in need.items():
            self.seen[e][sem_id] = val
            out.append((sem, val))
        return out

    def op(self, e, fn, reads=(), writes=()):
        reads = list(reads)
        writes = list(writes)
        deps = self._deps(reads, writes)
        waits = self._waits(e, deps)
        k = self.count[e]
        self.count[e] += 1
        epoch, idx = divmod(k, EPOCH)
        sem = self._csem(e, epoch)
        token = (e, (e, epoch), sem, idx + 1)
        self.last_tok[e] = token
        self._commit(token, reads, writes)

        def emit(eng, fn=fn, waits=waits, sem=sem):
            for (s, v) in waits:
                eng.wait_ge(s, v)
            fn(eng).then_inc(sem, 1)
        self.ops[e].append(emit)
        self.n_inst += 1 + len(waits)
        return token

    def dma(self, out_ap, in_ap, reads=(), writes=(), queue='sp', **kw):
        reads = list(reads)
        writes = list(writes)
        if not self.dma_sems:
            for i in range(N_DMA_SEMS):
                self.dma_sems.append(self.sem(f"dma_{i}"))
                self.dma_uses.append(0)
        si = self.dma_rr
        self.dma_rr = (self.dma_rr + 1) % N_DMA_SEMS
        sem = self.dma_sems[si]
        prev = self.dma_uses[si] * 16
        self.dma_uses[si] += 1
        target = prev + 16
        deps = self._deps(reads, writes)
        if prev > 0:
            deps.append(('dmaprev', ('dma', si), sem, prev))
        waits = self._waits(queue, deps)
        token = (f'dma{si}_{target}', ('dma', si), sem, target)
        self.dma_last[si] = token
        self._commit(token, reads, writes)

        def emit(eng, waits=waits, sem=sem, out_ap=out_ap, in_ap=in_ap, kw=kw):
            for (s, v) in waits:
                eng.wait_ge(s, v)
            eng.dma_start(out=out_ap, in_=in_ap, **kw).then_inc(sem, 16)
        self.ops[queue].append(emit)
        self.n_inst += 1 + len(waits)
        return token

    def barrier(self):
        toks = list(self.last_tok.values()) + list(self.dma_last.values())
        for e in self.engs:
            waits = self._waits(e, [t for t in toks if t[0] != e])

            def emit(eng, waits=waits):
                for (s, v) in waits:
                    eng.wait_ge(s, v)
            self.ops[e].append(emit)
            self.n_inst += len(waits)
        self.last_write = {}
        self.readers = {}

    def wait_all_outputs(self, tokens, e='sp'):
        waits = self._waits(e, tokens)

        def emit(eng, waits=waits):
            for (s, v) in waits:
                eng.wait_ge(s, v)
        self.ops[e].append(emit)

    def finish(self):
        nc = self.nc
        ops = self.ops
        with nc.Block() as block:
            @block.tensor
            def _(eng):
                for f in ops['pe']:
                    f(eng)

            @block.scalar
            def _(eng):
                for f in ops['act']:
                    f(eng)

            @block.vector
            def _(eng):
                for f in ops['dve']:
                    f(eng)

            @block.gpsimd
            def _(eng):
                for f in ops['pool']:
                    f(eng)

            @block.sync
            def _(eng):
                for f in ops['sp']:
                    f(eng)
        self.stack.close()

from concourse.bass_utils import run_bass_kernel_spmd

T = 2048
D = 1024
DIN = 3840
DFF = 2816
NFC = 22
EPS = 1e-6
LOG_E05 = 0.6065306597126334


class Arena:
    def __init__(self, ap_f32, nwords):
        self.ap = ap_f32
        self.n = nwords
        self.off = 0

    def reset(self):
        self.off = 0

    def alloc(self, shape, dtype):
        nel = 1
        for s in shape[1:]:
            nel *= s
        if dtype == BF16:
            nw = (nel + 1) // 2
        else:
            nw = nel
        nw = (nw + 1) // 2 * 2
        assert self.off + nw <= self.n, f"arena overflow {self.off}+{nw}>{self.n}"
        v = self.ap[0:shape[0], self.off:self.off + nw]
        self.off += nw
        if dtype != F32:
            v = v.bitcast(dtype)
        v = v[:, 0:nel]
        if len(shape) == 3:
            v = v.rearrange("p (a b) -> p a b", b=shape[2])
        elif len(shape) == 4:
            v = v.rearrange("p (a b c) -> p a b c", b=shape[2], c=shape[3])
        return v


def host_consts():
    c = {}
    c['c_ident'] = np.eye(128, dtype=np.float32)
    c['c_ones'] = np.ones((128, 128), np.float32)
    bd = np.zeros((128, 128), np.float32)
    bd[:64, :64] = 1
    bd[64:, 64:] = 1
    c['c_bd'] = bd
    half = 32
    freqs = (np.float32(10000.0) ** (-np.arange(half, dtype=np.float32) / np.float32(half))).astype(np.float32)
    pos = np.arange(T, dtype=np.float32)
    ang = (pos[None, :] * freqs[:, None]).astype(np.float32)
    cs = np.cos(ang).astype(np.float32)
    sn = np.sin(ang).astype(np.float32)
    rot = np.zeros((128, 2, T), np.float32)
    for p in range(128):
        rot[p, 0] = cs[p % 32]
        rot[p, 1] = -sn[p % 32] if (p % 64) < 32 else sn[p % 32]
    c['c_rot'] = rot
    d = np.arange(4096, dtype=np.int64) - 2047
    n = np.abs(d)
    nf = np.maximum(n, 1).astype(np.float32)
    lg = (np.log(nf / np.float32(8.0)) / np.float32(np.log(16.0)) * np.float32(8.0)).astype(np.float32)
    large = np.minimum(8 + lg.astype(np.int32), 15)
    bucket = np.where(d > 0, 16, 0) + np.where(n < 8, n, large)
    oh = np.zeros((32, 4096), np.float32)
    oh[bucket, np.arange(4096)] = 1.0
    oh[:, 4095] = 0.0
    c['c_onehot'] = oh
    r = np.arange(128)[:, None]
    q = np.arange(128)[None, :]
    su = (q > r).astype(np.float32)
    ui = (q >= r).astype(np.float32)
    sl = (q < r).astype(np.float32)
    li = (q <= r).astype(np.float32)
    mtr = np.zeros((128, 2, 256), np.float32)
    mtr[:, 0] = np.concatenate([su, ui], axis=1)
    mtr[:, 1] = np.concatenate([sl, li], axis=1)
    c['c_mtr'] = mtr
    ml = np.zeros((128, 2, 128), np.float32)
    ml[:, 0] = sl
    ml[:, 1] = su
    c['c_ml'] = ml
    rst = np.ones((128, 512), np.float32)
    rst[:, ::128] = 0.0
    c['c_rst'] = rst
    lgam = np.log1p(-np.exp2(-5.0 - np.arange(6, dtype=np.float32))).astype(np.float32)
    c['c_lgam'] = np.tile(lgam[None, :], (128, 1)).astype(np.float32)
    return c


CONST_SHAPES = {
    'c_ident': [128, 128], 'c_ones': [128, 128], 'c_bd': [128, 128], 'c_rot': [128, 2, T],
    'c_onehot': [32, 4096], 'c_mtr': [128, 2, 256], 'c_ml': [128, 2, 128], 'c_rst': [128, 512],
    'c_lgam': [128, 6],
}

PARAM_SHAPES = {
    'mix_norm_g': [2, 1024], 'w_in': [2, 1024, 3840], 'w_out': [2, 1024, 1024],
    'ret_gn_w': [2, 384], 'ret_gn_b': [2, 384], 'rwkv_mu': [2, 6, 2, 384], 'rwkv_w0': [2, 2, 384],
    'rwkv_w1': [2, 2, 384, 64], 'rwkv_w2': [2, 2, 64, 384], 'rwkv_a0': [2, 2, 384],
    'rwkv_a1': [2, 2, 384, 64], 'rwkv_a2': [2, 2, 64, 384], 'rwkv_g1': [2, 384, 128],
    'rwkv_g2': [2, 128, 384], 'rwkv_k_k': [2, 384], 'rwkv_k_a': [2, 384], 'rwkv_r_k': [2, 6, 64],
    'rwkv_ln_w': [2, 384], 'rwkv_ln_b': [2, 384], 'diff_lambda': [2, 4, 32], 'diff_subln_w': [2, 64],
    'rel_bias': [32, 4], 'ffn_norm_g': [2, 1024], 'w_gate': [2, 1024, 2816], 'w_up': [2, 1024, 2816],
    'w_down': [2, 2816, 1024], 'final_norm_g': [1024],
}


def build(nseq=2, depth=2, mixers=('ret', 'rwkv', 'diff'), ffn=True):
    nc = bass.Bass("TRN2", target_bir_lowering=False)
    NTOK = nseq * T
    xT_in = nc.dram_tensor("xT", [D, NTOK], F32, kind="ExternalInput").ap()
    prm = {k: nc.dram_tensor(k, shp, F32, kind="ExternalInput").ap() for k, shp in PARAM_SHAPES.items()}
    cst = {k: nc.dram_tensor(k, shp, F32, kind="ExternalInput").ap() for k, shp in CONST_SHAPES.items()}
    outT = nc.dram_tensor("outT", [D, NTOK], F32, kind="ExternalOutput").ap()
    xres = nc.dram_tensor("xres", [D, NTOK], F32).ap()
    bvec_d = nc.dram_tensor("bvec_d", [4, 4096], F32).ap()
    lw_d = nc.dram_tensor("rw_lw", [2, 384, T], F32).ap()
    as_d = nc.dram_tensor("rw_as", [2, 384, T], BF16).ap()
    gate_d = nc.dram_tensor("rw_gate", [384, T], BF16).ap()

    P = Prog(nc)
    hT = P.sbuf("hT", [128, 8, T], BF16)
    mT = P.sbuf("mT", [128, 8, T], BF16)
    stg = [P.sbuf(f"stg{i}", [128, 8, 256], F32) for i in range(2)]
    wbs = [P.sbuf(f"wb{i}", [128, 8, 256], BF16) for i in range(4)]
    ident = P.sbuf("ident", [128, 128], F32)
    ones = P.sbuf("ones", [128, 128], F32)
    bdm = P.sbuf("bdm", [128, 128], F32)
    mtr = P.sbuf("mtr", [128, 2, 256], F32)
    mlm = P.sbuf("mlm", [128, 2, 128], F32)
    rst = P.sbuf("rst", [128, 512], F32)
    lgam = P.sbuf("lgam", [128, 6], F32)
    zt = P.sbuf("zt", [128, 640], BF16)
    NPC = 256
    pp = P.sbuf("pp", [128, NPC], F32)
    ARW = 26300
    arena_t = P.sbuf("arena", [128, ARW], F32)
    A = Arena(arena_t[:], ARW)
    ps = [P.psum(f"ps{i}", [128, 512], F32) for i in range(7)]
    psb_t = P.psum("psb", [128, 1024], BF16)
    psb = psb_t[:]

    def MM(out, lhsT, rhs, start, stop, R, W):
        return P.op('pe', lambda e: e.matmul(out, lhsT, rhs, start=start, stop=stop, skip_group_check=True), R, W)

    def TRN(out, in_, idn, R, W):
        return P.op('pe', lambda e: e.transpose(out, in_, idn), R, W)

    def TT(eng, out, in0, in1, op, R, W):
        return P.op(eng, lambda e: e.tensor_tensor(out, in0, in1, op), R, W)

    def TS(eng, out, in0, s1, s2, op0, op1, R, W):
        if s2 is None:
            return P.op(eng, lambda e: e.tensor_scalar(out, in0, s1, None, op0), R, W)
        return P.op(eng, lambda e: e.tensor_scalar(out, in0, s1, s2, op0, op1), R, W)

    def STT(out, in0, scalar, in1, op0, op1, R, W):
        return P.op('dve', lambda e: e.scalar_tensor_tensor(out, in0, scalar, in1, op0, op1), R, W)

    def ACT(out, in_, func, R, W, bias=0.0, scale=1.0):
        return P.op('act', lambda e: e.activation(out, in_, func, bias=bias, scale=scale), R, W)

    def CP(eng, out, in_, R, W):
        if eng == 'act':
            return P.op('act', lambda e: e.copy(out, in_), R, W)
        return P.op(eng, lambda e: e.tensor_copy(out, in_), R, W)

    def RSUM(out, in_, R, W):
        return P.op('dve', lambda e: e.reduce_sum(out, in_, AX.X), R, W)

    def RCP(out, in_, R, W):
        return P.op('dve', lambda e: e.reciprocal(out, in_), R, W)

    def MSET(eng, ap, val, W):
        return P.op(eng, lambda e: e.memset(ap, val), (), W)

    def fm(ap2d):
        return ap2d.rearrange("(c p) n -> p c n", p=128)

    P.dma(ident[:], cst['c_ident'], writes=['ident'])
    P.dma(ones[:], cst['c_ones'], writes=['ones'])
    P.dma(bdm[:], cst['c_bd'], writes=['bdm'])
    P.dma(mtr[:], cst['c_mtr'], writes=['mtr'])
    P.dma(mlm[:], cst['c_ml'], writes=['mlm'])
    P.dma(rst[:], cst['c_rst'], writes=['rst'])
    P.dma(lgam[:], cst['c_lgam'], writes=['lgam'])
    MSET('pool', zt[:], 0.0, ['zt'])
    ppo = {}
    ppn = [0]

    def pcol(name, vec_ap, n):
        off = ppn[0]
        ppn[0] += n
        assert ppn[0] <= NPC
        P.dma(pp[:, off:off + n], vec_ap.rearrange("(c p) -> p c", p=128), writes=[f'pp_{name}'],
              allow_slow_non_contiguous=True)
        ppo[name] = off
        return off

    for l in range(depth):
        pcol(f"g1_{l}", prm['mix_norm_g'][l], 8)
        pcol(f"g2_{l}", prm['ffn_norm_g'][l], 8)
        pcol(f"rgw_{l}", prm['ret_gn_w'][l], 3)
        pcol(f"rgb_{l}", prm['ret_gn_b'][l], 3)
        for f in range(6):
            for s in range(2):
                pcol(f"mu_{l}_{f}_{s}", prm['rwkv_mu'][l, f, s], 3)
        for d in range(2):
            pcol(f"w0_{l}_{d}", prm['rwkv_w0'][l, d], 3)
            pcol(f"a0_{l}_{d}", prm['rwkv_a0'][l, d], 3)
        pcol(f"kk_{l}", prm['rwkv_k_k'][l], 3)
        pcol(f"ka_{l}", prm['rwkv_k_a'][l], 3)
        pcol(f"rk_{l}", prm['rwkv_r_k'][l].rearrange("h n -> (h n)"), 3)
        pcol(f"lnw_{l}", prm['rwkv_ln_w'][l], 3)
        pcol(f"lnb_{l}", prm['rwkv_ln_b'][l], 3)
        off = ppn[0]
        ppn[0] += 1
        sw = prm['diff_subln_w'][l].rearrange("(c p) -> p c", p=64)
        P.dma(pp[0:64, off:off + 1], sw, writes=[f'ppsa{l}'], allow_slow_non_contiguous=True)
        P.dma(pp[64:128, off:off + 1], sw, writes=[f'ppsb{l}'], allow_slow_non_contiguous=True)
        ppo[f"sub_{l}"] = off
    pcol("gf", prm['final_norm_g'], 8)
    P.barrier()

    def pc(name, c=0):
        o = ppo[name] + c
        return pp[:, o:o + 1]

    wstate = {'s': 0, 'b': 0}

    def load_w(pieces, nrc, swap_from=None):
        si = wstate['s']
        wstate['s'] = (si + 1) % 2
        bi = wstate['b']
        wstate['b'] = (bi + 1) % 4
        off = 0
        names = []
        for i, ap in enumerate(pieces):
            n = ap.shape[1]
            P.dma(stg[si][:, 0:nrc, off:off + n], fm(ap), writes=[f'stg{si}_{i}'])
            names.append(f'stg{si}_{i}')
            off += n
        CP('pool', wbs[bi][:, 0:nrc, 0:off], stg[si][:, 0:nrc, 0:off], names, [f'wb{bi}'])
        return wbs[bi], f'wb{bi}', stg[si], names

    def xview(ap2d, s):
        return fm(ap2d)[:, :, s * T:(s + 1) * T]

    def rmsnorm(src, gname, s, out_dram=None):
        A.reset()
        sq = [A.alloc([128, 512], BF16) for _ in range(2)]
        rs = A.alloc([128, 512], F32)
        onesb = A.alloc([128, 128], BF16)
        xbs = [A.alloc([128, 8, 512], F32) for _ in range(2)]
        obs = [A.alloc([128, 8, 512], F32) for _ in range(2)] if out_dram is not None else None
        CP('pool', onesb, ones[:], [], ['onesb'])
        for b in range(4):
            tk = slice(b * 512, (b + 1) * 512)
            xb = xbs[b % 2]
            xbn = f'xb{b % 2}'
            P.dma(xb, src[:, :, tk], reads=[f'xr_{s}_{dc}_{b}' for dc in range(8)], writes=[xbn])
            for c in range(8):
                ACT(sq[c % 2], xb[:, c, :], AF.Square, [xbn], [f'sq{c % 2}'])
                MM(ps[b % 2][:], onesb, sq[c % 2], c == 0, c == 7, [f'sq{c % 2}', 'onesb'], [f'ps{b % 2}'])
            ACT(rs, ps[b % 2][:], AF.Sqrt, [f'ps{b % 2}'], ['rs'], bias=EPS, scale=1.0 / D)
            RCP(rs, rs, ['rs'], ['rs'])
            for c in range(8):
                if out_dram is None:
                    STT(hT[:, c, tk], xb[:, c, :], pc(gname, c), rs, ALU.mult, ALU.mult, [xbn, 'rs', 'pp'], ['hT'])
                else:
                    STT(obs[b % 2][:, c, :], xb[:, c, :], pc(gname, c), rs, ALU.mult, ALU.mult, [xbn, 'rs', 'pp'], [f'ob{b % 2}'])
            if out_dram is not None:
                tok = P.dma(out_dram[:, :, tk], obs[b % 2], reads=[f'ob{b % 2}'])
                out_tokens.append(tok)
        P.barrier()

    out_tokens = []

    def proj_fm(psum_ap, psname, w, wname, c0, m, tk):
        for c in range(8):
            MM(psum_ap, w[:, c, c0:c0 + m], hT[:, c, tk], c == 0, c == 7, [wname, 'hT'], [psname])

    def proj_tm(psum_ap, psname, w, wname, c0, n, t):
        for c in range(8):
            MM(psum_ap, hT[:, c, t * 128:(t + 1) * 128], w[:, c, c0:c0 + n], c == 0, c == 7, [wname, 'hT'], [psname])

    def resid_add(psum_ap, psname, src, dst, s, dc, b, rt, rtname):
        tk = slice(b * 512, (b + 1) * 512)
        xn = f'xr_{s}_{dc}_{b}'
        P.dma(rt, src[:, dc, tk], reads=[xn], writes=[rtname])
        TT('dve', rt, psum_ap, rt, ALU.add, [psname, rtname], [rtname])
        P.dma(dst[:, dc, tk], rt, reads=[rtname], writes=[xn])


    def xnames(s, b):
        return [f'xr_{s}_{dc}_{b}' for dc in range(8)]

    def out_proj(l, s, src, dst):
        A.reset()
        rts = [A.alloc([128, 512], F32) for _ in range(4)]
        k = 0
        for g in range(4):
            w, wn, _, _ = load_w([prm['w_out'][l][:, g * 256:(g + 1) * 256]], 8)
            for dci in range(2):
                dc = g * 2 + dci
                for b in range(4):
                    tk = slice(b * 512, (b + 1) * 512)
                    pb = ps[k % 4]
                    pn = f'ps{k % 4}'
                    for c in range(8):
                        MM(pb[:], w[:, c, dci * 128:(dci + 1) * 128], mT[:, c, tk], c == 0, c == 7, [wn, 'mT'], [pn])
                    resid_add(pb[:], pn, src, dst, s, dc, b, rts[k % 4], f'rt{k % 4}')
                    k += 1
        P.barrier()

    def ffn_phase(l, s, xr):
        TB = 2048
        for tb in range(T // TB):
            A.reset()
            hid = A.alloc([128, NFC, TB], BF16)
            sg = [A.alloc([128, 512], F32) for _ in range(2)]
            rts = [A.alloc([128, 512], F32) for _ in range(3)]
            k = 0
            for fc in range(NFC):
                if fc % 2 == 0:
                    wg_, wgn, _, _ = load_w([prm['w_gate'][l][:, fc * 128:(fc + 2) * 128]], 8)
                    wu_, wun, _, _ = load_w([prm['w_up'][l][:, fc * 128:(fc + 2) * 128]], 8)
                fo = (fc % 2) * 128
                for sb in range(TB // 512):
                    tk = slice(tb * TB + sb * 512, tb * TB + (sb + 1) * 512)
                    hk = slice(sb * 512, (sb + 1) * 512)
                    pg, pgn = ps[(2 * k) % 4], f'ps{(2 * k) % 4}'
                    pu, pun = ps[(2 * k + 1) % 4], f'ps{(2 * k + 1) % 4}'
                    proj_fm(pg[:], pgn, wg_, wgn, fo, 128, tk)
                    proj_fm(pu[:], pun, wu_, wun, fo, 128, tk)
                    ACT(sg[k % 2], pg[:], AF.Silu, [pgn], [f'sg{k % 2}'])
                    TT('dve', hid[:, fc, hk], pu[:], sg[k % 2], ALU.mult, [pun, f'sg{k % 2}'], ['hid'])
                    k += 1
            k = 0
            for g in range(4):
                ws = []
                for (r0, nrc) in ((0, 8), (8, 8), (16, 6)):
                    w, wn, _, _ = load_w([prm['w_down'][l][r0 * 128:(r0 + nrc) * 128, g * 256:(g + 1) * 256]], nrc)
                    ws.append((w, wn, r0, nrc))
                for dci in range(2):
                    dc = g * 2 + dci
                    for sb in range(TB // 512):
                        b = tb * (TB // 512) + sb
                        hk = slice(sb * 512, (sb + 1) * 512)
                        pb, pn = ps[4 + k % 3], f'ps{4 + k % 3}'
                        for (w, wn, r0, nrc) in ws:
                            for c in range(nrc):
                                fcx = r0 + c
                                MM(pb[:], w[:, c, dci * 128:(dci + 1) * 128], hid[:, fcx, hk], fcx == 0, fcx == NFC - 1,
                                   [wn, 'hid'], [pn])
                        resid_add(pb[:], pn, xr, xr, s, dc, b, rts[k % 3], f'rt{k % 3}')
                        k += 1
            P.barrier()

    def gn_alloc(nq):
        return {'sq': A.alloc([128, nq, 64], F32), 's1': A.alloc([128, nq], F32), 's2': A.alloc([128, nq], F32),
                'm2': A.alloc([128, nq], F32)}

    def gn_stats(y, yname, nq, eps, center, g):
        sq, s1, s2, m2 = g['sq'], g['s1'], g['s2'], g['m2']
        if center:
            RSUM(s1, y, [yname], ['gs1'])
            TS('dve', s1, s1, 1.0 / 64, None, ALU.mult, None, ['gs1'], ['gs1'])
            TT('dve', y, y, s1.unsqueeze(2).to_broadcast([128, nq, 64]), ALU.subtract, [yname, 'gs1'], [yname])
        TT('dve', sq, y, y, ALU.mult, [yname], ['gsq'])
        RSUM(s2, sq, ['gsq'], ['gs2'])
        ACT(m2, s2, AF.Sqrt, ['gs2'], ['gm2'], bias=eps, scale=1.0 / 64)
        RCP(m2, m2, ['gm2'], ['gm2'])
        TT('dve', y, y, m2.unsqueeze(2).to_broadcast([128, nq, 64]), ALU.mult, [yname, 'gm2'], [yname])

    def retention_phase(l, s):
        W = prm['w_in'][l]
        for hp in range(3):
            A.reset()
            rot = A.alloc([128, 2, T], F32)
            qT = A.alloc([128, T], BF16)
            kT = A.alloc([128, T], BF16)
            gT = A.alloc([128, T], BF16)
            vtm = A.alloc([128, 16, 128], BF16)
            t1 = A.alloc([128, 512], F32)
            t2 = A.alloc([128, 512], F32)
            wsw = A.alloc([128, 8, 256], BF16)
            P.dma(rot, cst['c_rot'], writes=['rot'])
            c0 = hp * 128
            w, wn, sg_, sgn = load_w([W[:, c0:c0 + 128], W[:, 384 + c0:384 + c0 + 128]], 8)
            for j in range(8):
                src0 = j * 32
                dst0 = (j ^ 1) * 32
                CP('pool', wsw[:, :, dst0:dst0 + 32], sg_[:, :, src0:src0 + 32], sgn, ['wsw'])
            for which, dstT in ((0, qT), (1, kT)):
                for b in range(4):
                    tk = slice(b * 512, (b + 1) * 512)
                    proj_fm(ps[0][:], 'ps0', w, wn, which * 128, 128, tk)
                    proj_fm(ps[1][:], 'ps1', wsw, 'wsw', which * 128, 128, tk)
                    TT('dve', t1, ps[0][:], rot[:, 0, tk], ALU.mult, ['ps0', 'rot'], ['t1'])
                    TT('dve', t2, ps[1][:], rot[:, 1, tk], ALU.mult, ['ps1', 'rot'], ['t2'])
                    TT('pool', dstT[:, tk], t1, t2, ALU.add, ['t1', 't2'], ['qkT'])
            w2, wn2, _, _ = load_w([W[:, 768 + c0:768 + c0 + 128], W[:, 1152 + c0:1152 + c0 + 128]], 8)
            for t in range(16):
                pb, pn = ps[t % 2], f'ps{t % 2}'
                proj_tm(pb[:, 0:128], pn, w2, wn2, 0, 128, t)
                CP('act', vtm[:, t, :], pb[:, 0:128], [pn], ['vtm'])
            for b in range(4):
                tk = slice(b * 512, (b + 1) * 512)
                pb, pn = ps[2 + b % 2], f'ps{2 + b % 2}'
                proj_fm(pb[:], pn, w2, wn2, 128, 128, tk)
                ACT(gT[:, tk], pb[:], AF.Silu, [pn], ['gT'])
            mk = A.alloc([128, 3968], F32)
            pTs = [A.alloc([128, 512], BF16) for _ in range(5)]
            ypair = A.alloc([128, 16, 128], F32)
            tmp = A.alloc([128, 512], F32)
            gsc = gn_alloc(4)
            for hh in range(2):
                h = hp * 2 + hh
                hb = hh * 64
                P.op('pool', lambda e: e.iota(mk, [[1, 3968]], base=-1920, channel_multiplier=-1,
                                              allow_small_or_imprecise_dtypes=True), (), ['mk'])
                ACT(mk, mk, AF.Abs, ['mk'], ['mk'])
                ACT(mk, mk, AF.Exp, ['mk', 'lgam'], ['mk'], bias=float(np.log(0.125)), scale=lgam[:, h:h + 1])
                for qb in range(4):
                    q0 = qb * 512
                    MM(ps[4][:], zt[:, 0:128], zt[:, 128:640], True, True, ['zt'], ['ps4'])
                    NK = 16

                    def score(kt):
                        bi_ = (0, 1, 2, 3, 6)[kt % 5]
                        sb_, sn = ps[bi_], f'ps{bi_}'
                        MM(sb_[:], kT[hb:hb + 64, kt * 128:(kt + 1) * 128], qT[hb:hb + 64, q0:q0 + 512], True, True,
                           ['qkT'], [sn])
                        off = q0 - kt * 128 + 1920
                        TT('dve', pTs[kt % 5], sb_[:], mk[:, off:off + 512], ALU.mult, [sn, 'mk'], [f'pT{kt % 5}'])

                    def pv(kt):
                        for qi in range(4):
                            MM(ps[4][:, qi * 64:(qi + 1) * 64], pTs[kt % 5][:, qi * 128:(qi + 1) * 128],
                               vtm[:, kt, hb:hb + 64], False, kt == NK - 1 and qi == 3, [f'pT{kt % 5}', 'vtm'], ['ps4'])
                    score(0)
                    score(1)
                    score(2)
                    score(3)
                    for kt in range(NK):
                        if kt + 4 < NK:
                            score(kt + 4)
                        pv(kt)
                    yv = ypair[:, qb * 4:(qb + 1) * 4, hb:hb + 64]
                    CP('act', yv, ps[4][:, 0:256].rearrange("p (a b) -> p a b", b=64), ['ps4'], ['ypair'])
                    gn_stats(yv, 'ypair', 4, 1e-5, True, gsc)
            for qb in range(4):
                q0 = qb * 512
                for qi in range(4):
                    TRN(ps[5][:, qi * 128:(qi + 1) * 128], ypair[:, qb * 4 + qi, :], ident[:], ['ypair'], ['ps5'])
                TS('dve', tmp, ps[5][:], pp[:, ppo[f'rgw_{l}'] + hp:ppo[f'rgw_{l}'] + hp + 1],
                   pp[:, ppo[f'rgb_{l}'] + hp:ppo[f'rgb_{l}'] + hp + 1], ALU.mult, ALU.add, ['ps5'], ['tmp'])
                TT('pool', mT[:, hp, q0:q0 + 512], tmp, gT[:, q0:q0 + 512], ALU.mult, ['tmp', 'gT'], ['mT'])
            P.barrier()

    def diff_setup():
        A.reset()
        oh = A.alloc([32, 4096], F32)
        rb = A.alloc([32, 4], F32)
        bv = A.alloc([4, 4096], F32)
        P.dma(oh, cst['c_onehot'], writes=['oh'])
        P.dma(rb, prm['rel_bias'], writes=['rb'])
        for j in range(8):
            MM(ps[0][0:4, :], rb, oh[:, j * 512:(j + 1) * 512], True, True, ['oh', 'rb'], ['ps0'])
            CP('dve', bv[:, j * 512:(j + 1) * 512], ps[0][0:4, :], ['ps0'], ['bv'])
        P.dma(bvec_d, bv, reads=['bv'], writes=['bvec_d'])
        P.barrier()

    def diff_phase(l, s):
        W = prm['w_in'][l]
        lam_init = 0.8 - 0.6 * float(np.exp(-0.3 * l))
        A.reset()
        lvt = A.alloc([128, 4, 32], F32)
        lp = A.alloc([128, 2, 32], F32)
        ls = A.alloc([128, 2], F32)
        nlam = A.alloc([128, 1], F32)
        P.dma(lvt, bass.AP(prm['diff_lambda'].tensor, l * 128, [[0, 128], [32, 4], [1, 32]]), writes=['lvt'])
        TT('dve', lp[:, 0, :], lvt[:, 0, :], lvt[:, 1, :], ALU.mult, ['lvt'], ['lp'])
        TT('dve', lp[:, 1, :], lvt[:, 2, :], lvt[:, 3, :], ALU.mult, ['lvt'], ['lp'])
        RSUM(ls, lp, ['lp'], ['ls'])
        ACT(ls, ls, AF.Exp, ['ls'], ['ls'])
        TT('dve', nlam, ls[:, 1:2], ls[:, 0:1], ALU.subtract, ['ls'], ['nlam'])
        TS('dve', nlam, nlam, -lam_init, None, ALU.add, None, ['nlam'], ['nlam'])
        qc = [A.alloc([128, T], BF16) for _ in range(2)]
        kT = A.alloc([128, T], BF16)
        vaug = A.alloc([128, 16, 65], BF16)
        MSET('pool', qc[0], 0.0, ['qkT'])
        MSET('pool', qc[1], 0.0, ['qkT'])
        MSET('pool', kT[64:128, :], 0.0, ['qkT'])
        bm = A.alloc([128, 3968], F32)
        tmps = [A.alloc([128, 512], F32) for _ in range(4)]
        pTs = [A.alloc([128, 512], BF16) for _ in range(4)]
        opair = A.alloc([128, 16, 128], F32)
        o1 = A.alloc([128, 4, 64], F32)
        rr = A.alloc([128, 2, 4], F32)
        gsc = gn_alloc(4)
        MSET('pool', vaug[:, :, 64:65], 1.0, ['vaug'])
        for h in range(4):
            hb = (h % 2) * 64
            mc = 6 + h // 2
            w, wn, _, _ = load_w([W[:, 3072 + h * 64:3072 + (h + 1) * 64], W[:, 3328 + h * 64:3328 + (h + 1) * 64],
                                  W[:, 3584 + h * 64:3584 + (h + 1) * 64]], 8)
            P.dma(bm, bass.AP(bvec_d.tensor, h * 4096, [[1, 128], [1, 3968]]), reads=['bvec_d'], writes=['bm'])
            for which in (0, 1):
                for b in range(4):
                    tk = slice(b * 512, (b + 1) * 512)
                    pb, pn = ps[b % 2], f'ps{b % 2}'
                    proj_fm(pb[0:64, :], pn, w, wn, which * 64, 64, tk)
                    if which == 1:
                        CP('act', kT[0:64, tk], pb[0:64, :], [pn], ['qkT'])
                    else:
                        CP('act', qc[0][0:32, tk], pb[0:32, :], [pn], ['qkT'])
                        CP('act', qc[1][32:64, tk], pb[32:64, :], [pn], ['qkT'])
            for t in range(16):
                pb, pn = ps[t % 2], f'ps{t % 2}'
                proj_tm(pb[:, 0:64], pn, w, wn, 128, 64, t)
                CP('act', vaug[:, t, 0:64], pb[:, 0:64], [pn], ['vaug'])
            for qb in range(4):
                q0 = qb * 512
                MM(ps[4][:], zt[:, 0:128], zt[:, 128:640], True, True, ['zt'], ['ps4'])
                MM(ps[5][:], zt[:, 0:128], zt[:, 128:640], True, True, ['zt'], ['ps5'])
                NS = 32

                def score(i):
                    kt, c = divmod(i, 2)
                    sb_, sn = ps[i % 4], f'ps{i % 4}'
                    MM(sb_[:], kT[:, kt * 128:(kt + 1) * 128], qc[c][:, q0:q0 + 512], True, True, ['qkT'], [sn])
                    j0 = kt * 128 - q0 + 2047
                    bview = bm[:, j0 - 511:j0 + 1][:, ::-1]
                    STT(tmps[i % 4], sb_[:], float(32 ** -0.5), bview, ALU.mult, ALU.add, [sn, 'bm'], [f'tm{i % 4}'])
                    ACT(pTs[i % 4], tmps[i % 4], AF.Exp, [f'tm{i % 4}'], [f'pT{i % 4}'])

                def pv(i):
                    kt, c = divmod(i, 2)
                    acc, an = ps[4 + c], f'ps{4 + c}'
                    for qi in range(4):
                        MM(acc[:, qi * 65:(qi + 1) * 65], pTs[i % 4][:, qi * 128:(qi + 1) * 128], vaug[:, kt, :],
                           False, i >= NS - 2 and qi == 3, [f'pT{i % 4}', 'vaug'], [an])
                score(0)
                score(1)
                score(2)
                for i in range(NS):
                    if i + 3 < NS:
                        score(i + 3)
                    pv(i)
                a0 = ps[4][:, 0:260].rearrange("p (a b) -> p a b", b=65)
                a1 = ps[5][:, 0:260].rearrange("p (a b) -> p a b", b=65)
                RCP(rr[:, 0, :], a0[:, :, 64], ['ps4'], ['rr'])
                RCP(rr[:, 1, :], a1[:, :, 64], ['ps5'], ['rr'])
                TS('dve', rr[:, 1, :], rr[:, 1, :], nlam, None, ALU.mult, None, ['rr', 'nlam'], ['rr'])
                o0 = opair[:, qb * 4:(qb + 1) * 4, hb:hb + 64]
                TT('dve', o0, a0[:, :, 0:64], rr[:, 0, :].unsqueeze(2).to_broadcast([128, 4, 64]), ALU.mult, ['ps4', 'rr'], ['opair'])
                TT('dve', o1, a1[:, :, 0:64], rr[:, 1, :].unsqueeze(2).to_broadcast([128, 4, 64]), ALU.mult, ['ps5', 'rr'], ['o1'])
                TT('dve', o0, o0, o1, ALU.add, ['opair', 'o1'], ['opair'])
                gn_stats(o0, 'opair', 4, EPS, False, gsc)
            if h % 2 == 1:
                so = ppo[f'sub_{l}']
                for qb in range(4):
                    q0 = qb * 512
                    for qi in range(4):
                        TRN(ps[6][:, qi * 128:(qi + 1) * 128], opair[:, qb * 4 + qi, :], ident[:], ['opair'], ['ps6'])
                    TS('dve', mT[:, mc, q0:q0 + 512], ps[6][:], pp[:, so:so + 1], 1.0 - lam_init,
                       ALU.mult, ALU.mult, ['ps6'], ['mT'])
        P.barrier()

    def rwkv_phase(l, s):
        W = prm['w_in'][l]
        base = 1536
        A.reset()
        Fz = A.alloc([128, 3, T + 2], F32)
        xs = A.alloc([128, 3, T], BF16)
        hw = [A.alloc([128, T], BF16) for _ in range(2)]
        ha = [A.alloc([128, T], BF16) for _ in range(2)]
        hg = A.alloc([128, T], BF16)
        w1b = A.alloc([128, 2, 3, 64], BF16)
        a1b = A.alloc([128, 2, 3, 64], BF16)
        g1b = A.alloc([128, 3, 128], BF16)
        w2b = A.alloc([64, 2, 384], BF16)
        a2b = A.alloc([64, 2, 384], BF16)
        g2b = A.alloc([128, 384], BF16)
        lst = A.alloc([128, 1536], F32)
        cm = A.alloc([128, 6, 3], F32)
        ka1 = A.alloc([128, 3], F32)
        ka2 = A.alloc([128, 3], F32)
        t1 = A.alloc([128, 512], F32)
        t2 = A.alloc([128, 512], F32)
        tsh = [A.alloc([128, 512], F32) for _ in range(2)]
        lwt = [A.alloc([128, 512], F32) for _ in range(2)]
        ast = [A.alloc([128, 512], BF16) for _ in range(2)]
        gtt = [A.alloc([128, 512], BF16) for _ in range(2)]
        for d in range(2):
            P.dma(lst[:, 0:192].rearrange("p (c r) -> p c r", r=64), fm(prm['rwkv_w1'][l, d]), writes=['lst'])
            CP('pool', w1b[:, d], lst[:, 0:192].rearrange("p (c r) -> p c r", r=64), ['lst'], ['lw8'])
            P.dma(lst[:, 192:384].rearrange("p (c r) -> p c r", r=64), fm(prm['rwkv_a1'][l, d]), writes=['lst2'])
            CP('pool', a1b[:, d], lst[:, 192:384].rearrange("p (c r) -> p c r", r=64), ['lst2'], ['lw8'])
            P.dma(lst[0:64, 384:768], prm['rwkv_w2'][l, d], writes=['lst3'])
            CP('pool', w2b[:, d, :], lst[0:64, 384:768], ['lst3'], ['lw8'])
            P.dma(lst[0:64, 768:1152], prm['rwkv_a2'][l, d], writes=['lst4'])
            CP('pool', a2b[:, d, :], lst[0:64, 768:1152], ['lst4'], ['lw8'])
        P.dma(lst[:, 0:384].rearrange("p (c r) -> p c r", r=128), fm(prm['rwkv_g1'][l]), writes=['lst', 'lst2'])
        CP('pool', g1b, lst[:, 0:384].rearrange("p (c r) -> p c r", r=128), ['lst', 'lst2'], ['lw8'])
        P.dma(lst[:, 1152:1536], prm['rwkv_g2'][l], writes=['lst5'])
        CP('pool', g2b, lst[:, 1152:1536], ['lst5'], ['lw8'])
        for f in range(6):
            o0 = ppo[f'mu_{l}_{f}_0']
            o1 = ppo[f'mu_{l}_{f}_1']
            TT('dve', cm[:, f, :], pp[:, o0:o0 + 3], pp[:, o1:o1 + 3], ALU.add, [], ['cm'])
        TS('dve', cm, cm, -1.0, 1.0, ALU.mult, ALU.add, ['cm'], ['cm'])
        oka = ppo[f'ka_{l}']
        TS('dve', ka1, pp[:, oka:oka + 3], -1.0, 1.0, ALU.mult, ALU.add, [], ['ka1'])
        TS('dve', ka2, pp[:, oka:oka + 3], -2.0, 2.0, ALU.mult, ALU.add, [], ['ka2'])

        def shift_mix(Fv, f, j, dst, dname, fname):
            o0 = ppo[f'mu_{l}_{f}_0'] + j
            o1 = ppo[f'mu_{l}_{f}_1'] + j
            for b in range(4):
                b0 = b * 512
                TS('dve', t1, Fv[:, 1 + b0:1 + b0 + 512], cm[:, f, j:j + 1], None, ALU.mult, None, [fname, 'cm'], ['t1'])
                STT(t2, Fv[:, b0:b0 + 512], pp[:, o0:o0 + 1], t1, ALU.mult, ALU.add, [fname, 't1'], ['t2'])
                STT(dst[:, b0:b0 + 512], Fv[:, 2 + b0:2 + b0 + 512], pp[:, o1:o1 + 1], t2, ALU.mult, ALU.add,
                    [fname, 't2'], [dname])

        def proj_to_F(Fv, fname, w, wn, c0):
            for b in range(4):
                tk = slice(b * 512, (b + 1) * 512)
                pb, pn = ps[b % 2], f'ps{b % 2}'
                proj_fm(pb[:], pn, w, wn, c0, 128, tk)
                CP('act', Fv[:, 1 + b * 512:1 + (b + 1) * 512], pb[:], [pn], [fname])

        MSET('pool', Fz[:, :, 0:1], 0.0, ['Fz'])
        MSET('pool', Fz[:, :, T + 1:T + 2], 0.0, ['Fz'])
        for j in range(3):
            w, wn, _, _ = load_w([W[:, base + 1152 + j * 128:base + 1152 + (j + 1) * 128]], 8)
            proj_to_F(Fz[:, j, :], 'Fz', w, wn, 0)
        for (f, kind) in ((3, 'w'), (4, 'a'), (5, 'g')):
            for j in range(3):
                shift_mix(Fz[:, j, :], f, j, xs[:, j, :], 'xs', 'Fz')
            for b in range(4):
                tk = slice(b * 512, (b + 1) * 512)
                if kind == 'g':
                    for j in range(3):
                        MM(ps[2][:], g1b[:, j, :], xs[:, j, tk], j == 0, j == 2, ['xs', 'lw8'], ['ps2'])
                    ACT(hg[:, tk], ps[2][:], AF.Sigmoid, ['ps2'], ['hg'])
                else:
                    for d in range(2):
                        pb, pn = ps[2 + d], f'ps{2 + d}'
                        wl = w1b if kind == 'w' else a1b
                        for j in range(3):
                            MM(pb[0:64, :], wl[:, d, j, :], xs[:, j, tk], j == 0, j == 2, ['xs', 'lw8'], [pn])
                        if kind == 'w':
                            ACT(hw[d][0:64, tk], pb[0:64, :], AF.Tanh, [pn], ['hw'])
                        else:
                            CP('act', ha[d][0:64, tk], pb[0:64, :], [pn], ['ha'])
        k = 0
        for j in range(3):
            cj = slice(j * 128, (j + 1) * 128)
            for b in range(4):
                tk = slice(b * 512, (b + 1) * 512)
                for d in range(2):
                    pb, pn = ps[k % 4], f'ps{k % 4}'
                    MM(pb[:], w2b[:, d, cj], hw[d][0:64, tk], True, True, ['hw', 'lw8'], [pn])
                    ACT(lwt[k % 2], pb[:], AF.Sigmoid, [pn], [f'lwt{k % 2}'], bias=pc(f'w0_{l}_{d}', j))
                    TS('dve', lwt[k % 2], lwt[k % 2], -LOG_E05, None, ALU.mult, None, [f'lwt{k % 2}'], [f'lwt{k % 2}'])
                    P.dma(lw_d[d, cj, tk], lwt[k % 2], reads=[f'lwt{k % 2}'], writes=[f'lw_{d}_{j}_{b}'])
                    k += 1
                    pb, pn = ps[k % 4], f'ps{k % 4}'
                    MM(pb[:], a2b[:, d, cj], ha[d][0:64, tk], True, True, ['ha', 'lw8'], [pn])
                    ACT(ast[k % 2], pb[:], AF.Sigmoid, [pn], [f'ast{k % 2}'], bias=pc(f'a0_{l}_{d}', j))
                    P.dma(as_d[d, cj, tk], ast[k % 2], reads=[f'ast{k % 2}'], writes=[f'as_{d}_{j}_{b}'])
                    k += 1
                pb, pn = ps[k % 4], f'ps{k % 4}'
                MM(pb[:], g2b[:, cj], hg[:, tk], True, True, ['hg', 'lw8'], [pn])
                CP('act', gtt[k % 2], pb[:], [pn], [f'gtt{k % 2}'])
                P.dma(gate_d[cj, tk], gtt[k % 2], reads=[f'gtt{k % 2}'], writes=[f'gate_{j}_{b}'])
                k += 1
        P.barrier()

        for j in range(3):
            cj = slice(j * 128, (j + 1) * 128)
            A.reset()
            xr = A.alloc([128, T], BF16)
            xk = A.alloc([128, T], BF16)
            xv = A.alloc([128, T], BF16)
            Vtm = A.alloc([128, 16, 128], BF16)
            ysum = A.alloc([128, 16, 128], F32)
            kkn = A.alloc([128, T], BF16)
            bonus = A.alloc([128, T], BF16)
            ka1 = A.alloc([128, 3], F32)
            identb = A.alloc([128, 128], BF16)
            gsc = gn_alloc(8)
            gtile = A.alloc([128, 512], BF16)
            te1 = A.alloc([128, 512], F32)
            mark = A.off
            Fv = A.alloc([128, T + 2], F32)
            cm = A.alloc([128, 6, 3], F32)
            ka2 = A.alloc([128, 3], F32)
            t1 = A.alloc([128, 512], F32)
            t2 = A.alloc([128, 512], F32)
            tsh = [A.alloc([128, 512], F32) for _ in range(2)]
            as0 = A.alloc([128, 512], BF16)
            as1 = A.alloc([128, 512], BF16)
            CP('pool', identb, ident[:], [], ['identb'])
            for f in range(6):
                o0 = ppo[f'mu_{l}_{f}_0']
                o1 = ppo[f'mu_{l}_{f}_1']
                TT('dve', cm[:, f, :], pp[:, o0:o0 + 3], pp[:, o1:o1 + 3], ALU.add, [], ['cm'])
            TS('dve', cm, cm, -1.0, 1.0, ALU.mult, ALU.add, ['cm'], ['cm'])
            TS('dve', ka1, pp[:, oka:oka + 3], -1.0, 1.0, ALU.mult, ALU.add, [], ['ka1'])
            TS('dve', ka2, pp[:, oka:oka + 3], -2.0, 2.0, ALU.mult, ALU.add, [], ['ka2'])
            MSET('pool', Fv[:, 0:1], 0.0, ['Fv'])
            MSET('pool', Fv[:, T + 1:T + 2], 0.0, ['Fv'])
            w, wn, _, _ = load_w([W[:, base + j * 128:base + (j + 1) * 128],
                                  W[:, base + 384 + j * 128:base + 384 + (j + 1) * 128]], 8)
            wv_, wvn, _, _ = load_w([W[:, base + 768 + j * 128:base + 768 + (j + 1) * 128]], 8)
            proj_to_F(Fv, 'Fv', w, wn, 0)
            shift_mix(Fv, 0, j, xr, 'xr', 'Fv')
            proj_to_F(Fv, 'Fv', w, wn, 128)
            shift_mix(Fv, 1, j, xk, 'xk', 'Fv')
            proj_to_F(Fv, 'Fv', wv_, wvn, 0)
            shift_mix(Fv, 2, j, xv, 'xv', 'Fv')
            okk = ppo[f'kk_{l}'] + j
            ork = ppo[f'rk_{l}'] + j
            for b in range(4):
                tk = slice(b * 512, (b + 1) * 512)
                TS('dve', t1, xk[:, tk], pp[:, okk:okk + 1], None, ALU.mult, None, ['xk'], ['t1'])
                ACT(t2, t1, AF.Square, ['t1'], ['t2'])
                MM(ps[0][:], bdm[:], t2, True, True, ['t2'], ['ps0'])
                ACT(t2, ps[0][:], AF.Sqrt, ['ps0'], ['t2'])
                TS('dve', t2, t2, 1e-12, None, ALU.max, None, ['t2'], ['t2'])
                RCP(t2, t2, ['t2'], ['t2'])
                TT('dve', kkn[:, tk], t1, t2, ALU.mult, ['t1', 't2'], ['kkn'])
                P.dma(as0, as_d[0, cj, tk], reads=[f'as_0_{j}_{b}'], writes=['as0'])
                P.dma(as1, as_d[1, cj, tk], reads=[f'as_1_{j}_{b}'], writes=['as1'])
                TT('pool', t1, as0, as1, ALU.add, ['as0', 'as1'], ['t1'])
                TS('dve', t1, t1, pp[:, oka + j:oka + j + 1], ka2[:, j:j + 1], ALU.mult, ALU.add, ['t1', 'ka2'], ['t1'])
                TT('dve', t1, t1, xk[:, tk], ALU.mult, ['t1', 'xk'], ['t1'])
                TT('dve', t1, t1, xr[:, tk], ALU.mult, ['t1', 'xr'], ['t1'])
                TS('dve', t2, t1, pp[:, ork:ork + 1], None, ALU.mult, None, ['t1'], ['t2'])
                MM(ps[1][:], bdm[:], t2, True, True, ['t2'], ['ps1'])
                TT('dve', bonus[:, tk], ps[1][:], xv[:, tk], ALU.mult, ['ps1', 'xv'], ['bonus'])
            for t in range(16):
                TRN(psb[:, (t % 8) * 128:(t % 8 + 1) * 128], xv[:, t * 128:(t + 1) * 128], identb, ['xv', 'identb'], ['ps7'])
                if t % 8 == 7:
                    g8 = t // 8
                    CP('act', Vtm[:, g8 * 8:(g8 + 1) * 8, :], psb.rearrange("p (a b) -> p a b", b=128), ['ps7'], ['Vtm'])
            MSET('pool', ysum, 0.0, ['ysum'])
            P.barrier()
            A.off = mark

            def unit_alloc():
                u = {}
                for nm in ('lwt', 'cl', 'gi', 'gv', 't1', 't2'):
                    u[nm] = A.alloc([128, 512], F32)
                for nm in ('as0', 'bb', 'Bt', 'Bbar', 'kd', 'Kt', 'Kbar'):
                    u[nm] = A.alloc([128, 512], BF16)
                u['AR'] = A.alloc([128, 2, 512], BF16)
                u['BKz'] = A.alloc([128, 16, 128], BF16).rearrange("p (a b c) d -> p a b c d", b=2, c=2)
                u['G'] = [A.alloc([128, 512], BF16) for _ in range(2)]
                u['Nm'] = A.alloc([128, 256], BF16)
                u['MM2'] = [A.alloc([128, 512], BF16) for _ in range(2)]
                u['Mb'] = [m_[:, 0:256] for m_ in u['MM2']]
                u['MTb'] = [m_[:, 256:512] for m_ in u['MM2']]
                u['PTb'] = [A.alloc([128, 256], BF16) for _ in range(2)]
                u['RHSb'] = A.alloc([128, 128], BF16)
                u['Ub'] = A.alloc([128, 128], BF16)
                u['Hf'] = A.alloc([128, 64], F32)
                u['Hs'] = A.alloc([128, 64], BF16)
                return u

            def unit(d, u):
                n = lambda s_: f'{s_}_{d}'
                lwt_, cl, gi, gv, t1, t2 = u['lwt'], u['cl'], u['gi'], u['gv'], u['t1'], u['t2']
                as0, bb, Bt, Bbar, kd, Kt, Kbar = u['as0'], u['bb'], u['Bt'], u['Bbar'], u['kd'], u['Kt'], u['Kbar']
                AR, BKz, G, Nm, Mb, MTb, PTb = u['AR'], u['BKz'], u['G'], u['Nm'], u['Mb'], u['MTb'], u['PTb']
                RHSb, Ub, Hf, Hs = u['RHSb'], u['Ub'], u['Hf'], u['Hs']
                Xb, Yb, Zb = ps[3 * d], ps[3 * d + 1], ps[3 * d + 2]
                pA = [Xb[:, 0:256], Yb[:, 0:256]]
                pB = [Xb[:, 256:512], Yb[:, 256:512]]
                pAn = [f'ps{3 * d}', f'ps{3 * d + 1}']
                pBn = pAn
                pD, pE = Zb[:, 0:256], Zb[:, 256:512]
                pDn, pEn = f'ps{3 * d + 2}', f'ps{3 * d + 2}'
                pFh = [Xb[:, 256:384], Xb[:, 384:512]]
                pU, pS = Xb[:, 256:384], Yb[:, 256:320]
                pUn, pSn = pAn[0], pAn[1]
                pT = psb
                pTn = 'ps7'
                MSET('pool', Hf, 0.0, [n('Hf')])
                MSET('pool', Hs, 0.0, [n('Hs')])
                MSET('pool', BKz, 0.0, [n('BKz')])
                for bi in range(4):
                    b = bi if d == 0 else 3 - bi
                    tk = slice(b * 512, (b + 1) * 512)
                    P.dma(lwt_, lw_d[d, cj, tk], reads=[f'lw_{d}_{j}_{b}'], writes=[n('lwt')])
                    P.dma(as0, as_d[d, cj, tk], reads=[f'as_{d}_{j}_{b}'], writes=[n('as0')])
                    yield
                    if d == 0:
                        P.op('dve', lambda e, cl=cl, lwt_=lwt_: e.tensor_tensor_scan(cl, rst[:], lwt_, 0.0, ALU.mult, ALU.add),
                             [n('lwt')], [n('cl')])
                    else:
                        P.op('dve', lambda e, cl=cl, lwt_=lwt_: e.tensor_tensor_scan(cl[:, ::-1], rst[:], lwt_[:, ::-1], 0.0,
                                                                                    ALU.mult, ALU.add), [n('lwt')], [n('cl')])
                    cl3 = cl.rearrange("p (a b) -> p a b", b=128)
                    tot = cl3[:, :, 127:128] if d == 0 else cl3[:, :, 0:1]
                    ACT(gi, cl, AF.Exp, [n('cl')], [n('gi')])
                    ACT(gv, cl, AF.Exp, [n('cl')], [n('gv')], scale=-1.0)
                    TT('pool', t1, cl, lwt_, ALU.subtract, [n('cl'), n('lwt')], [n('t1')])
                    ACT(t1, t1, AF.Exp, [n('t1')], [n('t1')])
                    TT('dve', t2.rearrange("p (a b) -> p a b", b=128), cl3, tot.to_broadcast([128, 4, 128]), ALU.subtract,
                       [n('cl')], [n('t2')])
                    ACT(t2, t2, AF.Exp, [n('t2')], [n('t2')], scale=-1.0)
                    yield
                    STT(AR[:, 0, :], kkn[:, tk], -1.0, t1, ALU.mult, ALU.mult, ['kkn', n('t1')], [n('AR')])
                    TT('pool', AR[:, 1, :], xr[:, tk], gi, ALU.mult, ['xr', n('gi')], [n('AR')])
                    TT('pool', bb, kkn[:, tk], as0, ALU.mult, ['kkn', n('as0')], [n('bb')])
                    TT('dve', Bt, bb, gv, ALU.mult, [n('bb'), n('gv')], [n('Bt')])
                    TT('pool', Bbar, bb, t2, ALU.mult, [n('bb'), n('t2')], [n('Bbar')])
                    TS('dve', kd, as0, pp[:, oka + j:oka + j + 1], ka1[:, j:j + 1], ALU.mult, ALU.add, [n('as0'), 'ka1'], [n('kd')])
                    TT('pool', kd, kd, xk[:, tk], ALU.mult, [n('kd'), 'xk'], [n('kd')])
                    TT('dve', Kt, kd, gv, ALU.mult, [n('kd'), n('gv')], [n('Kt')])
                    TT('pool', Kbar, kd, t2, ALU.mult, [n('kd'), n('t2')], [n('Kbar')])
                    yield
                    for w_, src_, sn_ in ((0, Bbar, n('Bbar')), (1, Kbar, n('Kbar'))):
                        for ci in range(4):
                            TRN(pT[:, (w_ * 4 + ci) * 128:(w_ * 4 + ci + 1) * 128], src_[:, ci * 128:(ci + 1) * 128], identb,
                                [sn_, 'identb'], [pTn])
                    for w_ in range(2):
                        for ci in range(4):
                            for hh in range(2):
                                c0_ = (w_ * 4 + ci) * 128 + hh * 64
                                CP('act', BKz[:, ci, w_, hh, hh * 64:(hh + 1) * 64], pT[:, c0_:c0_ + 64],
                                   [pTn], [n('BKz')])
                    yield
                    for cii in range(4):
                        ci = cii if d == 0 else 3 - cii
                        cc = slice(ci * 128, (ci + 1) * 128)
                        cg = b * 4 + ci
                        for hh in range(2):
                            hb = hh * 64
                            MM(pA[hh], Bt[hb:hb + 64, cc], AR[hb:hb + 64, :, cc], True, True, [n('Bt'), n('AR')], [pAn[hh]])
                            MM(pB[hh], Kt[hb:hb + 64, cc], AR[hb:hb + 64, :, cc], True, True, [n('Kt'), n('AR')], [pBn[hh]])
                        yield
                        for hh in range(2):
                            TT('dve', G[hh][:, 0:256], pA[hh], mtr[:, d, 0:256], ALU.mult, [pAn[hh]], [n(f'G{hh}')])
                            TT('dve', G[hh][:, 256:512], pB[hh], mtr[:, d, 0:256], ALU.mult, [pBn[hh]], [n(f'G{hh}')])
                        for hh in range(2):
                            hb = hh * 64
                            MM(pA[hh][:, 0:128], AR[hb:hb + 64, 0, cc], Bt[hb:hb + 64, cc], True, True,
                               [n('Bt'), n('AR')], [pAn[hh]])
                        yield
                        for hh in range(2):
                            TT('dve', Nm[:, hh * 128:(hh + 1) * 128], pA[hh][:, 0:128], mlm[:, d, 0:128], ALU.mult,
                               [pAn[hh]], [n('Nm')])
                        NTh = [G[hh][:, 0:128] for hh in range(2)]
                        ARBh = [G[hh][:, 128:256] for hh in range(2)]
                        AKh = [G[hh][:, 256:384] for hh in range(2)]
                        ARKh = [G[hh][:, 384:512] for hh in range(2)]
                        for hh in range(2):
                            TT('dve', PTb[0][:, hh * 128:(hh + 1) * 128], NTh[hh], ident[:], ALU.add, [n(f'G{hh}')], [n('PTb0')])
                        curM = [Nm[:, hh * 128:(hh + 1) * 128] for hh in range(2)]
                        curMn = [n('Nm')] * 2
                        curMT = NTh
                        curMTn = [n('G0'), n('G1')]
                        pcur = 0
                        yield
                        for lev in range(1, 7):
                            sl_ = lev % 2
                            for hh in range(2):
                                MM(pD[:, hh * 128:(hh + 1) * 128], curMT[hh], curM[hh], True, True,
                                   [curMn[hh], curMTn[hh]], [pDn])
                            if lev < 6:
                                for hh in range(2):
                                    MM(pE[:, hh * 128:(hh + 1) * 128], curM[hh], curMT[hh], True, True,
                                       [curMn[hh], curMTn[hh]], [pEn])
                            yield
                            if lev < 6:
                                CP('act', u['MM2'][sl_], Zb[:], [pDn], [n(f'Mb{sl_}'), n(f'MTb{sl_}')])
                            else:
                                CP('act', Mb[sl_], pD, [pDn], [n(f'Mb{sl_}')])
                            curM = [Mb[sl_][:, hh * 128:(hh + 1) * 128] for hh in range(2)]
                            curMn = [n(f'Mb{sl_}')] * 2
                            curMT = [MTb[sl_][:, hh * 128:(hh + 1) * 128] for hh in range(2)]
                            curMTn = [n(f'MTb{sl_}')] * 2
                            for hh in range(2):
                                MM(pFh[hh], curM[hh], PTb[pcur][:, hh * 128:(hh + 1) * 128], True, True,
                                   [curMn[hh], n(f'PTb{pcur}')], [pAn[0]])
                            yield
                            TT('dve', PTb[1 - pcur], Xb[:, 256:512], PTb[pcur], ALU.add, [pAn[0], n(f'PTb{pcur}')],
                               [n(f'PTb{1 - pcur}')])
                            pcur = 1 - pcur
                        PTc = PTb[pcur]
                        PTn = n(f'PTb{pcur}')
                        for hh in range(2):
                            hb = hh * 64
                            vs = slice(hh * 64, (hh + 1) * 64)
                            o = pA[hh][:, 128:192]
                            MM(o, AR[hb:hb + 64, 0, cc], Hs[hb:hb + 64, :], True, False, [n('AR'), n('Hs')], [pAn[hh]])
                            MM(o, AKh[hh], Vtm[:, cg, vs], False, True, [n(f'G{hh}'), 'Vtm'], [pAn[hh]])
                        yield
                        CP('act', RHSb[:, 0:64], pA[0][:, 128:192], [pAn[0]], [n('RHSb')])
                        CP('dve', RHSb[:, 64:128], pA[1][:, 128:192], [pAn[1]], [n('RHSb')])
                        for hh in range(2):
                            vs = slice(hh * 64, (hh + 1) * 64)
                            MM(pU[:, hh * 64:(hh + 1) * 64], PTc[:, hh * 128:(hh + 1) * 128], RHSb[:, vs], True, True,
                               [PTn, n('RHSb')], [pUn])
                        yield
                        CP('dve', Ub, pU, [pUn], [n('Ub')])
                        for hh in range(2):
                            hb = hh * 64
                            vs = slice(hh * 64, (hh + 1) * 64)
                            o = pA[hh][:, 192:256]
                            MM(o, AR[hb:hb + 64, 1, cc], Hs[hb:hb + 64, :], True, False, [n('AR'), n('Hs')], [pAn[hh]])
                            MM(o, ARBh[hh], Ub[:, vs], False, False, [n(f'G{hh}'), n('Ub')], [pAn[hh]])
                            MM(o, ARKh[hh], Vtm[:, cg, vs], False, True, [n(f'G{hh}'), 'Vtm'], [pAn[hh]])
                        o = pS
                        for hh in range(2):
                            vs = slice(hh * 64, (hh + 1) * 64)
                            MM(o, BKz[:, ci, 0, hh, :], Ub[:, vs], hh == 0, False, [n('BKz'), n('Ub')], [pSn])
                            MM(o, BKz[:, ci, 1, hh, :], Vtm[:, cg, vs], False, hh == 1, [n('BKz'), 'Vtm'], [pSn])
                        yield
                        for hh in range(2):
                            vs = slice(hh * 64, (hh + 1) * 64)
                            TT('dve', ysum[:, cg, vs], pA[hh][:, 192:256], ysum[:, cg, vs], ALU.add,
                               [pAn[hh], f'ysum{cg}'], [f'ysum{cg}'])
                        gcol = gi[:, ci * 128 + 127:ci * 128 + 128] if d == 0 else gi[:, ci * 128:ci * 128 + 1]
                        STT(Hf, Hf, gcol, pS, ALU.mult, ALU.add, [n('Hf'), n('gi'), pSn], [n('Hf')])
                        CP('act', Hs, Hf, [n('Hf')], [n('Hs')])
                        yield

            units = [unit(d, unit_alloc()) for d in range(2)]
            active = list(units)
            first = True
            import os
            if os.environ.get('RW_SERIAL'):
                for g_ in units:
                    for _ in g_:
                        pass
                active = []
            NDUM = int(os.environ.get('RW_DUMMY', '0'))
            while active:
                for g_ in list(active):
                    try:
                        next(g_)
                    except StopIteration:
                        active.remove(g_)
                    for _ in range(NDUM):
                        MM(ps[6][:], zt[:, 0:128], zt[:, 128:640], True, True, ['zt'], ['ps6'])
            P.barrier()
            for g4 in range(4):
                tk = slice(g4 * 512, (g4 + 1) * 512)
                y8 = ysum[:, g4 * 4:(g4 + 1) * 4, :].rearrange("p a (h n) -> p (a h) n", n=64)
                gn_stats(y8, 'ysum', 8, 64e-5, True, gsc)
                for qi in range(4):
                    TRN(ps[3][:, qi * 128:(qi + 1) * 128], ysum[:, g4 * 4 + qi, :], ident[:], ['ysum'], ['ps3'])
                olw = ppo[f'lnw_{l}'] + j
                olb = ppo[f'lnb_{l}'] + j
                TS('dve', te1, ps[3][:], pp[:, olw:olw + 1], pp[:, olb:olb + 1], ALU.mult, ALU.add, ['ps3'], ['te1'])
                TT('pool', te1, te1, bonus[:, tk], ALU.add, ['te1', 'bonus'], ['te1'])
                P.dma(gtile, gate_d[cj, tk], reads=[f'gate_{j}_{g4}'], writes=['gtile'])
                TT('dve', mT[:, 3 + j, tk], te1, gtile, ALU.mult, ['te1', 'gtile'], ['mT'])
            P.barrier()

    diff_setup()
    for s in range(nseq):
        for l in range(depth):
            src = xview(xT_in, s) if l == 0 else xview(xres, s)
            dst = xview(xres, s)
            rmsnorm(src, f'g1_{l}', s)
            if 'ret' in mixers:
                retention_phase(l, s)
            else:
                MSET('pool', mT[:, 0:3, :], 0.0, ['mT'])
            if 'rwkv' in mixers:
                rwkv_phase(l, s)
            else:
                MSET('pool', mT[:, 3:6, :], 0.0, ['mT'])
            if 'diff' in mixers:
                diff_phase(l, s)
            else:
                MSET('pool', mT[:, 6:8, :], 0.0, ['mT'])
            P.barrier()
            out_proj(l, s, src, dst)
            if ffn:
                rmsnorm(dst, f'g2_{l}', s)
                ffn_phase(l, s, dst)
        rmsnorm(xview(xres, s), 'gf', s, out_dram=xview(outT, s))
    P.wait_all_outputs(out_tokens)
    P.finish()
    return nc, P


_CACHE = {}


def kernel(**inputs):
    x = np.asarray(inputs['x'], dtype=np.float32)
    consts = host_consts()
    if 'nc' not in _CACHE:
        _CACHE['nc'] = build()[0]
    nc = _CACHE['nc']
    params = {k: np.ascontiguousarray(np.asarray(inputs[k], dtype=np.float32)) for k in PARAM_SHAPES}
    in_maps = []
    for c in range(8):
        xs = np.ascontiguousarray(x[2 * c:2 * c + 2].reshape(2 * T, D).T)
        m = {'xT': xs}
        m.update(params)
        m.update(consts)
        in_maps.append(m)
    res = run_bass_kernel_spmd(nc, in_maps, core_ids=list(range(8)))
    out = np.empty((16, T, D), np.float32)
    for c in range(8):
        o = np.asarray(res.results[c]['outT'])
        out[2 * c:2 * c + 2] = o.T.reshape(2, T, D)
    return out
```

```python
import contextlib
import numpy as np
import concourse.bass as bass
import concourse.mybir as mybir

F32 = mybir.dt.float32
BF16 = mybir.dt.bfloat16
I32 = mybir.dt.int32
ALU = mybir.AluOpType
AF = mybir.ActivationFunctionType
AX = mybir.AxisListType

EPOCH = 16000
N_DMA_SEMS = 40


class Prog:
    def __init__(self, nc, same_engine_sync=True):
        self.nc = nc
        self.stack = contextlib.ExitStack()
        self.engs = ['pe', 'act', 'dve', 'pool', 'sp']
        self.ops = {e: [] for e in self.engs}
        self.count = {e: 0 for e in self.engs}
        self.csems = {e: [] for e in self.engs}
        self.seen = {e: {} for e in self.engs}
        self.last_write = {}
        self.readers = {}
        self.dma_sems = []
        self.dma_uses = []
        self.dma_rr = 0
        self.same_engine_sync = same_engine_sync
        self.n_inst = 0
        self.out_tokens = []
        self.last_tok = {}
        self.dma_last = {}

    def sem(self, name):
        return self.stack.enter_context(self.nc.semaphore(name))

    def sbuf(self, name, shape, dtype):
        return self.stack.enter_context(self.nc.sbuf_tensor(name, list(shape), dtype))

    def psum(self, name, shape, dtype):
        return self.stack.enter_context(self.nc.psum_tensor(name, list(shape), dtype))

    def _csem(self, e, epoch):
        while len(self.csems[e]) <= epoch:
            self.csems[e].append(self.sem(f"c_{e}_{len(self.csems[e])}"))
        return self.csems[e][epoch]

    def _deps(self, reads, writes):
        deps = []
        for r in reads:
            t = self.last_write.get(r)
            if t is not None:
                deps.append(t)
        for w in writes:
            t = self.last_write.get(w)
            if t is not None:
                deps.append(t)
            for t in self.readers.get(w, {}).values():
                deps.append(t)
        return deps

    def _commit(self, token, reads, writes):
        for r in reads:
            self.readers.setdefault(r, {})[token[0]] = token
        for w in writes:
            self.last_write[w] = token
            self.readers[w] = {}

    def _waits(self, e, deps):
        need = {}
        for (src, sem_id, sem, val) in deps:
            if src == e and (e == 'pe' or not self.same_engine_sync):
                continue
            if self.seen[e].get(sem_id, 0) >= val:
                continue
            if need.get(sem_id, (None, 0))[1] < val:
                need[sem_id] = (sem, val)
        out = []
        for sem_id, (sem, val) in need.items():
            self.seen[e][sem_id] = val
            out.append((sem, val))
        return out

    def op(self, e, fn, reads=(), writes=()):
        reads = list(reads)
        writes = list(writes)
        deps = self._deps(reads, writes)
        waits = self._waits(e, deps)
        k = self.count[e]
        self.count[e] += 1
        epoch, idx = divmod(k, EPOCH)
        sem = self._csem(e, epoch)
        token = (e, (e, epoch), sem, idx + 1)
        self.last_tok[e] = token
        self._commit(token, reads, writes)

        def emit(eng, fn=fn, waits=waits, sem=sem):
            for (s, v) in waits:
                eng.wait_ge(s, v)
            fn(eng).then_inc(sem, 1)
        self.ops[e].append(emit)
        self.n_inst += 1 + len(waits)
        return token

    def dma(self, out_ap, in_ap, reads=(), writes=(), queue='sp', **kw):
        reads = list(reads)
        writes = list(writes)
        if not self.dma_sems:
            for i in range(N_DMA_SEMS):
                self.dma_sems.append(self.sem(f"dma_{i}"))
                self.dma_uses.append(0)
        si = self.dma_rr
        self.dma_rr = (self.dma_rr + 1) % N_DMA_SEMS
        sem = self.dma_sems[si]
        prev = self.dma_uses[si] * 16
        self.dma_uses[si] += 1
        target = prev + 16
        deps = self._deps(reads, writes)
        if prev > 0:
            deps.append(('dmaprev', ('dma', si), sem, prev))
        waits = self._waits(queue, deps)
        token = (f'dma{si}_{target}', ('dma', si), sem, target)
        self.dma_last[si] = token
        self._commit(token, reads, writes)

        def emit(eng, waits=waits, sem=sem, out_ap=out_ap, in_ap=in_ap, kw=kw):
            for (s, v) in waits:
                eng.wait_ge(s, v)
            eng.dma_start(out=out_ap, in_=in_ap, **kw).then_inc(sem, 16)
        self.ops[queue].append(emit)
        self.n_inst += 1 + len(waits)
        return token

    def barrier(self):
        toks = list(self.last_tok.values()) + list(self.dma_last.values())
        for e in self.engs:
            waits = self._waits(e, [t for t in toks if t[0] != e])

            def emit(eng, waits=waits):
                for (s, v) in waits:
                    eng.wait_ge(s, v)
            self.ops[e].append(emit)
            self.n_inst += len(waits)
        self.last_write = {}
        self.readers = {}

    def wait_all_outputs(self, tokens, e='sp'):
        waits = self._waits(e, tokens)

        def emit(eng, waits=waits):
            for (s, v) in waits:
                eng.wait_ge(s, v)
        self.ops[e].append(emit)

    def finish(self):
        nc = self.nc
        ops = self.ops
        with nc.Block() as block:
            @block.tensor
            def _(eng):
                for f in ops['pe']:
                    f(eng)

            @block.scalar
            def _(eng):
                for f in ops['act']:
                    f(eng)

            @block.vector
            def _(eng):
                for f in ops['dve']:
                    f(eng)

            @block.gpsimd
            def _(eng):
                for f in ops['pool']:
                    f(eng)

            @block.sync
            def _(eng):
                for f in ops['sp']:
                    f(eng)
        self.stack.close()

from concourse.bass_utils import run_bass_kernel_spmd

T = 2048
D = 1024
DIN = 3840
DFF = 2816
NFC = 22
EPS = 1e-6
LOG_E05 = 0.6065306597126334


class Arena:
    def __init__(self, ap_f32, nwords):
        self.ap = ap_f32
        self.n = nwords
        self.off = 0

    def reset(self):
        self.off = 0

    def alloc(self, shape, dtype):
        nel = 1
        for s in shape[1:]:
            nel *= s
        if dtype == BF16:
            nw = (nel + 1) // 2
        else:
            nw = nel
        nw = (nw + 1) // 2 * 2
        assert self.off + nw <= self.n, f"arena overflow {self.off}+{nw}>{self.n}"
        v = self.ap[0:shape[0], self.off:self.off + nw]
        self.off += nw
        if dtype != F32:
            v = v.bitcast(dtype)
        v = v[:, 0:nel]
        if len(shape) == 3:
            v = v.rearrange("p (a b) -> p a b", b=shape[2])
        elif len(shape) == 4:
            v = v.rearrange("p (a b c) -> p a b c", b=shape[2], c=shape[3])
        return v


def host_consts():
    c = {}
    c['c_ident'] = np.eye(128, dtype=np.float32)
    c['c_ones'] = np.ones((128, 128), np.float32)
    bd = np.zeros((128, 128), np.float32)
    bd[:64, :64] = 1
    bd[64:, 64:] = 1
    c['c_bd'] = bd
    half = 32
    freqs = (np.float32(10000.0) ** (-np.arange(half, dtype=np.float32) / np.float32(half))).astype(np.float32)
    pos = np.arange(T, dtype=np.float32)
    ang = (pos[None, :] * freqs[:, None]).astype(np.float32)
    cs = np.cos(ang).astype(np.float32)
    sn = np.sin(ang).astype(np.float32)
    rot = np.zeros((128, 2, T), np.float32)
    for p in range(128):
        rot[p, 0] = cs[p % 32]
        rot[p, 1] = -sn[p % 32] if (p % 64) < 32 else sn[p % 32]
    c['c_rot'] = rot
    d = np.arange(4096, dtype=np.int64) - 2047
    n = np.abs(d)
    nf = np.maximum(n, 1).astype(np.float32)
    lg = (np.log(nf / np.float32(8.0)) / np.float32(np.log(16.0)) * np.float32(8.0)).astype(np.float32)
    large = np.minimum(8 + lg.astype(np.int32), 15)
    bucket = np.where(d > 0, 16, 0) + np.where(n < 8, n, large)
    oh = np.zeros((32, 4096), np.float32)
    oh[bucket, np.arange(4096)] = 1.0
    oh[:, 4095] = 0.0
    c['c_onehot'] = oh
    r = np.arange(128)[:, None]
    q = np.arange(128)[None, :]
    su = (q > r).astype(np.float32)
    ui = (q >= r).astype(np.float32)
    sl = (q < r).astype(np.float32)
    li = (q <= r).astype(np.float32)
    mtr = np.zeros((128, 2, 256), np.float32)
    mtr[:, 0] = np.concatenate([su, ui], axis=1)
    mtr[:, 1] = np.concatenate([sl, li], axis=1)
    c['c_mtr'] = mtr
    ml = np.zeros((128, 2, 128), np.float32)
    ml[:, 0] = sl
    ml[:, 1] = su
    c['c_ml'] = ml
    rst = np.ones((128, 512), np.float32)
    rst[:, ::128] = 0.0
    c['c_rst'] = rst
    lgam = np.log1p(-np.exp2(-5.0 - np.arange(6, dtype=np.float32))).astype(np.float32)
    c['c_lgam'] = np.tile(lgam[None, :], (128, 1)).astype(np.float32)
    return c


CONST_SHAPES = {
    'c_ident': [128, 128], 'c_ones': [128, 128], 'c_bd': [128, 128], 'c_rot': [128, 2, T],
    'c_onehot': [32, 4096], 'c_mtr': [128, 2, 256], 'c_ml': [128, 2, 128], 'c_rst': [128, 512],
    'c_lgam': [128, 6],
}

PARAM_SHAPES = {
    'mix_norm_g': [2, 1024], 'w_in': [2, 1024, 3840], 'w_out': [2, 1024, 1024],
    'ret_gn_w': [2, 384], 'ret_gn_b': [2, 384], 'rwkv_mu': [2, 6, 2, 384], 'rwkv_w0': [2, 2, 384],
    'rwkv_w1': [2, 2, 384, 64], 'rwkv_w2': [2, 2, 64, 384], 'rwkv_a0': [2, 2, 384],
    'rwkv_a1': [2, 2, 384, 64], 'rwkv_a2': [2, 2, 64, 384], 'rwkv_g1': [2, 384, 128],
    'rwkv_g2': [2, 128, 384], 'rwkv_k_k': [2, 384], 'rwkv_k_a': [2, 384], 'rwkv_r_k': [2, 6, 64],
    'rwkv_ln_w': [2, 384], 'rwkv_ln_b': [2, 384], 'diff_lambda': [2, 4, 32], 'diff_subln_w': [2, 64],
    'rel_bias': [32, 4], 'ffn_norm_g': [2, 1024], 'w_gate': [2, 1024, 2816], 'w_up': [2, 1024, 2816],
    'w_down': [2, 2816, 1024], 'final_norm_g': [1024],
}


def build(nseq=2, depth=2, mixers=('ret', 'rwkv', 'diff'), ffn=True):
    nc = bass.Bass("TRN2", target_bir_lowering=False)
    NTOK = nseq * T
    xT_in = nc.dram_tensor("xT", [D, NTOK], F32, kind="ExternalInput").ap()
    prm = {k: nc.dram_tensor(k, shp, F32, kind="ExternalInput").ap() for k, shp in PARAM_SHAPES.items()}
    cst = {k: nc.dram_tensor(k, shp, F32, kind="ExternalInput").ap() for k, shp in CONST_SHAPES.items()}
    outT = nc.dram_tensor("outT", [D, NTOK], F32, kind="ExternalOutput").ap()
    xres = nc.dram_tensor("xres", [D, NTOK], F32).ap()
    bvec_d = nc.dram_tensor("bvec_d", [4, 4096], F32).ap()
    lw_d = nc.dram_tensor("rw_lw", [2, 384, T], F32).ap()
    as_d = nc.dram_tensor("rw_as", [2, 384, T], BF16).ap()
    gate_d = nc.dram_tensor("rw_gate", [384, T], BF16).ap()

    P = Prog(nc)
    hT = P.sbuf("hT", [128, 8, T], BF16)
    mT = P.sbuf("mT", [128, 8, T], BF16)
    stg = [P.sbuf(f"stg{i}", [128, 8, 256], F32) for i in range(2)]
    wbs = [P.sbuf(f"wb{i}", [128, 8, 256], BF16) for i in range(4)]
    ident = P.sbuf("ident", [128, 128], F32)
    ones = P.sbuf("ones", [128, 128], F32)
    bdm = P.sbuf("bdm", [128, 128], F32)
    mtr = P.sbuf("mtr", [128, 2, 256], F32)
    mlm = P.sbuf("mlm", [128, 2, 128], F32)
    rst = P.sbuf("rst", [128, 512], F32)
    lgam = P.sbuf("lgam", [128, 6], F32)
    zt = P.sbuf("zt", [128, 640], BF16)
    NPC = 256
    pp = P.sbuf("pp", [128, NPC], F32)
    ARW = 26300
    arena_t = P.sbuf("arena", [128, ARW], F32)
    A = Arena(arena_t[:], ARW)
    ps = [P.psum(f"ps{i}", [128, 512], F32) for i in range(7)]
    psb_t = P.psum("psb", [128, 1024], BF16)
    psb = psb_t[:]

    def MM(out, lhsT, rhs, start, stop, R, W):
        return P.op('pe', lambda e: e.matmul(out, lhsT, rhs, start=start, stop=stop, skip_group_check=True), R, W)

    def TRN(out, in_, idn, R, W):
        return P.op('pe', lambda e: e.transpose(out, in_, idn), R, W)

    def TT(eng, out, in0, in1, op, R, W):
        return P.op(eng, lambda e: e.tensor_tensor(out, in0, in1, op), R, W)

    def TS(eng, out, in0, s1, s2, op0, op1, R, W):
        if s2 is None:
            return P.op(eng, lambda e: e.tensor_scalar(out, in0, s1, None, op0), R, W)
        return P.op(eng, lambda e: e.tensor_scalar(out, in0, s1, s2, op0, op1), R, W)

    def STT(out, in0, scalar, in1, op0, op1, R, W):
        return P.op('dve', lambda e: e.scalar_tensor_tensor(out, in0, scalar, in1, op0, op1), R, W)

    def ACT(out, in_, func, R, W, bias=0.0, scale=1.0):
        return P.op('act', lambda e: e.activation(out, in_, func, bias=bias, scale=scale), R, W)

    def CP(eng, out, in_, R, W):
        if eng == 'act':
            return P.op('act', lambda e: e.copy(out, in_), R, W)
        return P.op(eng, lambda e: e.tensor_copy(out, in_), R, W)

    def RSUM(out, in_, R, W):
        return P.op('dve', lambda e: e.reduce_sum(out, in_, AX.X), R, W)

    def RCP(out, in_, R, W):
        return P.op('dve', lambda e: e.reciprocal(out, in_), R, W)

    def MSET(eng, ap, val, W):
        return P.op(eng, lambda e: e.memset(ap, val), (), W)

    def fm(ap2d):
        return ap2d.rearrange("(c p) n -> p c n", p=128)

    P.dma(ident[:], cst['c_ident'], writes=['ident'])
    P.dma(ones[:], cst['c_ones'], writes=['ones'])
    P.dma(bdm[:], cst['c_bd'], writes=['bdm'])
    P.dma(mtr[:], cst['c_mtr'], writes=['mtr'])
    P.dma(mlm[:], cst['c_ml'], writes=['mlm'])
    P.dma(rst[:], cst['c_rst'], writes=['rst'])
    P.dma(lgam[:], cst['c_lgam'], writes=['lgam'])
    MSET('pool', zt[:], 0.0, ['zt'])
    ppo = {}
    ppn = [0]

    def pcol(name, vec_ap, n):
        off = ppn[0]
        ppn[0] += n
        assert ppn[0] <= NPC
        P.dma(pp[:, off:off + n], vec_ap.rearrange("(c p) -> p c", p=128), writes=[f'pp_{name}'],
              allow_slow_non_contiguous=True)
        ppo[name] = off
        return off

    for l in range(depth):
        pcol(f"g1_{l}", prm['mix_norm_g'][l], 8)
        pcol(f"g2_{l}", prm['ffn_norm_g'][l], 8)
        pcol(f"rgw_{l}", prm['ret_gn_w'][l], 3)
        pcol(f"rgb_{l}", prm['ret_gn_b'][l], 3)
        for f in range(6):
            for s in range(2):
                pcol(f"mu_{l}_{f}_{s}", prm['rwkv_mu'][l, f, s], 3)
        for d in range(2):
            pcol(f"w0_{l}_{d}", prm['rwkv_w0'][l, d], 3)
            pcol(f"a0_{l}_{d}", prm['rwkv_a0'][l, d], 3)
        pcol(f"kk_{l}", prm['rwkv_k_k'][l], 3)
        pcol(f"ka_{l}", prm['rwkv_k_a'][l], 3)
        pcol(f"rk_{l}", prm['rwkv_r_k'][l].rearrange("h n -> (h n)"), 3)
        pcol(f"lnw_{l}", prm['rwkv_ln_w'][l], 3)
        pcol(f"lnb_{l}", prm['rwkv_ln_b'][l], 3)
        off = ppn[0]
        ppn[0] += 1
        sw = prm['diff_subln_w'][l].rearrange("(c p) -> p c", p=64)
        P.dma(pp[0:64, off:off + 1], sw, writes=[f'ppsa{l}'], allow_slow_non_contiguous=True)
        P.dma(pp[64:128, off:off + 1], sw, writes=[f'ppsb{l}'], allow_slow_non_contiguous=True)
        ppo[f"sub_{l}"] = off
    pcol("gf", prm['final_norm_g'], 8)
    P.barrier()

    def pc(name, c=0):
        o = ppo[name] + c
        return pp[:, o:o + 1]

    wstate = {'s': 0, 'b': 0}

    def load_w(pieces, nrc, swap_from=None):
        si = wstate['s']
        wstate['s'] = (si + 1) % 2
        bi = wstate['b']
        wstate['b'] = (bi + 1) % 4
        off = 0
        names = []
        for i, ap in enumerate(pieces):
            n = ap.shape[1]
            P.dma(stg[si][:, 0:nrc, off:off + n], fm(ap), writes=[f'stg{si}_{i}'])
            names.append(f'stg{si}_{i}')
            off += n
        CP('pool', wbs[bi][:, 0:nrc, 0:off], stg[si][:, 0:nrc, 0:off], names, [f'wb{bi}'])
        return wbs[bi], f'wb{bi}', stg[si], names

    def xview(ap2d, s):
        return fm(ap2d)[:, :, s * T:(s + 1) * T]

    def rmsnorm(src, gname, s, out_dram=None):
        A.reset()
        sq = [A.alloc([128, 512], BF16) for _ in range(2)]
        rs = A.alloc([128, 512], F32)
        onesb = A.alloc([128, 128], BF16)
        xbs = [A.alloc([128, 8, 512], F32) for _ in range(2)]
        obs = [A.alloc([128, 8, 512], F32) for _ in range(2)] if out_dram is not None else None
        CP('pool', onesb, ones[:], [], ['onesb'])
        for b in range(4):
            tk = slice(b * 512, (b + 1) * 512)
            xb = xbs[b % 2]
            xbn = f'xb{b % 2}'
            P.dma(xb, src[:, :, tk], reads=[f'xr_{s}_{dc}_{b}' for dc in range(8)], writes=[xbn])
            for c in range(8):
                ACT(sq[c % 2], xb[:, c, :], AF.Square, [xbn], [f'sq{c % 2}'])
                MM(ps[b % 2][:], onesb, sq[c % 2], c == 0, c == 7, [f'sq{c % 2}', 'onesb'], [f'ps{b % 2}'])
            ACT(rs, ps[b % 2][:], AF.Sqrt, [f'ps{b % 2}'], ['rs'], bias=EPS, scale=1.0 / D)
            RCP(rs, rs, ['rs'], ['rs'])
            for c in range(8):
                if out_dram is None:
                    STT(hT[:, c, tk], xb[:, c, :], pc(gname, c), rs, ALU.mult, ALU.mult, [xbn, 'rs', 'pp'], ['hT'])
                else:
                    STT(obs[b % 2][:, c, :], xb[:, c, :], pc(gname, c), rs, ALU.mult, ALU.mult, [xbn, 'rs', 'pp'], [f'ob{b % 2}'])
            if out_dram is not None:
                tok = P.dma(out_dram[:, :, tk], obs[b % 2], reads=[f'ob{b % 2}'])
                out_tokens.append(tok)
        P.barrier()

    out_tokens = []

    def proj_fm(psum_ap, psname, w, wname, c0, m, tk):
        for c in range(8):
            MM(psum_ap, w[:, c, c0:c0 + m], hT[:, c, tk], c == 0, c == 7, [wname, 'hT'], [psname])

    def proj_tm(psum_ap, psname, w, wname, c0, n, t):
        for c in range(8):
            MM(psum_ap, hT[:, c, t * 128:(t + 1) * 128], w[:, c, c0:c0 + n], c == 0, c == 7, [wname, 'hT'], [psname])

    def resid_add(psum_ap, psname, src, dst, s, dc, b, rt, rtname):
        tk = slice(b * 512, (b + 1) * 512)
        xn = f'xr_{s}_{dc}_{b}'
        P.dma(rt, src[:, dc, tk], reads=[xn], writes=[rtname])
        TT('dve', rt, psum_ap, rt, ALU.add, [psname, rtname], [rtname])
        P.dma(dst[:, dc, tk], rt, reads=[rtname], writes=[xn])


    def xnames(s, b):
        return [f'xr_{s}_{dc}_{b}' for dc in range(8)]

    def out_proj(l, s, src, dst):
        A.reset()
        rts = [A.alloc([128, 512], F32) for _ in range(4)]
        k = 0
        for g in range(4):
            w, wn, _, _ = load_w([prm['w_out'][l][:, g * 256:(g + 1) * 256]], 8)
            for dci in range(2):
                dc = g * 2 + dci
                for b in range(4):
                    tk = slice(b * 512, (b + 1) * 512)
                    pb = ps[k % 4]
                    pn = f'ps{k % 4}'
                    for c in range(8):
                        MM(pb[:], w[:, c, dci * 128:(dci + 1) * 128], mT[:, c, tk], c == 0, c == 7, [wn, 'mT'], [pn])
                    resid_add(pb[:], pn, src, dst, s, dc, b, rts[k % 4], f'rt{k % 4}')
                    k += 1
        P.barrier()

    def ffn_phase(l, s, xr):
        TB = 2048
        for tb in range(T // TB):
            A.reset()
            hid = A.alloc([128, NFC, TB], BF16)
            sg = [A.alloc([128, 512], F32) for _ in range(2)]
            rts = [A.alloc([128, 512], F32) for _ in range(3)]
            k = 0
            for fc in range(NFC):
                if fc % 2 == 0:
                    wg_, wgn, _, _ = load_w([prm['w_gate'][l][:, fc * 128:(fc + 2) * 128]], 8)
                    wu_, wun, _, _ = load_w([prm['w_up'][l][:, fc * 128:(fc + 2) * 128]], 8)
                fo = (fc % 2) * 128
                for sb in range(TB // 512):
                    tk = slice(tb * TB + sb * 512, tb * TB + (sb + 1) * 512)
                    hk = slice(sb * 512, (sb + 1) * 512)
                    pg, pgn = ps[(2 * k) % 4], f'ps{(2 * k) % 4}'
                    pu, pun = ps[(2 * k + 1) % 4], f'ps{(2 * k + 1) % 4}'
                    proj_fm(pg[:], pgn, wg_, wgn, fo, 128, tk)
                    proj_fm(pu[:], pun, wu_, wun, fo, 128, tk)
                    ACT(sg[k % 2], pg[:], AF.Silu, [pgn], [f'sg{k % 2}'])
                    TT('dve', hid[:, fc, hk], pu[:], sg[k % 2], ALU.mult, [pun, f'sg{k % 2}'], ['hid'])
                    k += 1
            k = 0
            for g in range(4):
                ws = []
                for (r0, nrc) in ((0, 8), (8, 8), (16, 6)):
                    w, wn, _, _ = load_w([prm['w_down'][l][r0 * 128:(r0 + nrc) * 128, g * 256:(g + 1) * 256]], nrc)
                    ws.append((w, wn, r0, nrc))
                for dci in range(2):
                    dc = g * 2 + dci
                    for sb in range(TB // 512):
                        b = tb * (TB // 512) + sb
                        hk = slice(sb * 512, (sb + 1) * 512)
                        pb, pn = ps[4 + k % 3], f'ps{4 + k % 3}'
                        for (w, wn, r0, nrc) in ws:
                            for c in range(nrc):
                                fcx = r0 + c
                                MM(pb[:], w[:, c, dci * 128:(dci + 1) * 128], hid[:, fcx, hk], fcx == 0, fcx == NFC - 1,
                                   [wn, 'hid'], [pn])
                        resid_add(pb[:], pn, xr, xr, s, dc, b, rts[k % 3], f'rt{k % 3}')
                        k += 1
            P.barrier()

    def gn_alloc(nq):
        return {'sq': A.alloc([128, nq, 64], F32), 's1': A.alloc([128, nq], F32), 's2': A.alloc([128, nq], F32),
                'm2': A.alloc([128, nq], F32)}

    def gn_stats(y, yname, nq, eps, center, g):
        sq, s1, s2, m2 = g['sq'], g['s1'], g['s2'], g['m2']
        if center:
            RSUM(s1, y, [yname], ['gs1'])
            TS('dve', s1, s1, 1.0 / 64, None, ALU.mult, None, ['gs1'], ['gs1'])
            TT('dve', y, y, s1.unsqueeze(2).to_broadcast([128, nq, 64]), ALU.subtract, [yname, 'gs1'], [yname])
        TT('dve', sq, y, y, ALU.mult, [yname], ['gsq'])
        RSUM(s2, sq, ['gsq'], ['gs2'])
        ACT(m2, s2, AF.Sqrt, ['gs2'], ['gm2'], bias=eps, scale=1.0 / 64)
        RCP(m2, m2, ['gm2'], ['gm2'])
        TT('dve', y, y, m2.unsqueeze(2).to_broadcast([128, nq, 64]), ALU.mult, [yname, 'gm2'], [yname])

    def retention_phase(l, s):
        W = prm['w_in'][l]
        for hp in range(3):
            A.reset()
            rot = A.alloc([128, 2, T], F32)
            qT = A.alloc([128, T], BF16)
            kT = A.alloc([128, T], BF16)
            gT = A.alloc([128, T], BF16)
            vtm = A.alloc([128, 16, 128], BF16)
            t1 = A.alloc([128, 512], F32)
            t2 = A.alloc([128, 512], F32)
            wsw = A.alloc([128, 8, 256], BF16)
            P.dma(rot, cst['c_rot'], writes=['rot'])
            c0 = hp * 128
            w, wn, sg_, sgn = load_w([W[:, c0:c0 + 128], W[:, 384 + c0:384 + c0 + 128]], 8)
            for j in range(8):
                src0 = j * 32
                dst0 = (j ^ 1) * 32
                CP('pool', wsw[:, :, dst0:dst0 + 32], sg_[:, :, src0:src0 + 32], sgn, ['wsw'])
            for which, dstT in ((0, qT), (1, kT)):
                for b in range(4):
                    tk = slice(b * 512, (b + 1) * 512)
                    proj_fm(ps[0][:], 'ps0', w, wn, which * 128, 128, tk)
                    proj_fm(ps[1][:], 'ps1', wsw, 'wsw', which * 128, 128, tk)
                    TT('dve', t1, ps[0][:], rot[:, 0, tk], ALU.mult, ['ps0', 'rot'], ['t1'])
                    TT('dve', t2, ps[1][:], rot[:, 1, tk], ALU.mult, ['ps1', 'rot'], ['t2'])
                    TT('pool', dstT[:, tk], t1, t2, ALU.add, ['t1', 't2'], ['qkT'])
            w2, wn2, _, _ = load_w([W[:, 768 + c0:768 + c0 + 128], W[:, 1152 + c0:1152 + c0 + 128]], 8)
            for t in range(16):
                pb, pn = ps[t % 2], f'ps{t % 2}'
                proj_tm(pb[:, 0:128], pn, w2, wn2, 0, 128, t)
                CP('act', vtm[:, t, :], pb[:, 0:128], [pn], ['vtm'])
            for b in range(4):
                tk = slice(b * 512, (b + 1) * 512)
                pb, pn = ps[2 + b % 2], f'ps{2 + b % 2}'
                proj_fm(pb[:], pn, w2, wn2, 128, 128, tk)
                ACT(gT[:, tk], pb[:], AF.Silu, [pn], ['gT'])
            mk = A.alloc([128, 3968], F32)
            pTs = [A.alloc([128, 512], BF16) for _ in range(4)]
            ypair = A.alloc([128, 16, 128], F32)
            tmp = A.alloc([128, 512], F32)
            gsc = gn_alloc(4)
            for hh in range(2):
                h = hp * 2 + hh
                hb = hh * 64
                P.op('pool', lambda e: e.iota(mk, [[1, 3968]], base=-1920, channel_multiplier=-1,
                                              allow_small_or_imprecise_dtypes=True), (), ['mk'])
                ACT(mk, mk, AF.Abs, ['mk'], ['mk'])
                ACT(mk, mk, AF.Exp, ['mk', 'lgam'], ['mk'], bias=float(np.log(0.125)), scale=lgam[:, h:h + 1])
                for qb in range(4):
                    q0 = qb * 512
                    MM(ps[4][:], zt[:, 0:128], zt[:, 128:640], True, True, ['zt'], ['ps4'])
                    NK = 16

                    def score(kt):
                        sb_, sn = ps[kt % 4], f'ps{kt % 4}'
                        MM(sb_[:], kT[hb:hb + 64, kt * 128:(kt + 1) * 128], qT[hb:hb + 64, q0:q0 + 512], True, True,
                           ['qkT'], [sn])
                        off = q0 - kt * 128 + 1920
                        TT('dve', pTs[kt % 4], sb_[:], mk[:, off:off + 512], ALU.mult, [sn, 'mk'], [f'pT{kt % 4}'])

                    def pv(kt):
                        for qi in range(4):
                            MM(ps[4][:, qi * 64:(qi + 1) * 64], pTs[kt % 4][:, qi * 128:(qi + 1) * 128],
                               vtm[:, kt, hb:hb + 64], False, kt == NK - 1 and qi == 3, [f'pT{kt % 4}', 'vtm'], ['ps4'])
                    score(0)
                    score(1)
                    score(2)
                    for kt in range(NK):
                        if kt + 3 < NK:
                            score(kt + 3)
                        pv(kt)
                    yv = ypair[:, qb * 4:(qb + 1) * 4, hb:hb + 64]
                    CP('act', yv, ps[4][:, 0:256].rearrange("p (a b) -> p a b", b=64), ['ps4'], ['ypair'])
                    gn_stats(yv, 'ypair', 4, 1e-5, True, gsc)
            for qb in range(4):
                q0 = qb * 512
                for qi in range(4):
                    TRN(ps[5][:, qi * 128:(qi + 1) * 128], ypair[:, qb * 4 + qi, :], ident[:], ['ypair'], ['ps5'])
                TS('dve', tmp, ps[5][:], pp[:, ppo[f'rgw_{l}'] + hp:ppo[f'rgw_{l}'] + hp + 1],
                   pp[:, ppo[f'rgb_{l}'] + hp:ppo[f'rgb_{l}'] + hp + 1], ALU.mult, ALU.add, ['ps5'], ['tmp'])
                TT('pool', mT[:, hp, q0:q0 + 512], tmp, gT[:, q0:q0 + 512], ALU.mult, ['tmp', 'gT'], ['mT'])
            P.barrier()

    def diff_setup():
        A.reset()
        oh = A.alloc([32, 4096], F32)
        rb = A.alloc([32, 4], F32)
        bv = A.alloc([4, 4096], F32)
        P.dma(oh, cst['c_onehot'], writes=['oh'])
        P.dma(rb, prm['rel_bias'], writes=['rb'])
        for j in range(8):
            MM(ps[0][0:4, :], rb, oh[:, j * 512:(j + 1) * 512], True, True, ['oh', 'rb'], ['ps0'])
            CP('dve', bv[:, j * 512:(j + 1) * 512], ps[0][0:4, :], ['ps0'], ['bv'])
        P.dma(bvec_d, bv, reads=['bv'], writes=['bvec_d'])
        P.barrier()

    def diff_phase(l, s):
        W = prm['w_in'][l]
        lam_init = 0.8 - 0.6 * float(np.exp(-0.3 * l))
        A.reset()
        lvt = A.alloc([128, 4, 32], F32)
        lp = A.alloc([128, 2, 32], F32)
        ls = A.alloc([128, 2], F32)
        nlam = A.alloc([128, 1], F32)
        P.dma(lvt, bass.AP(prm['diff_lambda'].tensor, l * 128, [[0, 128], [32, 4], [1, 32]]), writes=['lvt'])
        TT('dve', lp[:, 0, :], lvt[:, 0, :], lvt[:, 1, :], ALU.mult, ['lvt'], ['lp'])
        TT('dve', lp[:, 1, :], lvt[:, 2, :], lvt[:, 3, :], ALU.mult, ['lvt'], ['lp'])
        RSUM(ls, lp, ['lp'], ['ls'])
        ACT(ls, ls, AF.Exp, ['ls'], ['ls'])
        TT('dve', nlam, ls[:, 1:2], ls[:, 0:1], ALU.subtract, ['ls'], ['nlam'])
        TS('dve', nlam, nlam, -lam_init, None, ALU.add, None, ['nlam'], ['nlam'])
        qc = [A.alloc([128, T], BF16) for _ in range(2)]
        kT = A.alloc([128, T], BF16)
        vaug = A.alloc([128, 16, 65], BF16)
        MSET('pool', qc[0], 0.0, ['qkT'])
        MSET('pool', qc[1], 0.0, ['qkT'])
        MSET('pool', kT[64:128, :], 0.0, ['qkT'])
        bm = A.alloc([128, 3968], F32)
        tmps = [A.alloc([128, 512], F32) for _ in range(5)]
        pTs = [A.alloc([128, 512], BF16) for _ in range(5)]
        opair = A.alloc([128, 16, 128], F32)
        o1 = A.alloc([128, 4, 64], F32)
        rr = A.alloc([128, 2, 4], F32)
        gsc = gn_alloc(4)
        MSET('pool', vaug[:, :, 64:65], 1.0, ['vaug'])
        for h in range(4):
            hb = (h % 2) * 64
            mc = 6 + h // 2
            w, wn, _, _ = load_w([W[:, 3072 + h * 64:3072 + (h + 1) * 64], W[:, 3328 + h * 64:3328 + (h + 1) * 64],
                                  W[:, 3584 + h * 64:3584 + (h + 1) * 64]], 8)
            P.dma(bm, bass.AP(bvec_d.tensor, h * 4096, [[1, 128], [1, 3968]]), reads=['bvec_d'], writes=['bm'])
            for which in (0, 1):
                for b in range(4):
                    tk = slice(b * 512, (b + 1) * 512)
                    pb, pn = ps[b % 2], f'ps{b % 2}'
                    proj_fm(pb[0:64, :], pn, w, wn, which * 64, 64, tk)
                    if which == 1:
                        CP('act', kT[0:64, tk], pb[0:64, :], [pn], ['qkT'])
                    else:
                        CP('act', qc[0][0:32, tk], pb[0:32, :], [pn], ['qkT'])
                        CP('act', qc[1][32:64, tk], pb[32:64, :], [pn], ['qkT'])
            for t in range(16):
                pb, pn = ps[t % 2], f'ps{t % 2}'
                proj_tm(pb[:, 0:64], pn, w, wn, 128, 64, t)
                CP('act', vaug[:, t, 0:64], pb[:, 0:64], [pn], ['vaug'])
            for qb in range(4):
                q0 = qb * 512
                MM(ps[4][:], zt[:, 0:128], zt[:, 128:640], True, True, ['zt'], ['ps4'])
                MM(ps[5][:], zt[:, 0:128], zt[:, 128:640], True, True, ['zt'], ['ps5'])
                NS = 32

                def score(i):
                    kt, c = divmod(i, 2)
                    bi_ = (0, 1, 2, 3, 6)[i % 5]
                    sb_, sn = ps[bi_], f'ps{bi_}'
                    MM(sb_[:], kT[:, kt * 128:(kt + 1) * 128], qc[c][:, q0:q0 + 512], True, True, ['qkT'], [sn])
                    j0 = kt * 128 - q0 + 2047
                    bview = bm[:, j0 - 511:j0 + 1][:, ::-1]
                    STT(tmps[i % 5], sb_[:], float(32 ** -0.5), bview, ALU.mult, ALU.add, [sn, 'bm'], [f'tm{i % 5}'])
                    ACT(pTs[i % 5], tmps[i % 5], AF.Exp, [f'tm{i % 5}'], [f'pT{i % 5}'])

                def pv(i):
                    kt, c = divmod(i, 2)
                    acc, an = ps[4 + c], f'ps{4 + c}'
                    for qi in range(4):
                        MM(acc[:, qi * 65:(qi + 1) * 65], pTs[i % 5][:, qi * 128:(qi + 1) * 128], vaug[:, kt, :],
                           False, i >= NS - 2 and qi == 3, [f'pT{i % 5}', 'vaug'], [an])
                score(0)
                score(1)
                score(2)
                score(3)
                for i in range(NS):
                    if i + 4 < NS:
                        score(i + 4)
                    pv(i)
                a0 = ps[4][:, 0:260].rearrange("p (a b) -> p a b", b=65)
                a1 = ps[5][:, 0:260].rearrange("p (a b) -> p a b", b=65)
                RCP(rr[:, 0, :], a0[:, :, 64], ['ps4'], ['rr'])
                RCP(rr[:, 1, :], a1[:, :, 64], ['ps5'], ['rr'])
                TS('dve', rr[:, 1, :], rr[:, 1, :], nlam, None, ALU.mult, None, ['rr', 'nlam'], ['rr'])
                o0 = opair[:, qb * 4:(qb + 1) * 4, hb:hb + 64]
                TT('dve', o0, a0[:, :, 0:64], rr[:, 0, :].unsqueeze(2).to_broadcast([128, 4, 64]), ALU.mult, ['ps4', 'rr'], ['opair'])
                TT('dve', o1, a1[:, :, 0:64], rr[:, 1, :].unsqueeze(2).to_broadcast([128, 4, 64]), ALU.mult, ['ps5', 'rr'], ['o1'])
                TT('dve', o0, o0, o1, ALU.add, ['opair', 'o1'], ['opair'])
                gn_stats(o0, 'opair', 4, EPS, False, gsc)
            if h % 2 == 1:
                so = ppo[f'sub_{l}']
                for qb in range(4):
                    q0 = qb * 512
                    for qi in range(4):
                        TRN(ps[4][:, qi * 128:(qi + 1) * 128], opair[:, qb * 4 + qi, :], ident[:], ['opair'], ['ps4'])
                    TS('dve', mT[:, mc, q0:q0 + 512], ps[4][:], pp[:, so:so + 1], 1.0 - lam_init,
                       ALU.mult, ALU.mult, ['ps4'], ['mT'])
        P.barrier()

    def rwkv_phase(l, s):
        W = prm['w_in'][l]
        base = 1536
        A.reset()
        Fz = A.alloc([128, 3, T + 2], F32)
        xs = A.alloc([128, 3, T], BF16)
        hw = [A.alloc([128, T], BF16) for _ in range(2)]
        ha = [A.alloc([128, T], BF16) for _ in range(2)]
        hg = A.alloc([128, T], BF16)
        w1b = A.alloc([128, 2, 3, 64], BF16)
        a1b = A.alloc([128, 2, 3, 64], BF16)
        g1b = A.alloc([128, 3, 128], BF16)
        w2b = A.alloc([64, 2, 384], BF16)
        a2b = A.alloc([64, 2, 384], BF16)
        g2b = A.alloc([128, 384], BF16)
        lst = A.alloc([128, 1536], F32)
        cm = A.alloc([128, 6, 3], F32)
        ka1 = A.alloc([128, 3], F32)
        ka2 = A.alloc([128, 3], F32)
        t1 = A.alloc([128, 512], F32)
        t2 = A.alloc([128, 512], F32)
        tsh = [A.alloc([128, 512], F32) for _ in range(2)]
        lwt = [A.alloc([128, 512], F32) for _ in range(2)]
        ast = [A.alloc([128, 512], BF16) for _ in range(2)]
        gtt = [A.alloc([128, 512], BF16) for _ in range(2)]
        for d in range(2):
            P.dma(lst[:, 0:192].rearrange("p (c r) -> p c r", r=64), fm(prm['rwkv_w1'][l, d]), writes=['lst'])
            CP('pool', w1b[:, d], lst[:, 0:192].rearrange("p (c r) -> p c r", r=64), ['lst'], ['lw8'])
            P.dma(lst[:, 192:384].rearrange("p (c r) -> p c r", r=64), fm(prm['rwkv_a1'][l, d]), writes=['lst2'])
            CP('pool', a1b[:, d], lst[:, 192:384].rearrange("p (c r) -> p c r", r=64), ['lst2'], ['lw8'])
            P.dma(lst[0:64, 384:768], prm['rwkv_w2'][l, d], writes=['lst3'])
            CP('pool', w2b[:, d, :], lst[0:64, 384:768], ['lst3'], ['lw8'])
            P.dma(lst[0:64, 768:1152], prm['rwkv_a2'][l, d], writes=['lst4'])
            CP('pool', a2b[:, d, :], lst[0:64, 768:1152], ['lst4'], ['lw8'])
        P.dma(lst[:, 0:384].rearrange("p (c r) -> p c r", r=128), fm(prm['rwkv_g1'][l]), writes=['lst', 'lst2'])
        CP('pool', g1b, lst[:, 0:384].rearrange("p (c r) -> p c r", r=128), ['lst', 'lst2'], ['lw8'])
        P.dma(lst[:, 1152:1536], prm['rwkv_g2'][l], writes=['lst5'])
        CP('pool', g2b, lst[:, 1152:1536], ['lst5'], ['lw8'])
        for f in range(6):
            o0 = ppo[f'mu_{l}_{f}_0']
            o1 = ppo[f'mu_{l}_{f}_1']
            TT('dve', cm[:, f, :], pp[:, o0:o0 + 3], pp[:, o1:o1 + 3], ALU.add, [], ['cm'])
        TS('dve', cm, cm, -1.0, 1.0, ALU.mult, ALU.add, ['cm'], ['cm'])
        oka = ppo[f'ka_{l}']
        TS('dve', ka1, pp[:, oka:oka + 3], -1.0, 1.0, ALU.mult, ALU.add, [], ['ka1'])
        TS('dve', ka2, pp[:, oka:oka + 3], -2.0, 2.0, ALU.mult, ALU.add, [], ['ka2'])

        def shift_mix(Fv, f, j, dst, dname, fname):
            o0 = ppo[f'mu_{l}_{f}_0'] + j
            o1 = ppo[f'mu_{l}_{f}_1'] + j
            for b in range(4):
                b0 = b * 512
                TS('dve', t1, Fv[:, 1 + b0:1 + b0 + 512], cm[:, f, j:j + 1], None, ALU.mult, None, [fname, 'cm'], ['t1'])
                STT(t2, Fv[:, b0:b0 + 512], pp[:, o0:o0 + 1], t1, ALU.mult, ALU.add, [fname, 't1'], ['t2'])
                STT(dst[:, b0:b0 + 512], Fv[:, 2 + b0:2 + b0 + 512], pp[:, o1:o1 + 1], t2, ALU.mult, ALU.add,
                    [fname, 't2'], [dname])

        def proj_to_F(Fv, fname, w, wn, c0):
            for b in range(4):
                tk = slice(b * 512, (b + 1) * 512)
                pb, pn = ps[b % 2], f'ps{b % 2}'
                proj_fm(pb[:], pn, w, wn, c0, 128, tk)
                CP('act', Fv[:, 1 + b * 512:1 + (b + 1) * 512], pb[:], [pn], [fname])

        MSET('pool', Fz[:, :, 0:1], 0.0, ['Fz'])
        MSET('pool', Fz[:, :, T + 1:T + 2], 0.0, ['Fz'])
        for j in range(3):
            w, wn, _, _ = load_w([W[:, base + 1152 + j * 128:base + 1152 + (j + 1) * 128]], 8)
            proj_to_F(Fz[:, j, :], 'Fz', w, wn, 0)
        for (f, kind) in ((3, 'w'), (4, 'a'), (5, 'g')):
            for j in range(3):
                shift_mix(Fz[:, j, :], f, j, xs[:, j, :], 'xs', 'Fz')
            for b in range(4):
                tk = slice(b * 512, (b + 1) * 512)
                if kind == 'g':
                    for j in range(3):
                        MM(ps[2][:], g1b[:, j, :], xs[:, j, tk], j == 0, j == 2, ['xs', 'lw8'], ['ps2'])
                    ACT(hg[:, tk], ps[2][:], AF.Sigmoid, ['ps2'], ['hg'])
                else:
                    for d in range(2):
                        pb, pn = ps[2 + d], f'ps{2 + d}'
                        wl = w1b if kind == 'w' else a1b
                        for j in range(3):
                            MM(pb[0:64, :], wl[:, d, j, :], xs[:, j, tk], j == 0, j == 2, ['xs', 'lw8'], [pn])
                        if kind == 'w':
                            ACT(hw[d][0:64, tk], pb[0:64, :], AF.Tanh, [pn], ['hw'])
                        else:
                            CP('act', ha[d][0:64, tk], pb[0:64, :], [pn], ['ha'])
        k = 0
        for j in range(3):
            cj = slice(j * 128, (j + 1) * 128)
            for b in range(4):
                tk = slice(b * 512, (b + 1) * 512)
                for d in range(2):
                    pb, pn = ps[k % 4], f'ps{k % 4}'
                    MM(pb[:], w2b[:, d, cj], hw[d][0:64, tk], True, True, ['hw', 'lw8'], [pn])
                    ACT(lwt[k % 2], pb[:], AF.Sigmoid, [pn], [f'lwt{k % 2}'], bias=pc(f'w0_{l}_{d}', j))
                    TS('dve', lwt[k % 2], lwt[k % 2], -LOG_E05, None, ALU.mult, None, [f'lwt{k % 2}'], [f'lwt{k % 2}'])
                    P.dma(lw_d[d, cj, tk], lwt[k % 2], reads=[f'lwt{k % 2}'], writes=[f'lw_{d}_{j}_{b}'])
                    k += 1
                    pb, pn = ps[k % 4], f'ps{k % 4}'
                    MM(pb[:], a2b[:, d, cj], ha[d][0:64, tk], True, True, ['ha', 'lw8'], [pn])
                    ACT(ast[k % 2], pb[:], AF.Sigmoid, [pn], [f'ast{k % 2}'], bias=pc(f'a0_{l}_{d}', j))
                    P.dma(as_d[d, cj, tk], ast[k % 2], reads=[f'ast{k % 2}'], writes=[f'as_{d}_{j}_{b}'])
                    k += 1
                pb, pn = ps[k % 4], f'ps{k % 4}'
                MM(pb[:], g2b[:, cj], hg[:, tk], True, True, ['hg', 'lw8'], [pn])
                CP('act', gtt[k % 2], pb[:], [pn], [f'gtt{k % 2}'])
                P.dma(gate_d[cj, tk], gtt[k % 2], reads=[f'gtt{k % 2}'], writes=[f'gate_{j}_{b}'])
                k += 1
        P.barrier()

        for j in range(3):
            cj = slice(j * 128, (j + 1) * 128)
            A.reset()
            xr = A.alloc([128, T], BF16)
            xk = A.alloc([128, T], BF16)
            xv = A.alloc([128, T], BF16)
            Vtm = A.alloc([128, 16, 128], BF16)
            ysum = A.alloc([128, 16, 128], F32)
            kkn = A.alloc([128, T], BF16)
            bonus = A.alloc([128, T], BF16)
            ka1 = A.alloc([128, 3], F32)
            identb = A.alloc([128, 128], BF16)
            gsc = gn_alloc(8)
            gtile = A.alloc([128, 512], BF16)
            te1 = A.alloc([128, 512], F32)
            mark = A.off
            Fv = A.alloc([128, T + 2], F32)
            cm = A.alloc([128, 6, 3], F32)
            ka2 = A.alloc([128, 3], F32)
            t1 = A.alloc([128, 512], F32)
            t2 = A.alloc([128, 512], F32)
            tsh = [A.alloc([128, 512], F32) for _ in range(2)]
            as0 = A.alloc([128, 512], BF16)
            as1 = A.alloc([128, 512], BF16)
            CP('pool', identb, ident[:], [], ['identb'])
            for f in range(6):
                o0 = ppo[f'mu_{l}_{f}_0']
                o1 = ppo[f'mu_{l}_{f}_1']
                TT('dve', cm[:, f, :], pp[:, o0:o0 + 3], pp[:, o1:o1 + 3], ALU.add, [], ['cm'])
            TS('dve', cm, cm, -1.0, 1.0, ALU.mult, ALU.add, ['cm'], ['cm'])
            TS('dve', ka1, pp[:, oka:oka + 3], -1.0, 1.0, ALU.mult, ALU.add, [], ['ka1'])
            TS('dve', ka2, pp[:, oka:oka + 3], -2.0, 2.0, ALU.mult, ALU.add, [], ['ka2'])
            MSET('pool', Fv[:, 0:1], 0.0, ['Fv'])
            MSET('pool', Fv[:, T + 1:T + 2], 0.0, ['Fv'])
            w, wn, _, _ = load_w([W[:, base + j * 128:base + (j + 1) * 128],
                                  W[:, base + 384 + j * 128:base + 384 + (j + 1) * 128]], 8)
            wv_, wvn, _, _ = load_w([W[:, base + 768 + j * 128:base + 768 + (j + 1) * 128]], 8)
            proj_to_F(Fv, 'Fv', w, wn, 0)
            shift_mix(Fv, 0, j, xr, 'xr', 'Fv')
            proj_to_F(Fv, 'Fv', w, wn, 128)
            shift_mix(Fv, 1, j, xk, 'xk', 'Fv')
            proj_to_F(Fv, 'Fv', wv_, wvn, 0)
            shift_mix(Fv, 2, j, xv, 'xv', 'Fv')
            okk = ppo[f'kk_{l}'] + j
            ork = ppo[f'rk_{l}'] + j
            for b in range(4):
                tk = slice(b * 512, (b + 1) * 512)
                TS('dve', t1, xk[:, tk], pp[:, okk:okk + 1], None, ALU.mult, None, ['xk'], ['t1'])
                ACT(t2, t1, AF.Square, ['t1'], ['t2'])
                MM(ps[0][:], bdm[:], t2, True, True, ['t2'], ['ps0'])
                ACT(t2, ps[0][:], AF.Sqrt, ['ps0'], ['t2'])
                TS('dve', t2, t2, 1e-12, None, ALU.max, None, ['t2'], ['t2'])
                RCP(t2, t2, ['t2'], ['t2'])
                TT('dve', kkn[:, tk], t1, t2, ALU.mult, ['t1', 't2'], ['kkn'])
                P.dma(as0, as_d[0, cj, tk], reads=[f'as_0_{j}_{b}'], writes=['as0'])
                P.dma(as1, as_d[1, cj, tk], reads=[f'as_1_{j}_{b}'], writes=['as1'])
                TT('pool', t1, as0, as1, ALU.add, ['as0', 'as1'], ['t1'])
                TS('dve', t1, t1, pp[:, oka + j:oka + j + 1], ka2[:, j:j + 1], ALU.mult, ALU.add, ['t1', 'ka2'], ['t1'])
                TT('dve', t1, t1, xk[:, tk], ALU.mult, ['t1', 'xk'], ['t1'])
                TT('dve', t1, t1, xr[:, tk], ALU.mult, ['t1', 'xr'], ['t1'])
                TS('dve', t2, t1, pp[:, ork:ork + 1], None, ALU.mult, None, ['t1'], ['t2'])
                MM(ps[1][:], bdm[:], t2, True, True, ['t2'], ['ps1'])
                TT('dve', bonus[:, tk], ps[1][:], xv[:, tk], ALU.mult, ['ps1', 'xv'], ['bonus'])
            for t in range(16):
                TRN(psb[:, (t % 8) * 128:(t % 8 + 1) * 128], xv[:, t * 128:(t + 1) * 128], identb, ['xv', 'identb'], ['ps7'])
                if t % 8 == 7:
                    g8 = t // 8
                    CP('act', Vtm[:, g8 * 8:(g8 + 1) * 8, :], psb.rearrange("p (a b) -> p a b", b=128), ['ps7'], ['Vtm'])
            MSET('pool', ysum, 0.0, ['ysum'])
            P.barrier()
            A.off = mark

            def unit_alloc():
                u = {}
                for nm in ('lwt', 'cl', 'gi', 'gv', 't1', 't2'):
                    u[nm] = A.alloc([128, 512], F32)
                for nm in ('as0', 'bb', 'Bt', 'Bbar', 'kd', 'Kt', 'Kbar'):
                    u[nm] = A.alloc([128, 512], BF16)
                u['AR'] = A.alloc([128, 2, 512], BF16)
                u['BKz'] = A.alloc([128, 16, 128], BF16).rearrange("p (a b c) d -> p a b c d", b=2, c=2)
                u['G'] = [A.alloc([128, 512], BF16) for _ in range(2)]
                u['Nm'] = A.alloc([128, 256], BF16)
                u['MM2'] = [A.alloc([128, 512], BF16) for _ in range(2)]
                u['Mb'] = [m_[:, 0:256] for m_ in u['MM2']]
                u['MTb'] = [m_[:, 256:512] for m_ in u['MM2']]
                u['PTb'] = [A.alloc([128, 256], BF16) for _ in range(2)]
                u['RHSb'] = A.alloc([128, 128], BF16)
                u['Ub'] = A.alloc([128, 128], BF16)
                u['Hf'] = A.alloc([128, 64], F32)
                u['Hs'] = A.alloc([128, 64], BF16)
                return u

            def unit(d, u):
                n = lambda s_: f'{s_}_{d}'
                lwt_, cl, gi, gv, t1, t2 = u['lwt'], u['cl'], u['gi'], u['gv'], u['t1'], u['t2']
                as0, bb, Bt, Bbar, kd, Kt, Kbar = u['as0'], u['bb'], u['Bt'], u['Bbar'], u['kd'], u['Kt'], u['Kbar']
                AR, BKz, G, Nm, Mb, MTb, PTb = u['AR'], u['BKz'], u['G'], u['Nm'], u['Mb'], u['MTb'], u['PTb']
                RHSb, Ub, Hf, Hs = u['RHSb'], u['Ub'], u['Hf'], u['Hs']
                Xb, Yb, Zb = ps[3 * d], ps[3 * d + 1], ps[3 * d + 2]
                pA = [Xb[:, 0:256], Yb[:, 0:256]]
                pB = [Xb[:, 256:512], Yb[:, 256:512]]
                pAn = [f'ps{3 * d}', f'ps{3 * d + 1}']
                pBn = pAn
                pD, pE = Zb[:, 0:256], Zb[:, 256:512]
                pDn, pEn = f'ps{3 * d + 2}', f'ps{3 * d + 2}'
                pFh = [Xb[:, 384:512], Yb[:, 384:512]]
                pU, pS = Xb[:, 256:384], Yb[:, 256:320]
                pUn, pSn = pAn[0], pAn[1]
                pT = psb
                pTn = 'ps7'
                MSET('pool', Hf, 0.0, [n('Hf')])
                MSET('pool', Hs, 0.0, [n('Hs')])
                MSET('pool', BKz, 0.0, [n('BKz')])
                for bi in range(4):
                    b = bi if d == 0 else 3 - bi
                    tk = slice(b * 512, (b + 1) * 512)
                    P.dma(lwt_, lw_d[d, cj, tk], reads=[f'lw_{d}_{j}_{b}'], writes=[n('lwt')])
                    P.dma(as0, as_d[d, cj, tk], reads=[f'as_{d}_{j}_{b}'], writes=[n('as0')])
                    yield
                    if d == 0:
                        P.op('dve', lambda e, cl=cl, lwt_=lwt_: e.tensor_tensor_scan(cl, rst[:], lwt_, 0.0, ALU.mult, ALU.add),
                             [n('lwt')], [n('cl')])
                    else:
                        P.op('dve', lambda e, cl=cl, lwt_=lwt_: e.tensor_tensor_scan(cl[:, ::-1], rst[:], lwt_[:, ::-1], 0.0,
                                                                                    ALU.mult, ALU.add), [n('lwt')], [n('cl')])
                    cl3 = cl.rearrange("p (a b) -> p a b", b=128)
                    tot = cl3[:, :, 127:128] if d == 0 else cl3[:, :, 0:1]
                    ACT(gi, cl, AF.Exp, [n('cl')], [n('gi')])
                    ACT(gv, cl, AF.Exp, [n('cl')], [n('gv')], scale=-1.0)
                    TT('pool', t1, cl, lwt_, ALU.subtract, [n('cl'), n('lwt')], [n('t1')])
                    ACT(t1, t1, AF.Exp, [n('t1')], [n('t1')])
                    TT('dve', t2.rearrange("p (a b) -> p a b", b=128), cl3, tot.to_broadcast([128, 4, 128]), ALU.subtract,
                       [n('cl')], [n('t2')])
                    ACT(t2, t2, AF.Exp, [n('t2')], [n('t2')], scale=-1.0)
                    yield
                    STT(AR[:, 0, :], kkn[:, tk], -1.0, t1, ALU.mult, ALU.mult, ['kkn', n('t1')], [n('AR')])
                    TT('pool', AR[:, 1, :], xr[:, tk], gi, ALU.mult, ['xr', n('gi')], [n('AR')])
                    TT('pool', bb, kkn[:, tk], as0, ALU.mult, ['kkn', n('as0')], [n('bb')])
                    TT('dve', Bt, bb, gv, ALU.mult, [n('bb'), n('gv')], [n('Bt')])
                    TT('pool', Bbar, bb, t2, ALU.mult, [n('bb'), n('t2')], [n('Bbar')])
                    TS('dve', kd, as0, pp[:, oka + j:oka + j + 1], ka1[:, j:j + 1], ALU.mult, ALU.add, [n('as0'), 'ka1'], [n('kd')])
                    TT('pool', kd, kd, xk[:, tk], ALU.mult, [n('kd'), 'xk'], [n('kd')])
                    TT('dve', Kt, kd, gv, ALU.mult, [n('kd'), n('gv')], [n('Kt')])
                    TT('pool', Kbar, kd, t2, ALU.mult, [n('kd'), n('t2')], [n('Kbar')])
                    yield
                    for w_, src_, sn_ in ((0, Bbar, n('Bbar')), (1, Kbar, n('Kbar'))):
                        for ci in range(4):
                            TRN(pT[:, (w_ * 4 + ci) * 128:(w_ * 4 + ci + 1) * 128], src_[:, ci * 128:(ci + 1) * 128], identb,
                                [sn_, 'identb'], [pTn])
                    for w_ in range(2):
                        for ci in range(4):
                            for hh in range(2):
                                c0_ = (w_ * 4 + ci) * 128 + hh * 64
                                CP('act', BKz[:, ci, w_, hh, hh * 64:(hh + 1) * 64], pT[:, c0_:c0_ + 64],
                                   [pTn], [n('BKz')])
                    yield
                    for cii in range(4):
                        ci = cii if d == 0 else 3 - cii
                        cc = slice(ci * 128, (ci + 1) * 128)
                        cg = b * 4 + ci
                        for hh in range(2):
                            hb = hh * 64
                            MM(pA[hh], Bt[hb:hb + 64, cc], AR[hb:hb + 64, :, cc], True, True, [n('Bt'), n('AR')], [pAn[hh]])
                            MM(pB[hh], Kt[hb:hb + 64, cc], AR[hb:hb + 64, :, cc], True, True, [n('Kt'), n('AR')], [pBn[hh]])
                        yield
                        for hh in range(2):
                            TT('dve', G[hh][:, 0:256], pA[hh], mtr[:, d, 0:256], ALU.mult, [pAn[hh]], [n(f'G{hh}')])
                            TT('dve', G[hh][:, 256:512], pB[hh], mtr[:, d, 0:256], ALU.mult, [pBn[hh]], [n(f'G{hh}')])
                        for hh in range(2):
                            hb = hh * 64
                            MM(pA[hh][:, 0:128], AR[hb:hb + 64, 0, cc], Bt[hb:hb + 64, cc], True, True,
                               [n('Bt'), n('AR')], [pAn[hh]])
                        yield
                        for hh in range(2):
                            TT('dve', Nm[:, hh * 128:(hh + 1) * 128], pA[hh][:, 0:128], mlm[:, d, 0:128], ALU.mult,
                               [pAn[hh]], [n('Nm')])
                        NTh = [G[hh][:, 0:128] for hh in range(2)]
                        ARBh = [G[hh][:, 128:256] for hh in range(2)]
                        AKh = [G[hh][:, 256:384] for hh in range(2)]
                        ARKh = [G[hh][:, 384:512] for hh in range(2)]
                        for hh in range(2):
                            TT('dve', PTb[0][:, hh * 128:(hh + 1) * 128], NTh[hh], ident[:], ALU.add, [n(f'G{hh}')], [n('PTb0')])
                        curM = [Nm[:, hh * 128:(hh + 1) * 128] for hh in range(2)]
                        curMn = [n('Nm')] * 2
                        curMT = NTh
                        curMTn = [n('G0'), n('G1')]
                        pcur = 0
                        yield
                        for lev in range(1, 7):
                            sl_ = lev % 2
                            for hh in range(2):
                                MM(pD[:, hh * 128:(hh + 1) * 128], curMT[hh], curM[hh], True, True,
                                   [curMn[hh], curMTn[hh]], [pDn])
                            if lev < 6:
                                for hh in range(2):
                                    MM(pE[:, hh * 128:(hh + 1) * 128], curM[hh], curMT[hh], True, True,
                                       [curMn[hh], curMTn[hh]], [pEn])
                            yield
                            if lev < 6:
                                CP('act', u['MM2'][sl_], Zb[:], [pDn], [n(f'Mb{sl_}'), n(f'MTb{sl_}')])
                            else:
                                CP('act', Mb[sl_], pD, [pDn], [n(f'Mb{sl_}')])
                            curM = [Mb[sl_][:, hh * 128:(hh + 1) * 128] for hh in range(2)]
                            curMn = [n(f'Mb{sl_}')] * 2
                            curMT = [MTb[sl_][:, hh * 128:(hh + 1) * 128] for hh in range(2)]
                            curMTn = [n(f'MTb{sl_}')] * 2
                            for hh in range(2):
                                MM(pFh[hh], curM[hh], PTb[pcur][:, hh * 128:(hh + 1) * 128], True, True,
                                   [curMn[hh], n(f'PTb{pcur}')], [pAn[hh]])
                            yield
                            for hh in range(2):
                                TT('dve', PTb[1 - pcur][:, hh * 128:(hh + 1) * 128], pFh[hh], PTb[pcur][:, hh * 128:(hh + 1) * 128],
                                   ALU.add, [pAn[hh], n(f'PTb{pcur}')], [n(f'PTb{1 - pcur}')])
                            pcur = 1 - pcur
                        PTc = PTb[pcur]
                        PTn = n(f'PTb{pcur}')
                        for hh in range(2):
                            hb = hh * 64
                            vs = slice(hh * 64, (hh + 1) * 64)
                            o = pA[hh][:, 128:192]
                            MM(o, AR[hb:hb + 64, 0, cc], Hs[hb:hb + 64, :], True, False, [n('AR'), n('Hs')], [pAn[hh]])
                            MM(o, AKh[hh], Vtm[:, cg, vs], False, True, [n(f'G{hh}'), 'Vtm'], [pAn[hh]])
                        yield
                        CP('act', RHSb[:, 0:64], pA[0][:, 128:192], [pAn[0]], [n('RHSb')])
                        CP('dve', RHSb[:, 64:128], pA[1][:, 128:192], [pAn[1]], [n('RHSb')])
                        for hh in range(2):
                            vs = slice(hh * 64, (hh + 1) * 64)
                            MM(pU[:, hh * 64:(hh + 1) * 64], PTc[:, hh * 128:(hh + 1) * 128], RHSb[:, vs], True, True,
                               [PTn, n('RHSb')], [pUn])
                        yield
                        CP('dve', Ub, pU, [pUn], [n('Ub')])
                        for hh in range(2):
                            hb = hh * 64
                            vs = slice(hh * 64, (hh + 1) * 64)
                            o = pA[hh][:, 192:256]
                            MM(o, AR[hb:hb + 64, 1, cc], Hs[hb:hb + 64, :], True, False, [n('AR'), n('Hs')], [pAn[hh]])
                            MM(o, ARBh[hh], Ub[:, vs], False, False, [n(f'G{hh}'), n('Ub')], [pAn[hh]])
                            MM(o, ARKh[hh], Vtm[:, cg, vs], False, True, [n(f'G{hh}'), 'Vtm'], [pAn[hh]])
                        o = pS
                        for hh in range(2):
                            vs = slice(hh * 64, (hh + 1) * 64)
                            MM(o, BKz[:, ci, 0, hh, :], Ub[:, vs], hh == 0, False, [n('BKz'), n('Ub')], [pSn])
                            MM(o, BKz[:, ci, 1, hh, :], Vtm[:, cg, vs], False, hh == 1, [n('BKz'), 'Vtm'], [pSn])
                        yield
                        for hh in range(2):
                            vs = slice(hh * 64, (hh + 1) * 64)
                            TT('dve', ysum[:, cg, vs], pA[hh][:, 192:256], ysum[:, cg, vs], ALU.add,
                               [pAn[hh], f'ysum{cg}'], [f'ysum{cg}'])
                        gcol = gi[:, ci * 128 + 127:ci * 128 + 128] if d == 0 else gi[:, ci * 128:ci * 128 + 1]
                        STT(Hf, Hf, gcol, pS, ALU.mult, ALU.add, [n('Hf'), n('gi'), pSn], [n('Hf')])
                        CP('act', Hs, Hf, [n('Hf')], [n('Hs')])
                        yield

            units = [unit(d, unit_alloc()) for d in range(2)]
            active = list(units)
            first = True
            import os
            if os.environ.get('RW_SERIAL'):
                for g_ in units:
                    for _ in g_:
                        pass
                active = []
            NDUM = int(os.environ.get('RW_DUMMY', '0'))
            while active:
                for g_ in list(active):
                    try:
                        next(g_)
                    except StopIteration:
                        active.remove(g_)
                    for _ in range(NDUM):
                        MM(ps[6][:], zt[:, 0:128], zt[:, 128:640], True, True, ['zt'], ['ps6'])
            P.barrier()
            for g4 in range(4):
                tk = slice(g4 * 512, (g4 + 1) * 512)
                y8 = ysum[:, g4 * 4:(g4 + 1) * 4, :].rearrange("p a (h n) -> p (a h) n", n=64)
                gn_stats(y8, 'ysum', 8, 64e-5, True, gsc)
                for qi in range(4):
                    TRN(ps[3][:, qi * 128:(qi + 1) * 128], ysum[:, g4 * 4 + qi, :], ident[:], ['ysum'], ['ps3'])
                olw = ppo[f'lnw_{l}'] + j
                olb = ppo[f'lnb_{l}'] + j
                TS('dve', te1, ps[3][:], pp[:, olw:olw + 1], pp[:, olb:olb + 1], ALU.mult, ALU.add, ['ps3'], ['te1'])
                TT('pool', te1, te1, bonus[:, tk], ALU.add, ['te1', 'bonus'], ['te1'])
                P.dma(gtile, gate_d[cj, tk], reads=[f'gate_{j}_{g4}'], writes=['gtile'])
                TT('dve', mT[:, 3 + j, tk], te1, gtile, ALU.mult, ['te1', 'gtile'], ['mT'])
            P.barrier()

    diff_setup()
    for s in range(nseq):
        for l in range(depth):
            src = xview(xT_in, s) if l == 0 else xview(xres, s)
            dst = xview(xres, s)
            rmsnorm(src, f'g1_{l}', s)
            if 'ret' in mixers:
                retention_phase(l, s)
            else:
                MSET('pool', mT[:, 0:3, :], 0.0, ['mT'])
            if 'rwkv' in mixers:
                rwkv_phase(l, s)
            else:
                MSET('pool', mT[:, 3:6, :], 0.0, ['mT'])
            if 'diff' in mixers:
                diff_phase(l, s)
            else:
                MSET('pool', mT[:, 6:8, :], 0.0, ['mT'])
            P.barrier()
            out_proj(l, s, src, dst)
            if ffn:
                rmsnorm(dst, f'g2_{l}', s)
                ffn_phase(l, s, dst)
        rmsnorm(xview(xres, s), 'gf', s, out_dram=xview(outT, s))
    P.wait_all_outputs(out_tokens)
    P.finish()
    return nc, P


_CACHE = {}


def kernel(**inputs):
    x = np.asarray(inputs['x'], dtype=np.float32)
    consts = host_consts()
    if 'nc' not in _CACHE:
        _CACHE['nc'] = build()[0]
    nc = _CACHE['nc']
    params = {k: np.ascontiguousarray(np.asarray(inputs[k], dtype=np.float32)) for k in PARAM_SHAPES}
    in_maps = []
    for c in range(8):
        xs = np.ascontiguousarray(x[2 * c:2 * c + 2].reshape(2 * T, D).T)
        m = {'xT': xs}
        m.update(params)
        m.update(consts)
        in_maps.append(m)
    res = run_bass_kernel_spmd(nc, in_maps, core_ids=list(range(8)))
    out = np.empty((16, T, D), np.float32)
    for c in range(8):
        o = np.asarray(res.results[c]['outT'])
        out[2 * c:2 * c + 2] = o.T.reshape(2, T, D)
    return out
```

```python
import contextlib
import numpy as np
import concourse.bass as bass
import concourse.mybir as mybir

F32 = mybir.dt.float32
BF16 = mybir.dt.bfloat16
I32 = mybir.dt.int32
ALU = mybir.AluOpType
AF = mybir.ActivationFunctionType
AX = mybir.AxisListType

EPOCH = 16000
N_DMA_SEMS = 40


class Prog:
    def __init__(self, nc, same_engine_sync=True):
        self.nc = nc
        self.stack = contextlib.ExitStack()
        self.engs = ['pe', 'act', 'dve', 'pool', 'sp']
        self.ops = {e: [] for e in self.engs}
        self.count = {e: 0 for e in self.engs}
        self.csems = {e: [] for e in self.engs}
        self.seen = {e: {} for e in self.engs}
        self.last_write = {}
        self.readers = {}
        self.dma_sems = []
        self.dma_uses = []
        self.dma_rr = 0
        self.same_engine_sync = same_engine_sync
        self.n_inst = 0
        self.out_tokens = []
        self.last_tok = {}
        self.dma_last = {}

    def sem(self, name):
        return self.stack.enter_context(self.nc.semaphore(name))

    def sbuf(self, name, shape, dtype):
        return self.stack.enter_context(self.nc.sbuf_tensor(name, list(shape), dtype))

    def psum(self, name, shape, dtype):
        return self.stack.enter_context(self.nc.psum_tensor(name, list(shape), dtype))

    def _csem(self, e, epoch):
        while len(self.csems[e]) <= epoch:
            self.csems[e].append(self.sem(f"c_{e}_{len(self.csems[e])}"))
        return self.csems[e][epoch]

    def _deps(self, reads, writes):
        deps = []
        for r in reads:
            t = self.last_write.get(r)
            if t is not None:
                deps.append(t)
        for w in writes:
            t = self.last_write.get(w)
            if t is not None:
                deps.append(t)
            for t in self.readers.get(w, {}).values():
                deps.append(t)
        return deps

    def _commit(self, token, reads, writes):
        for r in reads:
            self.readers.setdefault(r, {})[token[0]] = token
        for w in writes:
            self.last_write[w] = token
            self.readers[w] = {}

    def _waits(self, e, deps):
        need = {}
        for (src, sem_id, sem, val) in deps:
            if src == e and (e == 'pe' or not self.same_engine_sync):
                continue
            if self.seen[e].get(sem_id, 0) >= val:
                continue
            if need.get(sem_id, (None, 0))[1] < val:
                need[sem_id] = (sem, val)
        out = []
        for sem_id, (sem, val) in need.items():
            self.seen[e][sem_id] = val
            out.append((sem, val))
        return out

    def op(self, e, fn, reads=(), writes=()):
        reads = list(reads)
        writes = list(writes)
        deps = self._deps(reads, writes)
        waits = self._waits(e, deps)
        k = self.count[e]
        self.count[e] += 1
        epoch, idx = divmod(k, EPOCH)
        sem = self._csem(e, epoch)
        token = (e, (e, epoch), sem, idx + 1)
        self.last_tok[e] = token
        self._commit(token, reads, writes)

        def emit(eng, fn=fn, waits=waits, sem=sem):
            for (s, v) in waits:
                eng.wait_ge(s, v)
            fn(eng).then_inc(sem, 1)
        self.ops[e].append(emit)
        self.n_inst += 1 + len(waits)
        return token

    def dma(self, out_ap, in_ap, reads=(), writes=(), queue='sp', **kw):
        reads = list(reads)
        writes = list(writes)
        if not self.dma_sems:
            for i in range(N_DMA_SEMS):
                self.dma_sems.append(self.sem(f"dma_{i}"))
                self.dma_uses.append(0)
        si = self.dma_rr
        self.dma_rr = (self.dma_rr + 1) % N_DMA_SEMS
        sem = self.dma_sems[si]
        prev = self.dma_uses[si] * 16
        self.dma_uses[si] += 1
        target = prev + 16
        deps = self._deps(reads, writes)
        if prev > 0:
            deps.append(('dmaprev', ('dma', si), sem, prev))
        waits = self._waits(queue, deps)
        token = (f'dma{si}_{target}', ('dma', si), sem, target)
        self.dma_last[si] = token
        self._commit(token, reads, writes)

        def emit(eng, waits=waits, sem=sem, out_ap=out_ap, in_ap=in_ap, kw=kw):
            for (s, v) in waits:
                eng.wait_ge(s, v)
            eng.dma_start(out=out_ap, in_=in_ap, **kw).then_inc(sem, 16)
        self.ops[queue].append(emit)
        self.n_inst += 1 + len(waits)
        return token

    def barrier(self):
        toks = list(self.last_tok.values()) + list(self.dma_last.values())
        for e in self.engs:
            waits = self._waits(e, [t for t in toks if t[0] != e])

            def emit(eng, waits=waits):
                for (s, v) in waits:
                    eng.wait_ge(s, v)
            self.ops[e].append(emit)
            self.n_inst += len(waits)
        self.last_write = {}
        self.readers = {}

    def wait_all_outputs(self, tokens, e='sp'):
        waits = self._waits(e, tokens)

        def emit(eng, waits=waits):
            for (s, v) in waits:
                eng.wait_ge(s, v)
        self.ops[e].append(emit)

    def finish(self):
        nc = self.nc
        ops = self.ops
        with nc.Block() as block:
            @block.tensor
            def _(eng):
                for f in ops['pe']:
                    f(eng)

            @block.scalar
            def _(eng):
                for f in ops['act']:
                    f(eng)

            @block.vector
            def _(eng):
                for f in ops['dve']:
                    f(eng)

            @block.gpsimd
            def _(eng):
                for f in ops['pool']:
                    f(eng)

            @block.sync
            def _(eng):
                for f in ops['sp']:
                    f(eng)
        self.stack.close()

from concourse.bass_utils import run_bass_kernel_spmd

T = 2048
D = 1024
DIN = 3840
DFF = 2816
NFC = 22
EPS = 1e-6
LOG_E05 = 0.6065306597126334


class Arena:
    def __init__(self, ap_f32, nwords):
        self.ap = ap_f32
        self.n = nwords
        self.off = 0

    def reset(self):
        self.off = 0

    def alloc(self, shape, dtype):
        nel = 1
        for s in shape[1:]:
            nel *= s
        if dtype == BF16:
            nw = (nel + 1) // 2
        else:
            nw = nel
        nw = (nw + 1) // 2 * 2
        assert self.off + nw <= self.n, f"arena overflow {self.off}+{nw}>{self.n}"
        v = self.ap[0:shape[0], self.off:self.off + nw]
        self.off += nw
        if dtype != F32:
            v = v.bitcast(dtype)
        v = v[:, 0:nel]
        if len(shape) == 3:
            v = v.rearrange("p (a b) -> p a b", b=shape[2])
        elif len(shape) == 4:
            v = v.rearrange("p (a b c) -> p a b c", b=shape[2], c=shape[3])
        return v


def host_consts():
    c = {}
    c['c_ident'] = np.eye(128, dtype=np.float32)
    c['c_ones'] = np.ones((128, 128), np.float32)
    bd = np.zeros((128, 128), np.float32)
    bd[:64, :64] = 1
    bd[64:, 64:] = 1
    c['c_bd'] = bd
    half = 32
    freqs = (np.float32(10000.0) ** (-np.arange(half, dtype=np.float32) / np.float32(half))).astype(np.float32)
    pos = np.arange(T, dtype=np.float32)
    ang = (pos[None, :] * freqs[:, None]).astype(np.float32)
    cs = np.cos(ang).astype(np.float32)
    sn = np.sin(ang).astype(np.float32)
    rot = np.zeros((128, 2, T), np.float32)
    for p in range(128):
        rot[p, 0] = cs[p % 32]
        rot[p, 1] = -sn[p % 32] if (p % 64) < 32 else sn[p % 32]
    c['c_rot'] = rot
    d = np.arange(4096, dtype=np.int64) - 2047
    n = np.abs(d)
    nf = np.maximum(n, 1).astype(np.float32)
    lg = (np.log(nf / np.float32(8.0)) / np.float32(np.log(16.0)) * np.float32(8.0)).astype(np.float32)
    large = np.minimum(8 + lg.astype(np.int32), 15)
    bucket = np.where(d > 0, 16, 0) + np.where(n < 8, n, large)
    oh = np.zeros((32, 4096), np.float32)
    oh[bucket, np.arange(4096)] = 1.0
    oh[:, 4095] = 0.0
    c['c_onehot'] = oh
    r = np.arange(128)[:, None]
    q = np.arange(128)[None, :]
    su = (q > r).astype(np.float32)
    ui = (q >= r).astype(np.float32)
    sl = (q < r).astype(np.float32)
    li = (q <= r).astype(np.float32)
    mtr = np.zeros((128, 2, 256), np.float32)
    mtr[:, 0] = np.concatenate([su, ui], axis=1)
    mtr[:, 1] = np.concatenate([sl, li], axis=1)
    c['c_mtr'] = mtr
    ml = np.zeros((128, 2, 128), np.float32)
    ml[:, 0] = sl
    ml[:, 1] = su
    c['c_ml'] = ml
    rst = np.ones((128, 512), np.float32)
    rst[:, ::128] = 0.0
    c['c_rst'] = rst
    lgam = np.log1p(-np.exp2(-5.0 - np.arange(6, dtype=np.float32))).astype(np.float32)
    c['c_lgam'] = np.tile(lgam[None, :], (128, 1)).astype(np.float32)
    return c


CONST_SHAPES = {
    'c_ident': [128, 128], 'c_ones': [128, 128], 'c_bd': [128, 128], 'c_rot': [128, 2, T],
    'c_onehot': [32, 4096], 'c_mtr': [128, 2, 256], 'c_ml': [128, 2, 128], 'c_rst': [128, 512],
    'c_lgam': [128, 6],
}

PARAM_SHAPES = {
    'mix_norm_g': [2, 1024], 'w_in': [2, 1024, 3840], 'w_out': [2, 1024, 1024],
    'ret_gn_w': [2, 384], 'ret_gn_b': [2, 384], 'rwkv_mu': [2, 6, 2, 384], 'rwkv_w0': [2, 2, 384],
    'rwkv_w1': [2, 2, 384, 64], 'rwkv_w2': [2, 2, 64, 384], 'rwkv_a0': [2, 2, 384],
    'rwkv_a1': [2, 2, 384, 64], 'rwkv_a2': [2, 2, 64, 384], 'rwkv_g1': [2, 384, 128],
    'rwkv_g2': [2, 128, 384], 'rwkv_k_k': [2, 384], 'rwkv_k_a': [2, 384], 'rwkv_r_k': [2, 6, 64],
    'rwkv_ln_w': [2, 384], 'rwkv_ln_b': [2, 384], 'diff_lambda': [2, 4, 32], 'diff_subln_w': [2, 64],
    'rel_bias': [32, 4], 'ffn_norm_g': [2, 1024], 'w_gate': [2, 1024, 2816], 'w_up': [2, 1024, 2816],
    'w_down': [2, 2816, 1024], 'final_norm_g': [1024],
}


def build(nseq=2, depth=2, mixers=('ret', 'rwkv', 'diff'), ffn=True):
    nc = bass.Bass("TRN2", target_bir_lowering=False)
    NTOK = nseq * T
    xT_in = nc.dram_tensor("xT", [D, NTOK], F32, kind="ExternalInput").ap()
    prm = {k: nc.dram_tensor(k, shp, F32, kind="ExternalInput").ap() for k, shp in PARAM_SHAPES.items()}
    cst = {k: nc.dram_tensor(k, shp, F32, kind="ExternalInput").ap() for k, shp in CONST_SHAPES.items()}
    outT = nc.dram_tensor("outT", [D, NTOK], F32, kind="ExternalOutput").ap()
    xres = nc.dram_tensor("xres", [D, NTOK], F32).ap()
    bvec_d = nc.dram_tensor("bvec_d", [4, 4096], F32).ap()
    lw_d = nc.dram_tensor("rw_lw", [2, 384, T], F32).ap()
    as_d = nc.dram_tensor("rw_as", [2, 384, T], BF16).ap()
    gate_d = nc.dram_tensor("rw_gate", [384, T], BF16).ap()

    P = Prog(nc)
    hT = P.sbuf("hT", [128, 8, T], BF16)
    mT = P.sbuf("mT", [128, 8, T], BF16)
    stg = [P.sbuf(f"stg{i}", [128, 8, 256], F32) for i in range(2)]
    wbs = [P.sbuf(f"wb{i}", [128, 8, 256], BF16) for i in range(4)]
    ident = P.sbuf("ident", [128, 128], F32)
    ones = P.sbuf("ones", [128, 128], F32)
    bdm = P.sbuf("bdm", [128, 128], F32)
    mtr = P.sbuf("mtr", [128, 2, 256], F32)
    mlm = P.sbuf("mlm", [128, 2, 128], F32)
    rst = P.sbuf("rst", [128, 512], F32)
    lgam = P.sbuf("lgam", [128, 6], F32)
    zt = P.sbuf("zt", [128, 640], BF16)
    NPC = 256
    pp = P.sbuf("pp", [128, NPC], F32)
    ARW = 26300
    arena_t = P.sbuf("arena", [128, ARW], F32)
    A = Arena(arena_t[:], ARW)
    ps = [P.psum(f"ps{i}", [128, 512], F32) for i in range(7)]
    psb_t = P.psum("psb", [128, 1024], BF16)
    psb = psb_t[:]

    def MM(out, lhsT, rhs, start, stop, R, W):
        return P.op('pe', lambda e: e.matmul(out, lhsT, rhs, start=start, stop=stop, skip_group_check=True), R, W)

    def TRN(out, in_, idn, R, W):
        return P.op('pe', lambda e: e.transpose(out, in_, idn), R, W)

    def TT(eng, out, in0, in1, op, R, W):
        return P.op(eng, lambda e: e.tensor_tensor(out, in0, in1, op), R, W)

    def TS(eng, out, in0, s1, s2, op0, op1, R, W):
        if s2 is None:
            return P.op(eng, lambda e: e.tensor_scalar(out, in0, s1, None, op0), R, W)
        return P.op(eng, lambda e: e.tensor_scalar(out, in0, s1, s2, op0, op1), R, W)

    def STT(out, in0, scalar, in1, op0, op1, R, W):
        return P.op('dve', lambda e: e.scalar_tensor_tensor(out, in0, scalar, in1, op0, op1), R, W)

    def ACT(out, in_, func, R, W, bias=0.0, scale=1.0):
        return P.op('act', lambda e: e.activation(out, in_, func, bias=bias, scale=scale), R, W)

    def CP(eng, out, in_, R, W):
        if eng == 'act':
            return P.op('act', lambda e: e.copy(out, in_), R, W)
        return P.op(eng, lambda e: e.tensor_copy(out, in_), R, W)

    def RSUM(out, in_, R, W):
        return P.op('dve', lambda e: e.reduce_sum(out, in_, AX.X), R, W)

    def RCP(out, in_, R, W):
        return P.op('dve', lambda e: e.reciprocal(out, in_), R, W)

    def MSET(eng, ap, val, W):
        return P.op(eng, lambda e: e.memset(ap, val), (), W)

    def fm(ap2d):
        return ap2d.rearrange("(c p) n -> p c n", p=128)

    P.dma(ident[:], cst['c_ident'], writes=['ident'])
    P.dma(ones[:], cst['c_ones'], writes=['ones'])
    P.dma(bdm[:], cst['c_bd'], writes=['bdm'])
    P.dma(mtr[:], cst['c_mtr'], writes=['mtr'])
    P.dma(mlm[:], cst['c_ml'], writes=['mlm'])
    P.dma(rst[:], cst['c_rst'], writes=['rst'])
    P.dma(lgam[:], cst['c_lgam'], writes=['lgam'])
    MSET('pool', zt[:], 0.0, ['zt'])
    ppo = {}
    ppn = [0]

    def pcol(name, vec_ap, n):
        off = ppn[0]
        ppn[0] += n
        assert ppn[0] <= NPC
        P.dma(pp[:, off:off + n], vec_ap.rearrange("(c p) -> p c", p=128), writes=[f'pp_{name}'],
              allow_slow_non_contiguous=True)
        ppo[name] = off
        return off

    for l in range(depth):
        pcol(f"g1_{l}", prm['mix_norm_g'][l], 8)
        pcol(f"g2_{l}", prm['ffn_norm_g'][l], 8)
        pcol(f"rgw_{l}", prm['ret_gn_w'][l], 3)
        pcol(f"rgb_{l}", prm['ret_gn_b'][l], 3)
        for f in range(6):
            for s in range(2):
                pcol(f"mu_{l}_{f}_{s}", prm['rwkv_mu'][l, f, s], 3)
        for d in range(2):
            pcol(f"w0_{l}_{d}", prm['rwkv_w0'][l, d], 3)
            pcol(f"a0_{l}_{d}", prm['rwkv_a0'][l, d], 3)
        pcol(f"kk_{l}", prm['rwkv_k_k'][l], 3)
        pcol(f"ka_{l}", prm['rwkv_k_a'][l], 3)
        pcol(f"rk_{l}", prm['rwkv_r_k'][l].rearrange("h n -> (h n)"), 3)
        pcol(f"lnw_{l}", prm['rwkv_ln_w'][l], 3)
        pcol(f"lnb_{l}", prm['rwkv_ln_b'][l], 3)
        off = ppn[0]
        ppn[0] += 1
        sw = prm['diff_subln_w'][l].rearrange("(c p) -> p c", p=64)
        P.dma(pp[0:64, off:off + 1], sw, writes=[f'ppsa{l}'], allow_slow_non_contiguous=True)
        P.dma(pp[64:128, off:off + 1], sw, writes=[f'ppsb{l}'], allow_slow_non_contiguous=True)
        ppo[f"sub_{l}"] = off
    pcol("gf", prm['final_norm_g'], 8)
    P.barrier()

    def pc(name, c=0):
        o = ppo[name] + c
        return pp[:, o:o + 1]

    wstate = {'s': 0, 'b': 0}

    def load_w(pieces, nrc, swap_from=None):
        si = wstate['s']
        wstate['s'] = (si + 1) % 2
        bi = wstate['b']
        wstate['b'] = (bi + 1) % 4
        off = 0
        names = []
        for i, ap in enumerate(pieces):
            n = ap.shape[1]
            P.dma(stg[si][:, 0:nrc, off:off + n], fm(ap), writes=[f'stg{si}_{i}'])
            names.append(f'stg{si}_{i}')
            off += n
        CP('pool', wbs[bi][:, 0:nrc, 0:off], stg[si][:, 0:nrc, 0:off], names, [f'wb{bi}'])
        return wbs[bi], f'wb{bi}', stg[si], names

    def xview(ap2d, s):
        return fm(ap2d)[:, :, s * T:(s + 1) * T]

    def rmsnorm(src, gname, s, out_dram=None):
        A.reset()
        sq = [A.alloc([128, 512], BF16) for _ in range(2)]
        rs = A.alloc([128, 512], F32)
        onesb = A.alloc([128, 128], BF16)
        xbs = [A.alloc([128, 8, 512], F32) for _ in range(2)]
        obs = [A.alloc([128, 8, 512], F32) for _ in range(2)] if out_dram is not None else None
        CP('pool', onesb, ones[:], [], ['onesb'])
        for b in range(4):
            tk = slice(b * 512, (b + 1) * 512)
            xb = xbs[b % 2]
            xbn = f'xb{b % 2}'
            P.dma(xb, src[:, :, tk], reads=[f'xr_{s}_{dc}_{b}' for dc in range(8)], writes=[xbn])
            for c in range(8):
                ACT(sq[c % 2], xb[:, c, :], AF.Square, [xbn], [f'sq{c % 2}'])
                MM(ps[b % 2][:], onesb, sq[c % 2], c == 0, c == 7, [f'sq{c % 2}', 'onesb'], [f'ps{b % 2}'])
            ACT(rs, ps[b % 2][:], AF.Sqrt, [f'ps{b % 2}'], ['rs'], bias=EPS, scale=1.0 / D)
            RCP(rs, rs, ['rs'], ['rs'])
            for c in range(8):
                if out_dram is None:
                    STT(hT[:, c, tk], xb[:, c, :], pc(gname, c), rs, ALU.mult, ALU.mult, [xbn, 'rs', 'pp'], ['hT'])
                else:
                    STT(obs[b % 2][:, c, :], xb[:, c, :], pc(gname, c), rs, ALU.mult, ALU.mult, [xbn, 'rs', 'pp'], [f'ob{b % 2}'])
            if out_dram is not None:
                tok = P.dma(out_dram[:, :, tk], obs[b % 2], reads=[f'ob{b % 2}'])
                out_tokens.append(tok)
        P.barrier()

    out_tokens = []

    def proj_fm(psum_ap, psname, w, wname, c0, m, tk):
        for c in range(8):
            MM(psum_ap, w[:, c, c0:c0 + m], hT[:, c, tk], c == 0, c == 7, [wname, 'hT'], [psname])

    def proj_tm(psum_ap, psname, w, wname, c0, n, t):
        for c in range(8):
            MM(psum_ap, hT[:, c, t * 128:(t + 1) * 128], w[:, c, c0:c0 + n], c == 0, c == 7, [wname, 'hT'], [psname])

    def resid_add(psum_ap, psname, src, dst, s, dc, b, rt, rtname):
        tk = slice(b * 512, (b + 1) * 512)
        xn = f'xr_{s}_{dc}_{b}'
        P.dma(rt, src[:, dc, tk], reads=[xn], writes=[rtname])
        TT('dve', rt, psum_ap, rt, ALU.add, [psname, rtname], [rtname])
        P.dma(dst[:, dc, tk], rt, reads=[rtname], writes=[xn], queue='act')


    def xnames(s, b):
        return [f'xr_{s}_{dc}_{b}' for dc in range(8)]

    def out_proj(l, s, src, dst):
        A.reset()
        rts = [A.alloc([128, 512], F32) for _ in range(4)]
        k = 0
        for g in range(4):
            w, wn, _, _ = load_w([prm['w_out'][l][:, g * 256:(g + 1) * 256]], 8)
            for dci in range(2):
                dc = g * 2 + dci
                for b in range(4):
                    tk = slice(b * 512, (b + 1) * 512)
                    pb = ps[k % 4]
                    pn = f'ps{k % 4}'
                    for c in range(8):
                        MM(pb[:], w[:, c, dci * 128:(dci + 1) * 128], mT[:, c, tk], c == 0, c == 7, [wn, 'mT'], [pn])
                    resid_add(pb[:], pn, src, dst, s, dc, b, rts[k % 4], f'rt{k % 4}')
                    k += 1
        P.barrier()

    def ffn_phase(l, s, xr):
        TB = 2048
        for tb in range(T // TB):
            A.reset()
            hid = A.alloc([128, NFC, TB], BF16)
            sg = [A.alloc([128, 512], F32) for _ in range(2)]
            rts = [A.alloc([128, 512], F32) for _ in range(3)]
            k = 0
            for fc in range(NFC):
                if fc % 2 == 0:
                    wg_, wgn, _, _ = load_w([prm['w_gate'][l][:, fc * 128:(fc + 2) * 128]], 8)
                    wu_, wun, _, _ = load_w([prm['w_up'][l][:, fc * 128:(fc + 2) * 128]], 8)
                fo = (fc % 2) * 128
                for sb in range(TB // 512):
                    tk = slice(tb * TB + sb * 512, tb * TB + (sb + 1) * 512)
                    hk = slice(sb * 512, (sb + 1) * 512)
                    pg, pgn = ps[(2 * k) % 4], f'ps{(2 * k) % 4}'
                    pu, pun = ps[(2 * k + 1) % 4], f'ps{(2 * k + 1) % 4}'
                    proj_fm(pg[:], pgn, wg_, wgn, fo, 128, tk)
                    proj_fm(pu[:], pun, wu_, wun, fo, 128, tk)
                    ACT(sg[k % 2], pg[:], AF.Silu, [pgn], [f'sg{k % 2}'])
                    TT('dve', hid[:, fc, hk], pu[:], sg[k % 2], ALU.mult, [pun, f'sg{k % 2}'], ['hid'])
                    k += 1
            k = 0
            for g in range(4):
                ws = []
                for (r0, nrc) in ((0, 8), (8, 8), (16, 6)):
                    w, wn, _, _ = load_w([prm['w_down'][l][r0 * 128:(r0 + nrc) * 128, g * 256:(g + 1) * 256]], nrc)
                    ws.append((w, wn, r0, nrc))
                for dci in range(2):
                    dc = g * 2 + dci
                    for sb in range(TB // 512):
                        b = tb * (TB // 512) + sb
                        hk = slice(sb * 512, (sb + 1) * 512)
                        pb, pn = ps[4 + k % 3], f'ps{4 + k % 3}'
                        for (w, wn, r0, nrc) in ws:
                            for c in range(nrc):
                                fcx = r0 + c
                                MM(pb[:], w[:, c, dci * 128:(dci + 1) * 128], hid[:, fcx, hk], fcx == 0, fcx == NFC - 1,
                                   [wn, 'hid'], [pn])
                        resid_add(pb[:], pn, xr, xr, s, dc, b, rts[k % 3], f'rt{k % 3}')
                        k += 1
            P.barrier()

    def gn_alloc(nq):
        return {'sq': A.alloc([128, nq, 64], F32), 's1': A.alloc([128, nq], F32), 's2': A.alloc([128, nq], F32),
                'm2': A.alloc([128, nq], F32)}

    def gn_stats(y, yname, nq, eps, center, g):
        sq, s1, s2, m2 = g['sq'], g['s1'], g['s2'], g['m2']
        if center:
            RSUM(s1, y, [yname], ['gs1'])
            TS('dve', s1, s1, 1.0 / 64, None, ALU.mult, None, ['gs1'], ['gs1'])
            TT('dve', y, y, s1.unsqueeze(2).to_broadcast([128, nq, 64]), ALU.subtract, [yname, 'gs1'], [yname])
        TT('dve', sq, y, y, ALU.mult, [yname], ['gsq'])
        RSUM(s2, sq, ['gsq'], ['gs2'])
        ACT(m2, s2, AF.Sqrt, ['gs2'], ['gm2'], bias=eps, scale=1.0 / 64)
        RCP(m2, m2, ['gm2'], ['gm2'])
        TT('dve', y, y, m2.unsqueeze(2).to_broadcast([128, nq, 64]), ALU.mult, [yname, 'gm2'], [yname])

    def retention_phase(l, s):
        W = prm['w_in'][l]
        for hp in range(3):
            A.reset()
            rot = A.alloc([128, 2, T], F32)
            qT = A.alloc([128, T], BF16)
            kT = A.alloc([128, T], BF16)
            gT = A.alloc([128, T], BF16)
            vtm = A.alloc([128, 16, 128], BF16)
            t1 = A.alloc([128, 512], F32)
            t2 = A.alloc([128, 512], F32)
            wsw = A.alloc([128, 8, 256], BF16)
            P.dma(rot, cst['c_rot'], writes=['rot'])
            c0 = hp * 128
            w, wn, sg_, sgn = load_w([W[:, c0:c0 + 128], W[:, 384 + c0:384 + c0 + 128]], 8)
            for j in range(8):
                src0 = j * 32
                dst0 = (j ^ 1) * 32
                CP('pool', wsw[:, :, dst0:dst0 + 32], sg_[:, :, src0:src0 + 32], sgn, ['wsw'])
            for which, dstT in ((0, qT), (1, kT)):
                for b in range(4):
                    tk = slice(b * 512, (b + 1) * 512)
                    proj_fm(ps[0][:], 'ps0', w, wn, which * 128, 128, tk)
                    proj_fm(ps[1][:], 'ps1', wsw, 'wsw', which * 128, 128, tk)
                    TT('dve', t1, ps[0][:], rot[:, 0, tk], ALU.mult, ['ps0', 'rot'], ['t1'])
                    TT('dve', t2, ps[1][:], rot[:, 1, tk], ALU.mult, ['ps1', 'rot'], ['t2'])
                    TT('pool', dstT[:, tk], t1, t2, ALU.add, ['t1', 't2'], ['qkT'])
            w2, wn2, _, _ = load_w([W[:, 768 + c0:768 + c0 + 128], W[:, 1152 + c0:1152 + c0 + 128]], 8)
            for t in range(16):
                pb, pn = ps[t % 2], f'ps{t % 2}'
                proj_tm(pb[:, 0:128], pn, w2, wn2, 0, 128, t)
                CP('act', vtm[:, t, :], pb[:, 0:128], [pn], ['vtm'])
            for b in range(4):
                tk = slice(b * 512, (b + 1) * 512)
                pb, pn = ps[2 + b % 2], f'ps{2 + b % 2}'
                proj_fm(pb[:], pn, w2, wn2, 128, 128, tk)
                ACT(gT[:, tk], pb[:], AF.Silu, [pn], ['gT'])
            mk = A.alloc([128, 3968], F32)
            pTs = [A.alloc([128, 512], BF16) for _ in range(4)]
            ypair = A.alloc([128, 16, 128], F32)
            tmp = A.alloc([128, 512], F32)
            gsc = gn_alloc(4)
            for hh in range(2):
                h = hp * 2 + hh
                hb = hh * 64
                P.op('pool', lambda e: e.iota(mk, [[1, 3968]], base=-1920, channel_multiplier=-1,
                                              allow_small_or_imprecise_dtypes=True), (), ['mk'])
                ACT(mk, mk, AF.Abs, ['mk'], ['mk'])
                ACT(mk, mk, AF.Exp, ['mk', 'lgam'], ['mk'], bias=float(np.log(0.125)), scale=lgam[:, h:h + 1])
                for qb in range(4):
                    q0 = qb * 512
                    MM(ps[4][:], zt[:, 0:128], zt[:, 128:640], True, True, ['zt'], ['ps4'])
                    NK = 16

                    def score(kt):
                        sb_, sn = ps[kt % 4], f'ps{kt % 4}'
                        MM(sb_[:], kT[hb:hb + 64, kt * 128:(kt + 1) * 128], qT[hb:hb + 64, q0:q0 + 512], True, True,
                           ['qkT'], [sn])
                        off = q0 - kt * 128 + 1920
                        TT('dve', pTs[kt % 4], sb_[:], mk[:, off:off + 512], ALU.mult, [sn, 'mk'], [f'pT{kt % 4}'])

                    def pv(kt):
                        for qi in range(4):
                            MM(ps[4][:, qi * 64:(qi + 1) * 64], pTs[kt % 4][:, qi * 128:(qi + 1) * 128],
                               vtm[:, kt, hb:hb + 64], False, kt == NK - 1 and qi == 3, [f'pT{kt % 4}', 'vtm'], ['ps4'])
                    score(0)
                    score(1)
                    score(2)
                    for kt in range(NK):
                        if kt + 3 < NK:
                            score(kt + 3)
                        pv(kt)
                    yv = ypair[:, qb * 4:(qb + 1) * 4, hb:hb + 64]
                    CP('act', yv, ps[4][:, 0:256].rearrange("p (a b) -> p a b", b=64), ['ps4'], ['ypair'])
                    gn_stats(yv, 'ypair', 4, 1e-5, True, gsc)
            for qb in range(4):
                q0 = qb * 512
                for qi in range(4):
                    TRN(ps[5][:, qi * 128:(qi + 1) * 128], ypair[:, qb * 4 + qi, :], ident[:], ['ypair'], ['ps5'])
                TS('dve', tmp, ps[5][:], pp[:, ppo[f'rgw_{l}'] + hp:ppo[f'rgw_{l}'] + hp + 1],
                   pp[:, ppo[f'rgb_{l}'] + hp:ppo[f'rgb_{l}'] + hp + 1], ALU.mult, ALU.add, ['ps5'], ['tmp'])
                TT('pool', mT[:, hp, q0:q0 + 512], tmp, gT[:, q0:q0 + 512], ALU.mult, ['tmp', 'gT'], ['mT'])
            P.barrier()

    def diff_setup():
        A.reset()
        oh = A.alloc([32, 4096], F32)
        rb = A.alloc([32, 4], F32)
        bv = A.alloc([4, 4096], F32)
        P.dma(oh, cst['c_onehot'], writes=['oh'])
        P.dma(rb, prm['rel_bias'], writes=['rb'])
        for j in range(8):
            MM(ps[0][0:4, :], rb, oh[:, j * 512:(j + 1) * 512], True, True, ['oh', 'rb'], ['ps0'])
            CP('dve', bv[:, j * 512:(j + 1) * 512], ps[0][0:4, :], ['ps0'], ['bv'])
        P.dma(bvec_d, bv, reads=['bv'], writes=['bvec_d'])
        P.barrier()

    def diff_phase(l, s):
        W = prm['w_in'][l]
        lam_init = 0.8 - 0.6 * float(np.exp(-0.3 * l))
        A.reset()
        lvt = A.alloc([128, 4, 32], F32)
        lp = A.alloc([128, 2, 32], F32)
        ls = A.alloc([128, 2], F32)
        nlam = A.alloc([128, 1], F32)
        P.dma(lvt, bass.AP(prm['diff_lambda'].tensor, l * 128, [[0, 128], [32, 4], [1, 32]]), writes=['lvt'])
        TT('dve', lp[:, 0, :], lvt[:, 0, :], lvt[:, 1, :], ALU.mult, ['lvt'], ['lp'])
        TT('dve', lp[:, 1, :], lvt[:, 2, :], lvt[:, 3, :], ALU.mult, ['lvt'], ['lp'])
        RSUM(ls, lp, ['lp'], ['ls'])
        ACT(ls, ls, AF.Exp, ['ls'], ['ls'])
        TT('dve', nlam, ls[:, 1:2], ls[:, 0:1], ALU.subtract, ['ls'], ['nlam'])
        TS('dve', nlam, nlam, -lam_init, None, ALU.add, None, ['nlam'], ['nlam'])
        qc = [A.alloc([128, T], BF16) for _ in range(2)]
        kT = A.alloc([128, T], BF16)
        vaug = A.alloc([128, 16, 65], BF16)
        MSET('pool', qc[0], 0.0, ['qkT'])
        MSET('pool', qc[1], 0.0, ['qkT'])
        MSET('pool', kT[64:128, :], 0.0, ['qkT'])
        bm = A.alloc([128, 3968], F32)
        tmps = [A.alloc([128, 512], F32) for _ in range(4)]
        pTs = [A.alloc([128, 512], BF16) for _ in range(4)]
        opair = A.alloc([128, 16, 128], F32)
        o1 = A.alloc([128, 4, 64], F32)
        rr = A.alloc([128, 2, 4], F32)
        gsc = gn_alloc(4)
        MSET('pool', vaug[:, :, 64:65], 1.0, ['vaug'])
        for h in range(4):
            hb = (h % 2) * 64
            mc = 6 + h // 2
            w, wn, _, _ = load_w([W[:, 3072 + h * 64:3072 + (h + 1) * 64], W[:, 3328 + h * 64:3328 + (h + 1) * 64],
                                  W[:, 3584 + h * 64:3584 + (h + 1) * 64]], 8)
            P.dma(bm, bass.AP(bvec_d.tensor, h * 4096, [[1, 128], [1, 3968]]), reads=['bvec_d'], writes=['bm'])
            for which in (0, 1):
                for b in range(4):
                    tk = slice(b * 512, (b + 1) * 512)
                    pb, pn = ps[b % 2], f'ps{b % 2}'
                    proj_fm(pb[0:64, :], pn, w, wn, which * 64, 64, tk)
                    if which == 1:
                        CP('act', kT[0:64, tk], pb[0:64, :], [pn], ['qkT'])
                    else:
                        CP('act', qc[0][0:32, tk], pb[0:32, :], [pn], ['qkT'])
                        CP('act', qc[1][32:64, tk], pb[32:64, :], [pn], ['qkT'])
            for t in range(16):
                pb, pn = ps[t % 2], f'ps{t % 2}'
                proj_tm(pb[:, 0:64], pn, w, wn, 128, 64, t)
                CP('act', vaug[:, t, 0:64], pb[:, 0:64], [pn], ['vaug'])
            for qb in range(4):
                q0 = qb * 512
                MM(ps[4][:], zt[:, 0:128], zt[:, 128:640], True, True, ['zt'], ['ps4'])
                MM(ps[5][:], zt[:, 0:128], zt[:, 128:640], True, True, ['zt'], ['ps5'])
                NS = 32

                def score(i):
                    kt, c = divmod(i, 2)
                    sb_, sn = ps[i % 4], f'ps{i % 4}'
                    MM(sb_[:], kT[:, kt * 128:(kt + 1) * 128], qc[c][:, q0:q0 + 512], True, True, ['qkT'], [sn])
                    j0 = kt * 128 - q0 + 2047
                    bview = bm[:, j0 - 511:j0 + 1][:, ::-1]
                    STT(tmps[i % 4], sb_[:], float(32 ** -0.5), bview, ALU.mult, ALU.add, [sn, 'bm'], [f'tm{i % 4}'])
                    ACT(pTs[i % 4], tmps[i % 4], AF.Exp, [f'tm{i % 4}'], [f'pT{i % 4}'])

                def pv(i):
                    kt, c = divmod(i, 2)
                    acc, an = ps[4 + c], f'ps{4 + c}'
                    for qi in range(4):
                        MM(acc[:, qi * 65:(qi + 1) * 65], pTs[i % 4][:, qi * 128:(qi + 1) * 128], vaug[:, kt, :],
                           False, i >= NS - 2 and qi == 3, [f'pT{i % 4}', 'vaug'], [an])
                score(0)
                score(1)
                score(2)
                for i in range(NS):
                    if i + 3 < NS:
                        score(i + 3)
                    pv(i)
                a0 = ps[4][:, 0:260].rearrange("p (a b) -> p a b", b=65)
                a1 = ps[5][:, 0:260].rearrange("p (a b) -> p a b", b=65)
                RCP(rr[:, 0, :], a0[:, :, 64], ['ps4'], ['rr'])
                RCP(rr[:, 1, :], a1[:, :, 64], ['ps5'], ['rr'])
                TS('dve', rr[:, 1, :], rr[:, 1, :], nlam, None, ALU.mult, None, ['rr', 'nlam'], ['rr'])
                o0 = opair[:, qb * 4:(qb + 1) * 4, hb:hb + 64]
                TT('dve', o0, a0[:, :, 0:64], rr[:, 0, :].unsqueeze(2).to_broadcast([128, 4, 64]), ALU.mult, ['ps4', 'rr'], ['opair'])
                TT('dve', o1, a1[:, :, 0:64], rr[:, 1, :].unsqueeze(2).to_broadcast([128, 4, 64]), ALU.mult, ['ps5', 'rr'], ['o1'])
                TT('dve', o0, o0, o1, ALU.add, ['opair', 'o1'], ['opair'])
                gn_stats(o0, 'opair', 4, EPS, False, gsc)
            if h % 2 == 1:
                so = ppo[f'sub_{l}']
                for qb in range(4):
                    q0 = qb * 512
                    for qi in range(4):
                        TRN(ps[6][:, qi * 128:(qi + 1) * 128], opair[:, qb * 4 + qi, :], ident[:], ['opair'], ['ps6'])
                    TS('dve', mT[:, mc, q0:q0 + 512], ps[6][:], pp[:, so:so + 1], 1.0 - lam_init,
                       ALU.mult, ALU.mult, ['ps6'], ['mT'])
        P.barrier()

    def rwkv_phase(l, s):
        W = prm['w_in'][l]
        base = 1536
        A.reset()
        Fz = A.alloc([128, 3, T + 2], F32)
        xs = A.alloc([128, 3, T], BF16)
        hw = [A.alloc([128, T], BF16) for _ in range(2)]
        ha = [A.alloc([128, T], BF16) for _ in range(2)]
        hg = A.alloc([128, T], BF16)
        w1b = A.alloc([128, 2, 3, 64], BF16)
        a1b = A.alloc([128, 2, 3, 64], BF16)
        g1b = A.alloc([128, 3, 128], BF16)
        w2b = A.alloc([64, 2, 384], BF16)
        a2b = A.alloc([64, 2, 384], BF16)
        g2b = A.alloc([128, 384], BF16)
        lst = A.alloc([128, 1536], F32)
        cm = A.alloc([128, 6, 3], F32)
        ka1 = A.alloc([128, 3], F32)
        ka2 = A.alloc([128, 3], F32)
        t1 = A.alloc([128, 512], F32)
        t2 = A.alloc([128, 512], F32)
        tsh = [A.alloc([128, 512], F32) for _ in range(2)]
        lwt = [A.alloc([128, 512], F32) for _ in range(2)]
        ast = [A.alloc([128, 512], BF16) for _ in range(2)]
        gtt = [A.alloc([128, 512], BF16) for _ in range(2)]
        for d in range(2):
            P.dma(lst[:, 0:192].rearrange("p (c r) -> p c r", r=64), fm(prm['rwkv_w1'][l, d]), writes=['lst'])
            CP('pool', w1b[:, d], lst[:, 0:192].rearrange("p (c r) -> p c r", r=64), ['lst'], ['lw8'])
            P.dma(lst[:, 192:384].rearrange("p (c r) -> p c r", r=64), fm(prm['rwkv_a1'][l, d]), writes=['lst2'])
            CP('pool', a1b[:, d], lst[:, 192:384].rearrange("p (c r) -> p c r", r=64), ['lst2'], ['lw8'])
            P.dma(lst[0:64, 384:768], prm['rwkv_w2'][l, d], writes=['lst3'])
            CP('pool', w2b[:, d, :], lst[0:64, 384:768], ['lst3'], ['lw8'])
            P.dma(lst[0:64, 768:1152], prm['rwkv_a2'][l, d], writes=['lst4'])
            CP('pool', a2b[:, d, :], lst[0:64, 768:1152], ['lst4'], ['lw8'])
        P.dma(lst[:, 0:384].rearrange("p (c r) -> p c r", r=128), fm(prm['rwkv_g1'][l]), writes=['lst', 'lst2'])
        CP('pool', g1b, lst[:, 0:384].rearrange("p (c r) -> p c r", r=128), ['lst', 'lst2'], ['lw8'])
        P.dma(lst[:, 1152:1536], prm['rwkv_g2'][l], writes=['lst5'])
        CP('pool', g2b, lst[:, 1152:1536], ['lst5'], ['lw8'])
        for f in range(6):
            o0 = ppo[f'mu_{l}_{f}_0']
            o1 = ppo[f'mu_{l}_{f}_1']
            TT('dve', cm[:, f, :], pp[:, o0:o0 + 3], pp[:, o1:o1 + 3], ALU.add, [], ['cm'])
        TS('dve', cm, cm, -1.0, 1.0, ALU.mult, ALU.add, ['cm'], ['cm'])
        oka = ppo[f'ka_{l}']
        TS('dve', ka1, pp[:, oka:oka + 3], -1.0, 1.0, ALU.mult, ALU.add, [], ['ka1'])
        TS('dve', ka2, pp[:, oka:oka + 3], -2.0, 2.0, ALU.mult, ALU.add, [], ['ka2'])

        def shift_mix(Fv, f, j, dst, dname, fname):
            o0 = ppo[f'mu_{l}_{f}_0'] + j
            o1 = ppo[f'mu_{l}_{f}_1'] + j
            for b in range(4):
                b0 = b * 512
                TS('dve', t1, Fv[:, 1 + b0:1 + b0 + 512], cm[:, f, j:j + 1], None, ALU.mult, None, [fname, 'cm'], ['t1'])
                STT(t2, Fv[:, b0:b0 + 512], pp[:, o0:o0 + 1], t1, ALU.mult, ALU.add, [fname, 't1'], ['t2'])
                STT(dst[:, b0:b0 + 512], Fv[:, 2 + b0:2 + b0 + 512], pp[:, o1:o1 + 1], t2, ALU.mult, ALU.add,
                    [fname, 't2'], [dname])

        def proj_to_F(Fv, fname, w, wn, c0):
            for b in range(4):
                tk = slice(b * 512, (b + 1) * 512)
                pb, pn = ps[b % 2], f'ps{b % 2}'
                proj_fm(pb[:], pn, w, wn, c0, 128, tk)
                CP('act', Fv[:, 1 + b * 512:1 + (b + 1) * 512], pb[:], [pn], [fname])

        MSET('pool', Fz[:, :, 0:1], 0.0, ['Fz'])
        MSET('pool', Fz[:, :, T + 1:T + 2], 0.0, ['Fz'])
        for j in range(3):
            w, wn, _, _ = load_w([W[:, base + 1152 + j * 128:base + 1152 + (j + 1) * 128]], 8)
            proj_to_F(Fz[:, j, :], 'Fz', w, wn, 0)
        for (f, kind) in ((3, 'w'), (4, 'a'), (5, 'g')):
            for j in range(3):
                shift_mix(Fz[:, j, :], f, j, xs[:, j, :], 'xs', 'Fz')
            for b in range(4):
                tk = slice(b * 512, (b + 1) * 512)
                if kind == 'g':
                    for j in range(3):
                        MM(ps[2][:], g1b[:, j, :], xs[:, j, tk], j == 0, j == 2, ['xs', 'lw8'], ['ps2'])
                    ACT(hg[:, tk], ps[2][:], AF.Sigmoid, ['ps2'], ['hg'])
                else:
                    for d in range(2):
                        pb, pn = ps[2 + d], f'ps{2 + d}'
                        wl = w1b if kind == 'w' else a1b
                        for j in range(3):
                            MM(pb[0:64, :], wl[:, d, j, :], xs[:, j, tk], j == 0, j == 2, ['xs', 'lw8'], [pn])
                        if kind == 'w':
                            ACT(hw[d][0:64, tk], pb[0:64, :], AF.Tanh, [pn], ['hw'])
                        else:
                            CP('act', ha[d][0:64, tk], pb[0:64, :], [pn], ['ha'])
        k = 0
        for j in range(3):
            cj = slice(j * 128, (j + 1) * 128)
            for b in range(4):
                tk = slice(b * 512, (b + 1) * 512)
                for d in range(2):
                    pb, pn = ps[k % 4], f'ps{k % 4}'
                    MM(pb[:], w2b[:, d, cj], hw[d][0:64, tk], True, True, ['hw', 'lw8'], [pn])
                    ACT(lwt[k % 2], pb[:], AF.Sigmoid, [pn], [f'lwt{k % 2}'], bias=pc(f'w0_{l}_{d}', j))
                    TS('dve', lwt[k % 2], lwt[k % 2], -LOG_E05, None, ALU.mult, None, [f'lwt{k % 2}'], [f'lwt{k % 2}'])
                    P.dma(lw_d[d, cj, tk], lwt[k % 2], reads=[f'lwt{k % 2}'], writes=[f'lw_{d}_{j}_{b}'])
                    k += 1
                    pb, pn = ps[k % 4], f'ps{k % 4}'
                    MM(pb[:], a2b[:, d, cj], ha[d][0:64, tk], True, True, ['ha', 'lw8'], [pn])
                    ACT(ast[k % 2], pb[:], AF.Sigmoid, [pn], [f'ast{k % 2}'], bias=pc(f'a0_{l}_{d}', j))
                    P.dma(as_d[d, cj, tk], ast[k % 2], reads=[f'ast{k % 2}'], writes=[f'as_{d}_{j}_{b}'])
                    k += 1
                pb, pn = ps[k % 4], f'ps{k % 4}'
                MM(pb[:], g2b[:, cj], hg[:, tk], True, True, ['hg', 'lw8'], [pn])
                CP('act', gtt[k % 2], pb[:], [pn], [f'gtt{k % 2}'])
                P.dma(gate_d[cj, tk], gtt[k % 2], reads=[f'gtt{k % 2}'], writes=[f'gate_{j}_{b}'])
                k += 1
        P.barrier()

        for j in range(3):
            cj = slice(j * 128, (j + 1) * 128)
            A.reset()
            xr = A.alloc([128, T], BF16)
            xk = A.alloc([128, T], BF16)
            xv = A.alloc([128, T], BF16)
            Vtm = A.alloc([128, 16, 128], BF16)
            ysum = A.alloc([128, 16, 128], F32)
            kkn = A.alloc([128, T], BF16)
            bonus = A.alloc([128, T], BF16)
            ka1 = A.alloc([128, 3], F32)
            identb = A.alloc([128, 128], BF16)
            gsc = gn_alloc(8)
            gtile = A.alloc([128, 512], BF16)
            te1 = A.alloc([128, 512], F32)
            mark = A.off
            Fv = A.alloc([128, T + 2], F32)
            cm = A.alloc([128, 6, 3], F32)
            ka2 = A.alloc([128, 3], F32)
            t1 = A.alloc([128, 512], F32)
            t2 = A.alloc([128, 512], F32)
            tsh = [A.alloc([128, 512], F32) for _ in range(2)]
            as0 = A.alloc([128, 512], BF16)
            as1 = A.alloc([128, 512], BF16)
            CP('pool', identb, ident[:], [], ['identb'])
            for f in range(6):
                o0 = ppo[f'mu_{l}_{f}_0']
                o1 = ppo[f'mu_{l}_{f}_1']
                TT('dve', cm[:, f, :], pp[:, o0:o0 + 3], pp[:, o1:o1 + 3], ALU.add, [], ['cm'])
            TS('dve', cm, cm, -1.0, 1.0, ALU.mult, ALU.add, ['cm'], ['cm'])
            TS('dve', ka1, pp[:, oka:oka + 3], -1.0, 1.0, ALU.mult, ALU.add, [], ['ka1'])
            TS('dve', ka2, pp[:, oka:oka + 3], -2.0, 2.0, ALU.mult, ALU.add, [], ['ka2'])
            MSET('pool', Fv[:, 0:1], 0.0, ['Fv'])
            MSET('pool', Fv[:, T + 1:T + 2], 0.0, ['Fv'])
            w, wn, _, _ = load_w([W[:, base + j * 128:base + (j + 1) * 128],
                                  W[:, base + 384 + j * 128:base + 384 + (j + 1) * 128]], 8)
            wv_, wvn, _, _ = load_w([W[:, base + 768 + j * 128:base + 768 + (j + 1) * 128]], 8)
            proj_to_F(Fv, 'Fv', w, wn, 0)
            shift_mix(Fv, 0, j, xr, 'xr', 'Fv')
            proj_to_F(Fv, 'Fv', w, wn, 128)
            shift_mix(Fv, 1, j, xk, 'xk', 'Fv')
            proj_to_F(Fv, 'Fv', wv_, wvn, 0)
            shift_mix(Fv, 2, j, xv, 'xv', 'Fv')
            okk = ppo[f'kk_{l}'] + j
            ork = ppo[f'rk_{l}'] + j
            for b in range(4):
                tk = slice(b * 512, (b + 1) * 512)
                TS('dve', t1, xk[:, tk], pp[:, okk:okk + 1], None, ALU.mult, None, ['xk'], ['t1'])
                ACT(t2, t1, AF.Square, ['t1'], ['t2'])
                MM(ps[0][:], bdm[:], t2, True, True, ['t2'], ['ps0'])
                ACT(t2, ps[0][:], AF.Sqrt, ['ps0'], ['t2'])
                TS('dve', t2, t2, 1e-12, None, ALU.max, None, ['t2'], ['t2'])
                RCP(t2, t2, ['t2'], ['t2'])
                TT('dve', kkn[:, tk], t1, t2, ALU.mult, ['t1', 't2'], ['kkn'])
                P.dma(as0, as_d[0, cj, tk], reads=[f'as_0_{j}_{b}'], writes=['as0'])
                P.dma(as1, as_d[1, cj, tk], reads=[f'as_1_{j}_{b}'], writes=['as1'])
                TT('pool', t1, as0, as1, ALU.add, ['as0', 'as1'], ['t1'])
                TS('dve', t1, t1, pp[:, oka + j:oka + j + 1], ka2[:, j:j + 1], ALU.mult, ALU.add, ['t1', 'ka2'], ['t1'])
                TT('dve', t1, t1, xk[:, tk], ALU.mult, ['t1', 'xk'], ['t1'])
                TT('dve', t1, t1, xr[:, tk], ALU.mult, ['t1', 'xr'], ['t1'])
                TS('dve', t2, t1, pp[:, ork:ork + 1], None, ALU.mult, None, ['t1'], ['t2'])
                MM(ps[1][:], bdm[:], t2, True, True, ['t2'], ['ps1'])
                TT('dve', bonus[:, tk], ps[1][:], xv[:, tk], ALU.mult, ['ps1', 'xv'], ['bonus'])
            for t in range(16):
                TRN(psb[:, (t % 8) * 128:(t % 8 + 1) * 128], xv[:, t * 128:(t + 1) * 128], identb, ['xv', 'identb'], ['ps7'])
                if t % 8 == 7:
                    g8 = t // 8
                    CP('act', Vtm[:, g8 * 8:(g8 + 1) * 8, :], psb.rearrange("p (a b) -> p a b", b=128), ['ps7'], ['Vtm'])
            MSET('pool', ysum, 0.0, ['ysum'])
            P.barrier()
            A.off = mark

            def unit_alloc():
                u = {}
                for nm in ('lwt', 'cl', 'gi', 'gv', 't1', 't2'):
                    u[nm] = A.alloc([128, 512], F32)
                for nm in ('as0', 'bb', 'Bt', 'Bbar', 'kd', 'Kt', 'Kbar'):
                    u[nm] = A.alloc([128, 512], BF16)
                u['AR'] = A.alloc([128, 2, 512], BF16)
                u['BKz'] = A.alloc([128, 16, 128], BF16).rearrange("p (a b c) d -> p a b c d", b=2, c=2)
                u['G'] = [A.alloc([128, 512], BF16) for _ in range(2)]
                u['Nm'] = A.alloc([128, 256], BF16)
                u['MM2'] = [A.alloc([128, 512], BF16) for _ in range(2)]
                u['Mb'] = [m_[:, 0:256] for m_ in u['MM2']]
                u['MTb'] = [m_[:, 256:512] for m_ in u['MM2']]
                u['PTb'] = [A.alloc([128, 256], BF16) for _ in range(2)]
                u['RHSb'] = A.alloc([128, 128], BF16)
                u['Ub'] = A.alloc([128, 128], BF16)
                u['Hf'] = A.alloc([128, 64], F32)
                u['Hs'] = A.alloc([128, 64], BF16)
                return u

            def unit(d, u):
                n = lambda s_: f'{s_}_{d}'
                lwt_, cl, gi, gv, t1, t2 = u['lwt'], u['cl'], u['gi'], u['gv'], u['t1'], u['t2']
                as0, bb, Bt, Bbar, kd, Kt, Kbar = u['as0'], u['bb'], u['Bt'], u['Bbar'], u['kd'], u['Kt'], u['Kbar']
                AR, BKz, G, Nm, Mb, MTb, PTb = u['AR'], u['BKz'], u['G'], u['Nm'], u['Mb'], u['MTb'], u['PTb']
                RHSb, Ub, Hf, Hs = u['RHSb'], u['Ub'], u['Hf'], u['Hs']
                Xb, Yb, Zb = ps[3 * d], ps[3 * d + 1], ps[3 * d + 2]
                pA = [Xb[:, 0:256], Yb[:, 0:256]]
                pB = [Xb[:, 256:512], Yb[:, 256:512]]
                pAn = [f'ps{3 * d}', f'ps{3 * d + 1}']
                pBn = pAn
                pD, pE = Zb[:, 0:256], Zb[:, 256:512]
                pDn, pEn = f'ps{3 * d + 2}', f'ps{3 * d + 2}'
                pFh = [Xb[:, 384:512], Yb[:, 384:512]]
                pU, pS = Xb[:, 256:384], Yb[:, 256:320]
                pUn, pSn = pAn[0], pAn[1]
                pT = psb
                pTn = 'ps7'
                MSET('pool', Hf, 0.0, [n('Hf')])
                MSET('pool', Hs, 0.0, [n('Hs')])
                MSET('pool', BKz, 0.0, [n('BKz')])
                for bi in range(4):
                    b = bi if d == 0 else 3 - bi
                    tk = slice(b * 512, (b + 1) * 512)
                    P.dma(lwt_, lw_d[d, cj, tk], reads=[f'lw_{d}_{j}_{b}'], writes=[n('lwt')])
                    P.dma(as0, as_d[d, cj, tk], reads=[f'as_{d}_{j}_{b}'], writes=[n('as0')])
                    yield
                    if d == 0:
                        P.op('dve', lambda e, cl=cl, lwt_=lwt_: e.tensor_tensor_scan(cl, rst[:], lwt_, 0.0, ALU.mult, ALU.add),
                             [n('lwt')], [n('cl')])
                    else:
                        P.op('dve', lambda e, cl=cl, lwt_=lwt_: e.tensor_tensor_scan(cl[:, ::-1], rst[:], lwt_[:, ::-1], 0.0,
                                                                                    ALU.mult, ALU.add), [n('lwt')], [n('cl')])
                    cl3 = cl.rearrange("p (a b) -> p a b", b=128)
                    tot = cl3[:, :, 127:128] if d == 0 else cl3[:, :, 0:1]
                    ACT(gi, cl, AF.Exp, [n('cl')], [n('gi')])
                    ACT(gv, cl, AF.Exp, [n('cl')], [n('gv')], scale=-1.0)
                    TT('pool', t1, cl, lwt_, ALU.subtract, [n('cl'), n('lwt')], [n('t1')])
                    ACT(t1, t1, AF.Exp, [n('t1')], [n('t1')])
                    TT('dve', t2.rearrange("p (a b) -> p a b", b=128), cl3, tot.to_broadcast([128, 4, 128]), ALU.subtract,
                       [n('cl')], [n('t2')])
                    ACT(t2, t2, AF.Exp, [n('t2')], [n('t2')], scale=-1.0)
                    yield
                    STT(AR[:, 0, :], kkn[:, tk], -1.0, t1, ALU.mult, ALU.mult, ['kkn', n('t1')], [n('AR')])
                    TT('pool', AR[:, 1, :], xr[:, tk], gi, ALU.mult, ['xr', n('gi')], [n('AR')])
                    TT('pool', bb, kkn[:, tk], as0, ALU.mult, ['kkn', n('as0')], [n('bb')])
                    TT('dve', Bt, bb, gv, ALU.mult, [n('bb'), n('gv')], [n('Bt')])
                    TT('pool', Bbar, bb, t2, ALU.mult, [n('bb'), n('t2')], [n('Bbar')])
                    TS('dve', kd, as0, pp[:, oka + j:oka + j + 1], ka1[:, j:j + 1], ALU.mult, ALU.add, [n('as0'), 'ka1'], [n('kd')])
                    TT('pool', kd, kd, xk[:, tk], ALU.mult, [n('kd'), 'xk'], [n('kd')])
                    TT('dve', Kt, kd, gv, ALU.mult, [n('kd'), n('gv')], [n('Kt')])
                    TT('pool', Kbar, kd, t2, ALU.mult, [n('kd'), n('t2')], [n('Kbar')])
                    yield
                    for w_, src_, sn_ in ((0, Bbar, n('Bbar')), (1, Kbar, n('Kbar'))):
                        for ci in range(4):
                            TRN(pT[:, (w_ * 4 + ci) * 128:(w_ * 4 + ci + 1) * 128], src_[:, ci * 128:(ci + 1) * 128], identb,
                                [sn_, 'identb'], [pTn])
                    for w_ in range(2):
                        for ci in range(4):
                            for hh in range(2):
                                c0_ = (w_ * 4 + ci) * 128 + hh * 64
                                CP('act', BKz[:, ci, w_, hh, hh * 64:(hh + 1) * 64], pT[:, c0_:c0_ + 64],
                                   [pTn], [n('BKz')])
                    yield
                    for cii in range(4):
                        ci = cii if d == 0 else 3 - cii
                        cc = slice(ci * 128, (ci + 1) * 128)
                        cg = b * 4 + ci
                        for hh in range(2):
                            hb = hh * 64
                            MM(pA[hh], Bt[hb:hb + 64, cc], AR[hb:hb + 64, :, cc], True, True, [n('Bt'), n('AR')], [pAn[hh]])
                            MM(pB[hh], Kt[hb:hb + 64, cc], AR[hb:hb + 64, :, cc], True, True, [n('Kt'), n('AR')], [pBn[hh]])
                        yield
                        for hh in range(2):
                            TT('dve', G[hh][:, 0:256], pA[hh], mtr[:, d, 0:256], ALU.mult, [pAn[hh]], [n(f'G{hh}')])
                            TT('dve', G[hh][:, 256:512], pB[hh], mtr[:, d, 0:256], ALU.mult, [pBn[hh]], [n(f'G{hh}')])
                        for hh in range(2):
                            hb = hh * 64
                            MM(pA[hh][:, 0:128], AR[hb:hb + 64, 0, cc], Bt[hb:hb + 64, cc], True, True,
                               [n('Bt'), n('AR')], [pAn[hh]])
                        yield
                        for hh in range(2):
                            TT('dve', Nm[:, hh * 128:(hh + 1) * 128], pA[hh][:, 0:128], mlm[:, d, 0:128], ALU.mult,
                               [pAn[hh]], [n('Nm')])
                        NTh = [G[hh][:, 0:128] for hh in range(2)]
                        ARBh = [G[hh][:, 128:256] for hh in range(2)]
                        AKh = [G[hh][:, 256:384] for hh in range(2)]
                        ARKh = [G[hh][:, 384:512] for hh in range(2)]
                        for hh in range(2):
                            TT('dve', PTb[0][:, hh * 128:(hh + 1) * 128], NTh[hh], ident[:], ALU.add, [n(f'G{hh}')], [n('PTb0')])
                        curM = [Nm[:, hh * 128:(hh + 1) * 128] for hh in range(2)]
                        curMn = [n('Nm')] * 2
                        curMT = NTh
                        curMTn = [n('G0'), n('G1')]
                        pcur = 0
                        yield
                        for lev in range(1, 7):
                            sl_ = lev % 2
                            for hh in range(2):
                                MM(pD[:, hh * 128:(hh + 1) * 128], curMT[hh], curM[hh], True, True,
                                   [curMn[hh], curMTn[hh]], [pDn])
                            if lev < 6:
                                for hh in range(2):
                                    MM(pE[:, hh * 128:(hh + 1) * 128], curM[hh], curMT[hh], True, True,
                                       [curMn[hh], curMTn[hh]], [pEn])
                            yield
                            if lev < 6:
                                CP('act', u['MM2'][sl_], Zb[:], [pDn], [n(f'Mb{sl_}'), n(f'MTb{sl_}')])
                            else:
                                CP('act', Mb[sl_], pD, [pDn], [n(f'Mb{sl_}')])
                            curM = [Mb[sl_][:, hh * 128:(hh + 1) * 128] for hh in range(2)]
                            curMn = [n(f'Mb{sl_}')] * 2
                            curMT = [MTb[sl_][:, hh * 128:(hh + 1) * 128] for hh in range(2)]
                            curMTn = [n(f'MTb{sl_}')] * 2
                            for hh in range(2):
                                MM(pFh[hh], curM[hh], PTb[pcur][:, hh * 128:(hh + 1) * 128], True, True,
                                   [curMn[hh], n(f'PTb{pcur}')], [pAn[hh]])
                            yield
                            for hh in range(2):
                                TT('dve', PTb[1 - pcur][:, hh * 128:(hh + 1) * 128], pFh[hh], PTb[pcur][:, hh * 128:(hh + 1) * 128],
                                   ALU.add, [pAn[hh], n(f'PTb{pcur}')], [n(f'PTb{1 - pcur}')])
                            pcur = 1 - pcur
                        PTc = PTb[pcur]
                        PTn = n(f'PTb{pcur}')
                        for hh in range(2):
                            hb = hh * 64
                            vs = slice(hh * 64, (hh + 1) * 64)
                            o = pA[hh][:, 128:192]
                            MM(o, AR[hb:hb + 64, 0, cc], Hs[hb:hb + 64, :], True, False, [n('AR'), n('Hs')], [pAn[hh]])
                            MM(o, AKh[hh], Vtm[:, cg, vs], False, True, [n(f'G{hh}'), 'Vtm'], [pAn[hh]])
                        yield
                        CP('act', RHSb[:, 0:64], pA[0][:, 128:192], [pAn[0]], [n('RHSb')])
                        CP('dve', RHSb[:, 64:128], pA[1][:, 128:192], [pAn[1]], [n('RHSb')])
                        for hh in range(2):
                            vs = slice(hh * 64, (hh + 1) * 64)
                            MM(pU[:, hh * 64:(hh + 1) * 64], PTc[:, hh * 128:(hh + 1) * 128], RHSb[:, vs], True, True,
                               [PTn, n('RHSb')], [pUn])
                        yield
                        CP('dve', Ub, pU, [pUn], [n('Ub')])
                        for hh in range(2):
                            hb = hh * 64
                            vs = slice(hh * 64, (hh + 1) * 64)
                            o = pA[hh][:, 192:256]
                            MM(o, AR[hb:hb + 64, 1, cc], Hs[hb:hb + 64, :], True, False, [n('AR'), n('Hs')], [pAn[hh]])
                            MM(o, ARBh[hh], Ub[:, vs], False, False, [n(f'G{hh}'), n('Ub')], [pAn[hh]])
                            MM(o, ARKh[hh], Vtm[:, cg, vs], False, True, [n(f'G{hh}'), 'Vtm'], [pAn[hh]])
                        o = pS
                        for hh in range(2):
                            vs = slice(hh * 64, (hh + 1) * 64)
                            MM(o, BKz[:, ci, 0, hh, :], Ub[:, vs], hh == 0, False, [n('BKz'), n('Ub')], [pSn])
                            MM(o, BKz[:, ci, 1, hh, :], Vtm[:, cg, vs], False, hh == 1, [n('BKz'), 'Vtm'], [pSn])
                        yield
                        for hh in range(2):
                            vs = slice(hh * 64, (hh + 1) * 64)
                            TT('dve', ysum[:, cg, vs], pA[hh][:, 192:256], ysum[:, cg, vs], ALU.add,
                               [pAn[hh], f'ysum{cg}'], [f'ysum{cg}'])
                        gcol = gi[:, ci * 128 + 127:ci * 128 + 128] if d == 0 else gi[:, ci * 128:ci * 128 + 1]
                        STT(Hf, Hf, gcol, pS, ALU.mult, ALU.add, [n('Hf'), n('gi'), pSn], [n('Hf')])
                        CP('act', Hs, Hf, [n('Hf')], [n('Hs')])
                        yield

            units = [unit(d, unit_alloc()) for d in range(2)]
            active = list(units)
            first = True
            import os
            if os.environ.get('RW_SERIAL'):
                for g_ in units:
                    for _ in g_:
                        pass
                active = []
            NDUM = int(os.environ.get('RW_DUMMY', '0'))
            while active:
                for g_ in list(active):
                    try:
                        next(g_)
                    except StopIteration:
                        active.remove(g_)
                    for _ in range(NDUM):
                        MM(ps[6][:], zt[:, 0:128], zt[:, 128:640], True, True, ['zt'], ['ps6'])
            P.barrier()
            for g4 in range(4):
                tk = slice(g4 * 512, (g4 + 1) * 512)
                y8 = ysum[:, g4 * 4:(g4 + 1) * 4, :].rearrange("p a (h n) -> p (a h) n", n=64)
                gn_stats(y8, 'ysum', 8, 64e-5, True, gsc)
                for qi in range(4):
                    TRN(ps[3][:, qi * 128:(qi + 1) * 128], ysum[:, g4 * 4 + qi, :], ident[:], ['ysum'], ['ps3'])
                olw = ppo[f'lnw_{l}'] + j
                olb = ppo[f'lnb_{l}'] + j
                TS('dve', te1, ps[3][:], pp[:, olw:olw + 1], pp[:, olb:olb + 1], ALU.mult, ALU.add, ['ps3'], ['te1'])
                TT('pool', te1, te1, bonus[:, tk], ALU.add, ['te1', 'bonus'], ['te1'])
                P.dma(gtile, gate_d[cj, tk], reads=[f'gate_{j}_{g4}'], writes=['gtile'])
                TT('dve', mT[:, 3 + j, tk], te1, gtile, ALU.mult, ['te1', 'gtile'], ['mT'])
            P.barrier()

    diff_setup()
    for s in range(nseq):
        for l in range(depth):
            src = xview(xT_in, s) if l == 0 else xview(xres, s)
            dst = xview(xres, s)
            rmsnorm(src, f'g1_{l}', s)
            if 'ret' in mixers:
                retention_phase(l, s)
            else:
                MSET('pool', mT[:, 0:3, :], 0.0, ['mT'])
            if 'rwkv' in mixers:
                rwkv_phase(l, s)
            else:
                MSET('pool', mT[:, 3:6, :], 0.0, ['mT'])
            if 'diff' in mixers:
                diff_phase(l, s)
            else:
                MSET('pool', mT[:, 6:8, :], 0.0, ['mT'])
            P.barrier()
            out_proj(l, s, src, dst)
            if ffn:
                rmsnorm(dst, f'g2_{l}', s)
                ffn_phase(l, s, dst)
        rmsnorm(xview(xres, s), 'gf', s, out_dram=xview(outT, s))
    P.wait_all_outputs(out_tokens)
    P.finish()
    return nc, P


_CACHE = {}


def kernel(**inputs):
    x = np.asarray(inputs['x'], dtype=np.float32)
    consts = host_consts()
    if 'nc' not in _CACHE:
        _CACHE['nc'] = build()[0]
    nc = _CACHE['nc']
    params = {k: np.ascontiguousarray(np.asarray(inputs[k], dtype=np.float32)) for k in PARAM_SHAPES}
    in_maps = []
    for c in range(8):
        xs = np.ascontiguousarray(x[2 * c:2 * c + 2].reshape(2 * T, D).T)
        m = {'xT': xs}
        m.update(params)
        m.update(consts)
        in_maps.append(m)
    res = run_bass_kernel_spmd(nc, in_maps, core_ids=list(range(8)))
    out = np.empty((16, T, D), np.float32)
    for c in range(8):
        o = np.asarray(res.results[c]['outT'])
        out[2 * c:2 * c + 2] = o.T.reshape(2, T, D)
    return out
```

```python
import contextlib
import numpy as np
import concourse.bass as bass
import concourse.mybir as mybir

F32 = mybir.dt.float32
BF16 = mybir.dt.bfloat16
I32 = mybir.dt.int32
ALU = mybir.AluOpType
AF = mybir.ActivationFunctionType
AX = mybir.AxisListType

EPOCH = 16000
N_DMA_SEMS = 40


class Prog:
    def __init__(self, nc, same_engine_sync=True):
        self.nc = nc
        self.stack = contextlib.ExitStack()
        self.engs = ['pe', 'act', 'dve', 'pool', 'sp']
        self.ops = {e: [] for e in self.engs}
        self.count = {e: 0 for e in self.engs}
        self.csems = {e: [] for e in self.engs}
        self.seen = {e: {} for e in self.engs}
        self.last_write = {}
        self.readers = {}
        self.dma_sems = []
        self.dma_uses = []
        self.dma_rr = 0
        self.same_engine_sync = same_engine_sync
        self.n_inst = 0
        self.out_tokens = []
        self.last_tok = {}
        self.dma_last = {}

    def sem(self, name):
        return self.stack.enter_context(self.nc.semaphore(name))

    def sbuf(self, name, shape, dtype):
        return self.stack.enter_context(self.nc.sbuf_tensor(name, list(shape), dtype))

    def psum(self, name, shape, dtype):
        return self.stack.enter_context(self.nc.psum_tensor(name, list(shape), dtype))

    def _csem(self, e, epoch):
        while len(self.csems[e]) <= epoch:
            self.csems[e].append(self.sem(f"c_{e}_{len(self.csems[e])}"))
        return self.csems[e][epoch]

    def _deps(self, reads, writes):
        deps = []
        for r in reads:
            t = self.last_write.get(r)
            if t is not None:
                deps.append(t)
        for w in writes:
            t = self.last_write.get(w)
            if t is not None:
                deps.append(t)
            for t in self.readers.get(w, {}).values():
                deps.append(t)
        return deps

    def _commit(self, token, reads, writes):
        for r in reads:
            self.readers.setdefault(r, {})[token[0]] = token
        for w in writes:
            self.last_write[w] = token
            self.readers[w] = {}

    def _waits(self, e, deps):
        need = {}
        for (src, sem_id, sem, val) in deps:
            if src == e and (e == 'pe' or not self.same_engine_sync):
                continue
            if self.seen[e].get(sem_id, 0) >= val:
                continue
            if need.get(sem_id, (None, 0))[1] < val:
                need[sem_id] = (sem, val)
        out = []
        for sem_id, (sem, val) in need.items():
            self.seen[e][sem_id] = val
            out.append((sem, val))
        return out

    def op(self, e, fn, reads=(), writes=()):
        reads = list(reads)
        writes = list(writes)
        deps = self._deps(reads, writes)
        waits = self._waits(e, deps)
        k = self.count[e]
        self.count[e] += 1
        epoch, idx = divmod(k, EPOCH)
        sem = self._csem(e, epoch)
        token = (e, (e, epoch), sem, idx + 1)
        self.last_tok[e] = token
        self._commit(token, reads, writes)

        def emit(eng, fn=fn, waits=waits, sem=sem):
            for (s, v) in waits:
                eng.wait_ge(s, v)
            fn(eng).then_inc(sem, 1)
        self.ops[e].append(emit)
        self.n_inst += 1 + len(waits)
        return token

    def dma(self, out_ap, in_ap, reads=(), writes=(), queue='sp', **kw):
        reads = list(reads)
        writes = list(writes)
        if not self.dma_sems:
            for i in range(N_DMA_SEMS):
                self.dma_sems.append(self.sem(f"dma_{i}"))
                self.dma_uses.append(0)
        si = self.dma_rr
        self.dma_rr = (self.dma_rr + 1) % N_DMA_SEMS
        sem = self.dma_sems[si]
        prev = self.dma_uses[si] * 16
        self.dma_uses[si] += 1
        target = prev + 16
        deps = self._deps(reads, writes)
        if prev > 0:
            deps.append(('dmaprev', ('dma', si), sem, prev))
        waits = self._waits(queue, deps)
        token = (f'dma{si}_{target}', ('dma', si), sem, target)
        self.dma_last[si] = token
        self._commit(token, reads, writes)

        def emit(eng, waits=waits, sem=sem, out_ap=out_ap, in_ap=in_ap, kw=kw):
            for (s, v) in waits:
                eng.wait_ge(s, v)
            eng.dma_start(out=out_ap, in_=in_ap, **kw).then_inc(sem, 16)
        self.ops[queue].append(emit)
        self.n_inst += 1 + len(waits)
        return token

    def barrier(self):
        toks = list(self.last_tok.values()) + list(self.dma_last.values())
        for e in self.engs:
            waits = self._waits(e, [t for t in toks if t[0] != e])

            def emit(eng, waits=waits):
                for (s, v) in waits:
                    eng.wait_ge(s, v)
            self.ops[e].append(emit)
            self.n_inst += len(waits)
        self.last_write = {}
        self.readers = {}

    def wait_all_outputs(self, tokens, e='sp'):
        waits = self._waits(e, tokens)

        def emit(eng, waits=waits):
            for (s, v) in waits:
                eng.wait_ge(s, v)
        self.ops[e].append(emit)

    def finish(self):
        nc = self.nc
        ops = self.ops
        with nc.Block() as block:
            @block.tensor
            def _(eng):
                for f in ops['pe']:
                    f(eng)

            @block.scalar
            def _(eng):
                for f in ops['act']:
                    f(eng)

            @block.vector
            def _(eng):
                for f in ops['dve']:
                    f(eng)

            @block.gpsimd
            def _(eng):
                for f in ops['pool']:
                    f(eng)

            @block.sync
            def _(eng):
                for f in ops['sp']:
                    f(eng)
        self.stack.close()

from concourse.bass_utils import run_bass_kernel_spmd

T = 2048
D = 1024
DIN = 3840
DFF = 2816
NFC = 22
EPS = 1e-6
LOG_E05 = 0.6065306597126334


class Arena:
    def __init__(self, ap_f32, nwords):
        self.ap = ap_f32
        self.n = nwords
        self.off = 0

    def reset(self):
        self.off = 0

    def alloc(self, shape, dtype):
        nel = 1
        for s in shape[1:]:
            nel *= s
        if dtype == BF16:
            nw = (nel + 1) // 2
        else:
            nw = nel
        nw = (nw + 1) // 2 * 2
        assert self.off + nw <= self.n, f"arena overflow {self.off}+{nw}>{self.n}"
        v = self.ap[0:shape[0], self.off:self.off + nw]
        self.off += nw
        if dtype != F32:
            v = v.bitcast(dtype)
        v = v[:, 0:nel]
        if len(shape) == 3:
            v = v.rearrange("p (a b) -> p a b", b=shape[2])
        elif len(shape) == 4:
            v = v.rearrange("p (a b c) -> p a b c", b=shape[2], c=shape[3])
        return v


def host_consts():
    c = {}
    c['c_ident'] = np.eye(128, dtype=np.float32)
    c['c_ones'] = np.ones((128, 128), np.float32)
    bd = np.zeros((128, 128), np.float32)
    bd[:64, :64] = 1
    bd[64:, 64:] = 1
    c['c_bd'] = bd
    half = 32
    freqs = (np.float32(10000.0) ** (-np.arange(half, dtype=np.float32) / np.float32(half))).astype(np.float32)
    pos = np.arange(T, dtype=np.float32)
    ang = (pos[None, :] * freqs[:, None]).astype(np.float32)
    cs = np.cos(ang).astype(np.float32)
    sn = np.sin(ang).astype(np.float32)
    rot = np.zeros((128, 2, T), np.float32)
    for p in range(128):
        rot[p, 0] = cs[p % 32]
        rot[p, 1] = -sn[p % 32] if (p % 64) < 32 else sn[p % 32]
    c['c_rot'] = rot
    d = np.arange(4096, dtype=np.int64) - 2047
    n = np.abs(d)
    nf = np.maximum(n, 1).astype(np.float32)
    lg = (np.log(nf / np.float32(8.0)) / np.float32(np.log(16.0)) * np.float32(8.0)).astype(np.float32)
    large = np.minimum(8 + lg.astype(np.int32), 15)
    bucket = np.where(d > 0, 16, 0) + np.where(n < 8, n, large)
    oh = np.zeros((32, 4096), np.float32)
    oh[bucket, np.arange(4096)] = 1.0
    oh[:, 4095] = 0.0
    c['c_onehot'] = oh
    r = np.arange(128)[:, None]
    q = np.arange(128)[None, :]
    su = (q > r).astype(np.float32)
    ui = (q >= r).astype(np.float32)
    sl = (q < r).astype(np.float32)
    li = (q <= r).astype(np.float32)
    mtr = np.zeros((128, 2, 256), np.float32)
    mtr[:, 0] = np.concatenate([su, ui], axis=1)
    mtr[:, 1] = np.concatenate([sl, li], axis=1)
    c['c_mtr'] = mtr
    ml = np.zeros((128, 2, 128), np.float32)
    ml[:, 0] = sl
    ml[:, 1] = su
    c['c_ml'] = ml
    rst = np.ones((128, 512), np.float32)
    rst[:, ::128] = 0.0
    c['c_rst'] = rst
    lgam = np.log1p(-np.exp2(-5.0 - np.arange(6, dtype=np.float32))).astype(np.float32)
    c['c_lgam'] = np.tile(lgam[None, :], (128, 1)).astype(np.float32)
    return c


CONST_SHAPES = {
    'c_ident': [128, 128], 'c_ones': [128, 128], 'c_bd': [128, 128], 'c_rot': [128, 2, T],
    'c_onehot': [32, 4096], 'c_mtr': [128, 2, 256], 'c_ml': [128, 2, 128], 'c_rst': [128, 512],
    'c_lgam': [128, 6],
}

PARAM_SHAPES = {
    'mix_norm_g': [2, 1024], 'w_in': [2, 1024, 3840], 'w_out': [2, 1024, 1024],
    'ret_gn_w': [2, 384], 'ret_gn_b': [2, 384], 'rwkv_mu': [2, 6, 2, 384], 'rwkv_w0': [2, 2, 384],
    'rwkv_w1': [2, 2, 384, 64], 'rwkv_w2': [2, 2, 64, 384], 'rwkv_a0': [2, 2, 384],
    'rwkv_a1': [2, 2, 384, 64], 'rwkv_a2': [2, 2, 64, 384], 'rwkv_g1': [2, 384, 128],
    'rwkv_g2': [2, 128, 384], 'rwkv_k_k': [2, 384], 'rwkv_k_a': [2, 384], 'rwkv_r_k': [2, 6, 64],
    'rwkv_ln_w': [2, 384], 'rwkv_ln_b': [2, 384], 'diff_lambda': [2, 4, 32], 'diff_subln_w': [2, 64],
    'rel_bias': [32, 4], 'ffn_norm_g': [2, 1024], 'w_gate': [2, 1024, 2816], 'w_up': [2, 1024, 2816],
    'w_down': [2, 2816, 1024], 'final_norm_g': [1024],
}


def build(nseq=2, depth=2, mixers=('ret', 'rwkv', 'diff'), ffn=True):
    nc = bass.Bass("TRN2", target_bir_lowering=False)
    NTOK = nseq * T
    xT_in = nc.dram_tensor("xT", [D, NTOK], F32, kind="ExternalInput").ap()
    prm = {k: nc.dram_tensor(k, shp, F32, kind="ExternalInput").ap() for k, shp in PARAM_SHAPES.items()}
    cst = {k: nc.dram_tensor(k, shp, F32, kind="ExternalInput").ap() for k, shp in CONST_SHAPES.items()}
    outT = nc.dram_tensor("outT", [D, NTOK], F32, kind="ExternalOutput").ap()
    xres = nc.dram_tensor("xres", [D, NTOK], F32).ap()
    bvec_d = nc.dram_tensor("bvec_d", [4, 4096], F32).ap()
    lw_d = nc.dram_tensor("rw_lw", [2, 384, T], F32).ap()
    as_d = nc.dram_tensor("rw_as", [2, 384, T], BF16).ap()
    gate_d = nc.dram_tensor("rw_gate", [384, T], BF16).ap()

    P = Prog(nc)
    hT = P.sbuf("hT", [128, 8, T], BF16)
    mT = P.sbuf("mT", [128, 8, T], BF16)
    stg = [P.sbuf(f"stg{i}", [128, 8, 256], F32) for i in range(2)]
    wbs = [P.sbuf(f"wb{i}", [128, 8, 256], BF16) for i in range(4)]
    ident = P.sbuf("ident", [128, 128], F32)
    ones = P.sbuf("ones", [128, 128], F32)
    bdm = P.sbuf("bdm", [128, 128], F32)
    mtr = P.sbuf("mtr", [128, 2, 256], F32)
    mlm = P.sbuf("mlm", [128, 2, 128], F32)
    rst = P.sbuf("rst", [128, 512], F32)
    lgam = P.sbuf("lgam", [128, 6], F32)
    zt = P.sbuf("zt", [128, 640], BF16)
    NPC = 256
    pp = P.sbuf("pp", [128, NPC], F32)
    ARW = 26300
    arena_t = P.sbuf("arena", [128, ARW], F32)
    A = Arena(arena_t[:], ARW)
    ps = [P.psum(f"ps{i}", [128, 512], F32) for i in range(7)]
    psb_t = P.psum("psb", [128, 1024], BF16)
    psb = psb_t[:]

    def MM(out, lhsT, rhs, start, stop, R, W):
        return P.op('pe', lambda e: e.matmul(out, lhsT, rhs, start=start, stop=stop, skip_group_check=True), R, W)

    def TRN(out, in_, idn, R, W):
        return P.op('pe', lambda e: e.transpose(out, in_, idn), R, W)

    def TT(eng, out, in0, in1, op, R, W):
        return P.op(eng, lambda e: e.tensor_tensor(out, in0, in1, op), R, W)

    def TS(eng, out, in0, s1, s2, op0, op1, R, W):
        if s2 is None:
            return P.op(eng, lambda e: e.tensor_scalar(out, in0, s1, None, op0), R, W)
        return P.op(eng, lambda e: e.tensor_scalar(out, in0, s1, s2, op0, op1), R, W)

    def STT(out, in0, scalar, in1, op0, op1, R, W):
        return P.op('dve', lambda e: e.scalar_tensor_tensor(out, in0, scalar, in1, op0, op1), R, W)

    def ACT(out, in_, func, R, W, bias=0.0, scale=1.0):
        return P.op('act', lambda e: e.activation(out, in_, func, bias=bias, scale=scale), R, W)

    def CP(eng, out, in_, R, W):
        if eng == 'act':
            return P.op('act', lambda e: e.copy(out, in_), R, W)
        return P.op(eng, lambda e: e.tensor_copy(out, in_), R, W)

    def RSUM(out, in_, R, W):
        return P.op('dve', lambda e: e.reduce_sum(out, in_, AX.X), R, W)

    def RCP(out, in_, R, W):
        return P.op('dve', lambda e: e.reciprocal(out, in_), R, W)

    def MSET(eng, ap, val, W):
        return P.op(eng, lambda e: e.memset(ap, val), (), W)

    def fm(ap2d):
        return ap2d.rearrange("(c p) n -> p c n", p=128)

    P.dma(ident[:], cst['c_ident'], writes=['ident'])
    P.dma(ones[:], cst['c_ones'], writes=['ones'])
    P.dma(bdm[:], cst['c_bd'], writes=['bdm'])
    P.dma(mtr[:], cst['c_mtr'], writes=['mtr'])
    P.dma(mlm[:], cst['c_ml'], writes=['mlm'])
    P.dma(rst[:], cst['c_rst'], writes=['rst'])
    P.dma(lgam[:], cst['c_lgam'], writes=['lgam'])
    MSET('pool', zt[:], 0.0, ['zt'])
    ppo = {}
    ppn = [0]

    def pcol(name, vec_ap, n):
        off = ppn[0]
        ppn[0] += n
        assert ppn[0] <= NPC
        P.dma(pp[:, off:off + n], vec_ap.rearrange("(c p) -> p c", p=128), writes=[f'pp_{name}'],
              allow_slow_non_contiguous=True)
        ppo[name] = off
        return off

    for l in range(depth):
        pcol(f"g1_{l}", prm['mix_norm_g'][l], 8)
        pcol(f"g2_{l}", prm['ffn_norm_g'][l], 8)
        pcol(f"rgw_{l}", prm['ret_gn_w'][l], 3)
        pcol(f"rgb_{l}", prm['ret_gn_b'][l], 3)
        for f in range(6):
            for s in range(2):
                pcol(f"mu_{l}_{f}_{s}", prm['rwkv_mu'][l, f, s], 3)
        for d in range(2):
            pcol(f"w0_{l}_{d}", prm['rwkv_w0'][l, d], 3)
            pcol(f"a0_{l}_{d}", prm['rwkv_a0'][l, d], 3)
        pcol(f"kk_{l}", prm['rwkv_k_k'][l], 3)
        pcol(f"ka_{l}", prm['rwkv_k_a'][l], 3)
        pcol(f"rk_{l}", prm['rwkv_r_k'][l].rearrange("h n -> (h n)"), 3)
        pcol(f"lnw_{l}", prm['rwkv_ln_w'][l], 3)
        pcol(f"lnb_{l}", prm['rwkv_ln_b'][l], 3)
        off = ppn[0]
        ppn[0] += 1
        sw = prm['diff_subln_w'][l].rearrange("(c p) -> p c", p=64)
        P.dma(pp[0:64, off:off + 1], sw, writes=[f'ppsa{l}'], allow_slow_non_contiguous=True)
        P.dma(pp[64:128, off:off + 1], sw, writes=[f'ppsb{l}'], allow_slow_non_contiguous=True)
        ppo[f"sub_{l}"] = off
    pcol("gf", prm['final_norm_g'], 8)
    P.barrier()

    def pc(name, c=0):
        o = ppo[name] + c
        return pp[:, o:o + 1]

    wstate = {'s': 0, 'b': 0}

    def load_w(pieces, nrc, swap_from=None):
        si = wstate['s']
        wstate['s'] = (si + 1) % 2
        bi = wstate['b']
        wstate['b'] = (bi + 1) % 4
        off = 0
        names = []
        for i, ap in enumerate(pieces):
            n = ap.shape[1]
            P.dma(stg[si][:, 0:nrc, off:off + n], fm(ap), writes=[f'stg{si}_{i}'])
            names.append(f'stg{si}_{i}')
            off += n
        CP('pool', wbs[bi][:, 0:nrc, 0:off], stg[si][:, 0:nrc, 0:off], names, [f'wb{bi}'])
        return wbs[bi], f'wb{bi}', stg[si], names

    def xview(ap2d, s):
        return fm(ap2d)[:, :, s * T:(s + 1) * T]

    def rmsnorm(src, gname, s, out_dram=None):
        A.reset()
        sq = [A.alloc([128, 512], BF16) for _ in range(2)]
        rs = A.alloc([128, 512], F32)
        onesb = A.alloc([128, 128], BF16)
        xbs = [A.alloc([128, 8, 512], F32) for _ in range(2)]
        obs = [A.alloc([128, 8, 512], F32) for _ in range(2)] if out_dram is not None else None
        CP('pool', onesb, ones[:], [], ['onesb'])
        for b in range(4):
            tk = slice(b * 512, (b + 1) * 512)
            xb = xbs[b % 2]
            xbn = f'xb{b % 2}'
            P.dma(xb, src[:, :, tk], reads=[f'xr_{s}_{dc}_{b}' for dc in range(8)], writes=[xbn])
            for c in range(8):
                ACT(sq[c % 2], xb[:, c, :], AF.Square, [xbn], [f'sq{c % 2}'])
                MM(ps[b % 2][:], onesb, sq[c % 2], c == 0, c == 7, [f'sq{c % 2}', 'onesb'], [f'ps{b % 2}'])
            ACT(rs, ps[b % 2][:], AF.Sqrt, [f'ps{b % 2}'], ['rs'], bias=EPS, scale=1.0 / D)
            RCP(rs, rs, ['rs'], ['rs'])
            for c in range(8):
                if out_dram is None:
                    STT(hT[:, c, tk], xb[:, c, :], pc(gname, c), rs, ALU.mult, ALU.mult, [xbn, 'rs', 'pp'], ['hT'])
                else:
                    STT(obs[b % 2][:, c, :], xb[:, c, :], pc(gname, c), rs, ALU.mult, ALU.mult, [xbn, 'rs', 'pp'], [f'ob{b % 2}'])
            if out_dram is not None:
                tok = P.dma(out_dram[:, :, tk], obs[b % 2], reads=[f'ob{b % 2}'])
                out_tokens.append(tok)
        P.barrier()

    out_tokens = []

    def proj_fm(psum_ap, psname, w, wname, c0, m, tk):
        for c in range(8):
            MM(psum_ap, w[:, c, c0:c0 + m], hT[:, c, tk], c == 0, c == 7, [wname, 'hT'], [psname])

    def proj_tm(psum_ap, psname, w, wname, c0, n, t):
        for c in range(8):
            MM(psum_ap, hT[:, c, t * 128:(t + 1) * 128], w[:, c, c0:c0 + n], c == 0, c == 7, [wname, 'hT'], [psname])

    def resid_add(psum_ap, psname, src, dst, s, dc, b, rt, rtname):
        tk = slice(b * 512, (b + 1) * 512)
        xn = f'xr_{s}_{dc}_{b}'
        P.dma(rt, src[:, dc, tk], reads=[xn], writes=[rtname])
        TT('dve', rt, psum_ap, rt, ALU.add, [psname, rtname], [rtname])
        P.dma(dst[:, dc, tk], rt, reads=[rtname], writes=[xn], queue='act')


    def xnames(s, b):
        return [f'xr_{s}_{dc}_{b}' for dc in range(8)]

    def out_proj(l, s, src, dst):
        A.reset()
        rts = [A.alloc([128, 512], F32) for _ in range(4)]
        k = 0
        for g in range(4):
            w, wn, _, _ = load_w([prm['w_out'][l][:, g * 256:(g + 1) * 256]], 8)
            for dci in range(2):
                dc = g * 2 + dci
                for b in range(4):
                    tk = slice(b * 512, (b + 1) * 512)
                    pb = ps[k % 4]
                    pn = f'ps{k % 4}'
                    for c in range(8):
                        MM(pb[:], w[:, c, dci * 128:(dci + 1) * 128], mT[:, c, tk], c == 0, c == 7, [wn, 'mT'], [pn])
                    resid_add(pb[:], pn, src, dst, s, dc, b, rts[k % 4], f'rt{k % 4}')
                    k += 1
        P.barrier()

    def ffn_phase(l, s, xr):
        TB = 2048
        for tb in range(T // TB):
            A.reset()
            hid = A.alloc([128, NFC, TB], BF16)
            sg = [A.alloc([128, 512], F32) for _ in range(2)]
            rts = [A.alloc([128, 512], F32) for _ in range(3)]
            k = 0
            for fc in range(NFC):
                if fc % 2 == 0:
                    wg_, wgn, _, _ = load_w([prm['w_gate'][l][:, fc * 128:(fc + 2) * 128]], 8)
                    wu_, wun, _, _ = load_w([prm['w_up'][l][:, fc * 128:(fc + 2) * 128]], 8)
                fo = (fc % 2) * 128
                for sb in range(TB // 512):
                    tk = slice(tb * TB + sb * 512, tb * TB + (sb + 1) * 512)
                    hk = slice(sb * 512, (sb + 1) * 512)
                    pg, pgn = ps[(2 * k) % 4], f'ps{(2 * k) % 4}'
                    pu, pun = ps[(2 * k + 1) % 4], f'ps{(2 * k + 1) % 4}'
                    proj_fm(pg[:], pgn, wg_, wgn, fo, 128, tk)
                    proj_fm(pu[:], pun, wu_, wun, fo, 128, tk)
                    ACT(sg[k % 2], pg[:], AF.Silu, [pgn], [f'sg{k % 2}'])
                    TT('dve', hid[:, fc, hk], pu[:], sg[k % 2], ALU.mult, [pun, f'sg{k % 2}'], ['hid'])
                    k += 1
            k = 0
            for g in range(4):
                ws = []
                for (r0, nrc) in ((0, 8), (8, 8), (16, 6)):
                    w, wn, _, _ = load_w([prm['w_down'][l][r0 * 128:(r0 + nrc) * 128, g * 256:(g + 1) * 256]], nrc)
                    ws.append((w, wn, r0, nrc))
                for dci in range(2):
                    dc = g * 2 + dci
                    for sb in range(TB // 512):
                        b = tb * (TB // 512) + sb
                        hk = slice(sb * 512, (sb + 1) * 512)
                        pb, pn = ps[4 + k % 3], f'ps{4 + k % 3}'
                        for (w, wn, r0, nrc) in ws:
                            for c in range(nrc):
                                fcx = r0 + c
                                MM(pb[:], w[:, c, dci * 128:(dci + 1) * 128], hid[:, fcx, hk], fcx == 0, fcx == NFC - 1,
                                   [wn, 'hid'], [pn])
                        resid_add(pb[:], pn, xr, xr, s, dc, b, rts[k % 3], f'rt{k % 3}')
                        k += 1
            P.barrier()

    def gn_alloc(nq):
        return {'sq': A.alloc([128, nq, 64], F32), 's1': A.alloc([128, nq], F32), 's2': A.alloc([128, nq], F32),
                'm2': A.alloc([128, nq], F32)}

    def gn_stats(y, yname, nq, eps, center, g):
        sq, s1, s2, m2 = g['sq'], g['s1'], g['s2'], g['m2']
        if center:
            RSUM(s1, y, [yname], ['gs1'])
            TS('dve', s1, s1, 1.0 / 64, None, ALU.mult, None, ['gs1'], ['gs1'])
            TT('dve', y, y, s1.unsqueeze(2).to_broadcast([128, nq, 64]), ALU.subtract, [yname, 'gs1'], [yname])
        TT('dve', sq, y, y, ALU.mult, [yname], ['gsq'])
        RSUM(s2, sq, ['gsq'], ['gs2'])
        ACT(m2, s2, AF.Sqrt, ['gs2'], ['gm2'], bias=eps, scale=1.0 / 64)
        RCP(m2, m2, ['gm2'], ['gm2'])
        TT('dve', y, y, m2.unsqueeze(2).to_broadcast([128, nq, 64]), ALU.mult, [yname, 'gm2'], [yname])

    def retention_phase(l, s):
        W = prm['w_in'][l]
        for hp in range(3):
            A.reset()
            rot = A.alloc([128, 2, T], F32)
            qT = A.alloc([128, T], BF16)
            kT = A.alloc([128, T], BF16)
            gT = A.alloc([128, T], BF16)
            vtm = A.alloc([128, 16, 128], BF16)
            t1 = A.alloc([128, 512], F32)
            t2 = A.alloc([128, 512], F32)
            wsw = A.alloc([128, 8, 256], BF16)
            if hp == 0:
                P.dma(rot, cst['c_rot'], writes=['rot'])
            c0 = hp * 128
            w, wn, sg_, sgn = load_w([W[:, c0:c0 + 128], W[:, 384 + c0:384 + c0 + 128]], 8)
            for j in range(8):
                src0 = j * 32
                dst0 = (j ^ 1) * 32
                CP('pool', wsw[:, :, dst0:dst0 + 32], sg_[:, :, src0:src0 + 32], sgn, ['wsw'])
            for which, dstT in ((0, qT), (1, kT)):
                for b in range(4):
                    tk = slice(b * 512, (b + 1) * 512)
                    proj_fm(ps[0][:], 'ps0', w, wn, which * 128, 128, tk)
                    proj_fm(ps[1][:], 'ps1', wsw, 'wsw', which * 128, 128, tk)
                    TT('dve', t1, ps[0][:], rot[:, 0, tk], ALU.mult, ['ps0', 'rot'], ['t1'])
                    TT('dve', t2, ps[1][:], rot[:, 1, tk], ALU.mult, ['ps1', 'rot'], ['t2'])
                    TT('pool', dstT[:, tk], t1, t2, ALU.add, ['t1', 't2'], ['qkT'])
            w2, wn2, _, _ = load_w([W[:, 768 + c0:768 + c0 + 128], W[:, 1152 + c0:1152 + c0 + 128]], 8)
            for t in range(16):
                pb, pn = ps[t % 2], f'ps{t % 2}'
                proj_tm(pb[:, 0:128], pn, w2, wn2, 0, 128, t)
                CP('act', vtm[:, t, :], pb[:, 0:128], [pn], ['vtm'])
            for b in range(4):
                tk = slice(b * 512, (b + 1) * 512)
                pb, pn = ps[2 + b % 2], f'ps{2 + b % 2}'
                proj_fm(pb[:], pn, w2, wn2, 128, 128, tk)
                ACT(gT[:, tk], pb[:], AF.Silu, [pn], ['gT'])
            mk = A.alloc([128, 3968], F32)
            pTs = [A.alloc([128, 512], BF16) for _ in range(4)]
            ypair = A.alloc([128, 16, 128], F32)
            tmp = A.alloc([128, 512], F32)
            gsc = gn_alloc(4)
            for hh in range(2):
                h = hp * 2 + hh
                hb = hh * 64
                P.op('pool', lambda e: e.iota(mk, [[1, 3968]], base=-1920, channel_multiplier=-1,
                                              allow_small_or_imprecise_dtypes=True), (), ['mk'])
                ACT(mk, mk, AF.Abs, ['mk'], ['mk'])
                ACT(mk, mk, AF.Exp, ['mk', 'lgam'], ['mk'], bias=float(np.log(0.125)), scale=lgam[:, h:h + 1])
                for qb in range(4):
                    q0 = qb * 512
                    MM(ps[4][:], zt[:, 0:128], zt[:, 128:640], True, True, ['zt'], ['ps4'])
                    NK = 16

                    def score(kt):
                        sb_, sn = ps[kt % 4], f'ps{kt % 4}'
                        MM(sb_[:], kT[hb:hb + 64, kt * 128:(kt + 1) * 128], qT[hb:hb + 64, q0:q0 + 512], True, True,
                           ['qkT'], [sn])
                        off = q0 - kt * 128 + 1920
                        TT('dve', pTs[kt % 4], sb_[:], mk[:, off:off + 512], ALU.mult, [sn, 'mk'], [f'pT{kt % 4}'])

                    def pv(kt):
                        for qi in range(4):
                            MM(ps[4][:, qi * 64:(qi + 1) * 64], pTs[kt % 4][:, qi * 128:(qi + 1) * 128],
                               vtm[:, kt, hb:hb + 64], False, kt == NK - 1 and qi == 3, [f'pT{kt % 4}', 'vtm'], ['ps4'])
                    score(0)
                    score(1)
                    score(2)
                    for kt in range(NK):
                        if kt + 3 < NK:
                            score(kt + 3)
                        pv(kt)
                    yv = ypair[:, qb * 4:(qb + 1) * 4, hb:hb + 64]
                    CP('act', yv, ps[4][:, 0:256].rearrange("p (a b) -> p a b", b=64), ['ps4'], ['ypair'])
                    gn_stats(yv, 'ypair', 4, 1e-5, True, gsc)
            for qb in range(4):
                q0 = qb * 512
                for qi in range(4):
                    TRN(ps[5][:, qi * 128:(qi + 1) * 128], ypair[:, qb * 4 + qi, :], ident[:], ['ypair'], ['ps5'])
                TS('dve', tmp, ps[5][:], pp[:, ppo[f'rgw_{l}'] + hp:ppo[f'rgw_{l}'] + hp + 1],
                   pp[:, ppo[f'rgb_{l}'] + hp:ppo[f'rgb_{l}'] + hp + 1], ALU.mult, ALU.add, ['ps5'], ['tmp'])
                TT('pool', mT[:, hp, q0:q0 + 512], tmp, gT[:, q0:q0 + 512], ALU.mult, ['tmp', 'gT'], ['mT'])
            P.barrier()

    def diff_setup():
        A.reset()
        oh = A.alloc([32, 4096], F32)
        rb = A.alloc([32, 4], F32)
        bv = A.alloc([4, 4096], F32)
        P.dma(oh, cst['c_onehot'], writes=['oh'])
        P.dma(rb, prm['rel_bias'], writes=['rb'])
        for j in range(8):
            MM(ps[0][0:4, :], rb, oh[:, j * 512:(j + 1) * 512], True, True, ['oh', 'rb'], ['ps0'])
            CP('dve', bv[:, j * 512:(j + 1) * 512], ps[0][0:4, :], ['ps0'], ['bv'])
        P.dma(bvec_d, bv, reads=['bv'], writes=['bvec_d'])
        P.barrier()

    def diff_phase(l, s):
        W = prm['w_in'][l]
        lam_init = 0.8 - 0.6 * float(np.exp(-0.3 * l))
        A.reset()
        lvt = A.alloc([128, 4, 32], F32)
        lp = A.alloc([128, 2, 32], F32)
        ls = A.alloc([128, 2], F32)
        nlam = A.alloc([128, 1], F32)
        P.dma(lvt, bass.AP(prm['diff_lambda'].tensor, l * 128, [[0, 128], [32, 4], [1, 32]]), writes=['lvt'])
        TT('dve', lp[:, 0, :], lvt[:, 0, :], lvt[:, 1, :], ALU.mult, ['lvt'], ['lp'])
        TT('dve', lp[:, 1, :], lvt[:, 2, :], lvt[:, 3, :], ALU.mult, ['lvt'], ['lp'])
        RSUM(ls, lp, ['lp'], ['ls'])
        ACT(ls, ls, AF.Exp, ['ls'], ['ls'])
        TT('dve', nlam, ls[:, 1:2], ls[:, 0:1], ALU.subtract, ['ls'], ['nlam'])
        TS('dve', nlam, nlam, -lam_init, None, ALU.add, None, ['nlam'], ['nlam'])
        qc = [A.alloc([128, T], BF16) for _ in range(2)]
        kT = A.alloc([128, T], BF16)
        vaug = A.alloc([128, 16, 65], BF16)
        MSET('pool', qc[0], 0.0, ['qkT'])
        MSET('pool', qc[1], 0.0, ['qkT'])
        MSET('pool', kT[64:128, :], 0.0, ['qkT'])
        bms = [A.alloc([128, 3968], F32) for _ in range(2)]
        tmps = [A.alloc([128, 512], F32) for _ in range(4)]
        pTs = [A.alloc([128, 512], BF16) for _ in range(4)]
        opair = A.alloc([128, 16, 128], F32)
        o1 = A.alloc([128, 4, 64], F32)
        rr = A.alloc([128, 2, 4], F32)
        gsc = gn_alloc(4)
        MSET('pool', vaug[:, :, 64:65], 1.0, ['vaug'])
        for h in range(4):
            hb = (h % 2) * 64
            mc = 6 + h // 2
            def dload(hx):
                wx = load_w([W[:, 3072 + hx * 64:3072 + (hx + 1) * 64], W[:, 3328 + hx * 64:3328 + (hx + 1) * 64],
                             W[:, 3584 + hx * 64:3584 + (hx + 1) * 64]], 8)
                P.dma(bms[hx % 2], bass.AP(bvec_d.tensor, hx * 4096, [[1, 128], [1, 3968]]), reads=['bvec_d'],
                      writes=[f'bm{hx % 2}'])
                return wx
            if h == 0:
                nxt = dload(0)
            w, wn, _, _ = nxt
            bm = bms[h % 2]
            bmn = f'bm{h % 2}'
            for which in (0, 1):
                for b in range(4):
                    tk = slice(b * 512, (b + 1) * 512)
                    pb, pn = ps[b % 2], f'ps{b % 2}'
                    proj_fm(pb[0:64, :], pn, w, wn, which * 64, 64, tk)
                    if which == 1:
                        CP('act', kT[0:64, tk], pb[0:64, :], [pn], ['qkT'])
                    else:
                        CP('act', qc[0][0:32, tk], pb[0:32, :], [pn], ['qkT'])
                        CP('act', qc[1][32:64, tk], pb[32:64, :], [pn], ['qkT'])
            for t in range(16):
                pb, pn = ps[t % 2], f'ps{t % 2}'
                proj_tm(pb[:, 0:64], pn, w, wn, 128, 64, t)
                CP('act', vaug[:, t, 0:64], pb[:, 0:64], [pn], ['vaug'])
            if h + 1 < 4:
                nxt = dload(h + 1)
            for qb in range(4):
                q0 = qb * 512
                MM(ps[4][:], zt[:, 0:128], zt[:, 128:640], True, True, ['zt'], ['ps4'])
                MM(ps[5][:], zt[:, 0:128], zt[:, 128:640], True, True, ['zt'], ['ps5'])
                NS = 32

                def score(i):
                    kt, c = divmod(i, 2)
                    sb_, sn = ps[i % 4], f'ps{i % 4}'
                    MM(sb_[:], kT[:, kt * 128:(kt + 1) * 128], qc[c][:, q0:q0 + 512], True, True, ['qkT'], [sn])
                    j0 = kt * 128 - q0 + 2047
                    bview = bm[:, j0 - 511:j0 + 1][:, ::-1]
                    STT(tmps[i % 4], sb_[:], float(32 ** -0.5), bview, ALU.mult, ALU.add, [sn, bmn], [f'tm{i % 4}'])
                    ACT(pTs[i % 4], tmps[i % 4], AF.Exp, [f'tm{i % 4}'], [f'pT{i % 4}'])

                def pv(i):
                    kt, c = divmod(i, 2)
                    acc, an = ps[4 + c], f'ps{4 + c}'
                    for qi in range(4):
                        MM(acc[:, qi * 65:(qi + 1) * 65], pTs[i % 4][:, qi * 128:(qi + 1) * 128], vaug[:, kt, :],
                           False, i >= NS - 2 and qi == 3, [f'pT{i % 4}', 'vaug'], [an])
                score(0)
                score(1)
                score(2)
                for i in range(NS):
                    if i + 3 < NS:
                        score(i + 3)
                    pv(i)
                a0 = ps[4][:, 0:260].rearrange("p (a b) -> p a b", b=65)
                a1 = ps[5][:, 0:260].rearrange("p (a b) -> p a b", b=65)
                RCP(rr[:, 0, :], a0[:, :, 64], ['ps4'], ['rr'])
                RCP(rr[:, 1, :], a1[:, :, 64], ['ps5'], ['rr'])
                TS('dve', rr[:, 1, :], rr[:, 1, :], nlam, None, ALU.mult, None, ['rr', 'nlam'], ['rr'])
                o0 = opair[:, qb * 4:(qb + 1) * 4, hb:hb + 64]
                TT('dve', o0, a0[:, :, 0:64], rr[:, 0, :].unsqueeze(2).to_broadcast([128, 4, 64]), ALU.mult, ['ps4', 'rr'], ['opair'])
                TT('dve', o1, a1[:, :, 0:64], rr[:, 1, :].unsqueeze(2).to_broadcast([128, 4, 64]), ALU.mult, ['ps5', 'rr'], ['o1'])
                TT('dve', o0, o0, o1, ALU.add, ['opair', 'o1'], ['opair'])
                gn_stats(o0, 'opair', 4, EPS, False, gsc)
            if h % 2 == 1:
                so = ppo[f'sub_{l}']
                for qb in range(4):
                    q0 = qb * 512
                    for qi in range(4):
                        TRN(ps[6][:, qi * 128:(qi + 1) * 128], opair[:, qb * 4 + qi, :], ident[:], ['opair'], ['ps6'])
                    TS('dve', mT[:, mc, q0:q0 + 512], ps[6][:], pp[:, so:so + 1], 1.0 - lam_init,
                       ALU.mult, ALU.mult, ['ps6'], ['mT'])
        P.barrier()

    def rwkv_phase(l, s):
        W = prm['w_in'][l]
        base = 1536
        A.reset()
        Fz = A.alloc([128, 3, T + 2], F32)
        xs = A.alloc([128, 3, T], BF16)
        hw = [A.alloc([128, T], BF16) for _ in range(2)]
        ha = [A.alloc([128, T], BF16) for _ in range(2)]
        hg = A.alloc([128, T], BF16)
        w1b = A.alloc([128, 2, 3, 64], BF16)
        a1b = A.alloc([128, 2, 3, 64], BF16)
        g1b = A.alloc([128, 3, 128], BF16)
        w2b = A.alloc([64, 2, 384], BF16)
        a2b = A.alloc([64, 2, 384], BF16)
        g2b = A.alloc([128, 384], BF16)
        lst = A.alloc([128, 1536], F32)
        cm = A.alloc([128, 6, 3], F32)
        ka1 = A.alloc([128, 3], F32)
        ka2 = A.alloc([128, 3], F32)
        t1 = A.alloc([128, 512], F32)
        t2 = A.alloc([128, 512], F32)
        tsh = [A.alloc([128, 512], F32) for _ in range(2)]
        lwt = [A.alloc([128, 512], F32) for _ in range(2)]
        ast = [A.alloc([128, 512], BF16) for _ in range(2)]
        gtt = [A.alloc([128, 512], BF16) for _ in range(2)]
        for d in range(2):
            P.dma(lst[:, 0:192].rearrange("p (c r) -> p c r", r=64), fm(prm['rwkv_w1'][l, d]), writes=['lst'])
            CP('pool', w1b[:, d], lst[:, 0:192].rearrange("p (c r) -> p c r", r=64), ['lst'], ['lw8'])
            P.dma(lst[:, 192:384].rearrange("p (c r) -> p c r", r=64), fm(prm['rwkv_a1'][l, d]), writes=['lst2'])
            CP('pool', a1b[:, d], lst[:, 192:384].rearrange("p (c r) -> p c r", r=64), ['lst2'], ['lw8'])
            P.dma(lst[0:64, 384:768], prm['rwkv_w2'][l, d], writes=['lst3'])
            CP('pool', w2b[:, d, :], lst[0:64, 384:768], ['lst3'], ['lw8'])
            P.dma(lst[0:64, 768:1152], prm['rwkv_a2'][l, d], writes=['lst4'])
            CP('pool', a2b[:, d, :], lst[0:64, 768:1152], ['lst4'], ['lw8'])
        P.dma(lst[:, 0:384].rearrange("p (c r) -> p c r", r=128), fm(prm['rwkv_g1'][l]), writes=['lst', 'lst2'])
        CP('pool', g1b, lst[:, 0:384].rearrange("p (c r) -> p c r", r=128), ['lst', 'lst2'], ['lw8'])
        P.dma(lst[:, 1152:1536], prm['rwkv_g2'][l], writes=['lst5'])
        CP('pool', g2b, lst[:, 1152:1536], ['lst5'], ['lw8'])
        for f in range(6):
            o0 = ppo[f'mu_{l}_{f}_0']
            o1 = ppo[f'mu_{l}_{f}_1']
            TT('dve', cm[:, f, :], pp[:, o0:o0 + 3], pp[:, o1:o1 + 3], ALU.add, [], ['cm'])
        TS('dve', cm, cm, -1.0, 1.0, ALU.mult, ALU.add, ['cm'], ['cm'])
        oka = ppo[f'ka_{l}']
        TS('dve', ka1, pp[:, oka:oka + 3], -1.0, 1.0, ALU.mult, ALU.add, [], ['ka1'])
        TS('dve', ka2, pp[:, oka:oka + 3], -2.0, 2.0, ALU.mult, ALU.add, [], ['ka2'])

        def shift_mix(Fv, f, j, dst, dname, fname):
            o0 = ppo[f'mu_{l}_{f}_0'] + j
            o1 = ppo[f'mu_{l}_{f}_1'] + j
            for b in range(4):
                b0 = b * 512
                TS('dve', t1, Fv[:, 1 + b0:1 + b0 + 512], cm[:, f, j:j + 1], None, ALU.mult, None, [fname, 'cm'], ['t1'])
                STT(t2, Fv[:, b0:b0 + 512], pp[:, o0:o0 + 1], t1, ALU.mult, ALU.add, [fname, 't1'], ['t2'])
                STT(dst[:, b0:b0 + 512], Fv[:, 2 + b0:2 + b0 + 512], pp[:, o1:o1 + 1], t2, ALU.mult, ALU.add,
                    [fname, 't2'], [dname])

        def proj_to_F(Fv, fname, w, wn, c0):
            for b in range(4):
                tk = slice(b * 512, (b + 1) * 512)
                pb, pn = ps[b % 2], f'ps{b % 2}'
                proj_fm(pb[:], pn, w, wn, c0, 128, tk)
                CP('act', Fv[:, 1 + b * 512:1 + (b + 1) * 512], pb[:], [pn], [fname])

        MSET('pool', Fz[:, :, 0:1], 0.0, ['Fz'])
        MSET('pool', Fz[:, :, T + 1:T + 2], 0.0, ['Fz'])
        for j in range(3):
            w, wn, _, _ = load_w([W[:, base + 1152 + j * 128:base + 1152 + (j + 1) * 128]], 8)
            proj_to_F(Fz[:, j, :], 'Fz', w, wn, 0)
        for (f, kind) in ((3, 'w'), (4, 'a'), (5, 'g')):
            for j in range(3):
                shift_mix(Fz[:, j, :], f, j, xs[:, j, :], 'xs', 'Fz')
            for b in range(4):
                tk = slice(b * 512, (b + 1) * 512)
                if kind == 'g':
                    for j in range(3):
                        MM(ps[2][:], g1b[:, j, :], xs[:, j, tk], j == 0, j == 2, ['xs', 'lw8'], ['ps2'])
                    ACT(hg[:, tk], ps[2][:], AF.Sigmoid, ['ps2'], ['hg'])
                else:
                    for d in range(2):
                        pb, pn = ps[2 + d], f'ps{2 + d}'
                        wl = w1b if kind == 'w' else a1b
                        for j in range(3):
                            MM(pb[0:64, :], wl[:, d, j, :], xs[:, j, tk], j == 0, j == 2, ['xs', 'lw8'], [pn])
                        if kind == 'w':
                            ACT(hw[d][0:64, tk], pb[0:64, :], AF.Tanh, [pn], ['hw'])
                        else:
                            CP('act', ha[d][0:64, tk], pb[0:64, :], [pn], ['ha'])
        k = 0
        for j in range(3):
            cj = slice(j * 128, (j + 1) * 128)
            for b in range(4):
                tk = slice(b * 512, (b + 1) * 512)
                for d in range(2):
                    pb, pn = ps[k % 4], f'ps{k % 4}'
                    MM(pb[:], w2b[:, d, cj], hw[d][0:64, tk], True, True, ['hw', 'lw8'], [pn])
                    ACT(lwt[k % 2], pb[:], AF.Sigmoid, [pn], [f'lwt{k % 2}'], bias=pc(f'w0_{l}_{d}', j))
                    TS('dve', lwt[k % 2], lwt[k % 2], -LOG_E05, None, ALU.mult, None, [f'lwt{k % 2}'], [f'lwt{k % 2}'])
                    P.dma(lw_d[d, cj, tk], lwt[k % 2], reads=[f'lwt{k % 2}'], writes=[f'lw_{d}_{j}_{b}'])
                    k += 1
                    pb, pn = ps[k % 4], f'ps{k % 4}'
                    MM(pb[:], a2b[:, d, cj], ha[d][0:64, tk], True, True, ['ha', 'lw8'], [pn])
                    ACT(ast[k % 2], pb[:], AF.Sigmoid, [pn], [f'ast{k % 2}'], bias=pc(f'a0_{l}_{d}', j))
                    P.dma(as_d[d, cj, tk], ast[k % 2], reads=[f'ast{k % 2}'], writes=[f'as_{d}_{j}_{b}'])
                    k += 1
                pb, pn = ps[k % 4], f'ps{k % 4}'
                MM(pb[:], g2b[:, cj], hg[:, tk], True, True, ['hg', 'lw8'], [pn])
                CP('act', gtt[k % 2], pb[:], [pn], [f'gtt{k % 2}'])
                P.dma(gate_d[cj, tk], gtt[k % 2], reads=[f'gtt{k % 2}'], writes=[f'gate_{j}_{b}'])
                k += 1
        P.barrier()

        for j in range(3):
            cj = slice(j * 128, (j + 1) * 128)
            A.reset()
            xr = A.alloc([128, T], BF16)
            xk = A.alloc([128, T], BF16)
            xv = A.alloc([128, T], BF16)
            Vtm = A.alloc([128, 16, 128], BF16)
            ysum = A.alloc([128, 16, 128], F32)
            kkn = A.alloc([128, T], BF16)
            bonus = A.alloc([128, T], BF16)
            ka1 = A.alloc([128, 3], F32)
            identb = A.alloc([128, 128], BF16)
            gsc = gn_alloc(8)
            gtile = A.alloc([128, 512], BF16)
            te1 = A.alloc([128, 512], F32)
            mark = A.off
            Fv = A.alloc([128, T + 2], F32)
            cm = A.alloc([128, 6, 3], F32)
            ka2 = A.alloc([128, 3], F32)
            t1 = A.alloc([128, 512], F32)
            t2 = A.alloc([128, 512], F32)
            tsh = [A.alloc([128, 512], F32) for _ in range(2)]
            as0 = A.alloc([128, 512], BF16)
            as1 = A.alloc([128, 512], BF16)
            CP('pool', identb, ident[:], [], ['identb'])
            for f in range(6):
                o0 = ppo[f'mu_{l}_{f}_0']
                o1 = ppo[f'mu_{l}_{f}_1']
                TT('dve', cm[:, f, :], pp[:, o0:o0 + 3], pp[:, o1:o1 + 3], ALU.add, [], ['cm'])
            TS('dve', cm, cm, -1.0, 1.0, ALU.mult, ALU.add, ['cm'], ['cm'])
            TS('dve', ka1, pp[:, oka:oka + 3], -1.0, 1.0, ALU.mult, ALU.add, [], ['ka1'])
            TS('dve', ka2, pp[:, oka:oka + 3], -2.0, 2.0, ALU.mult, ALU.add, [], ['ka2'])
            MSET('pool', Fv[:, 0:1], 0.0, ['Fv'])
            MSET('pool', Fv[:, T + 1:T + 2], 0.0, ['Fv'])
            w, wn, _, _ = load_w([W[:, base + j * 128:base + (j + 1) * 128],
                                  W[:, base + 384 + j * 128:base + 384 + (j + 1) * 128]], 8)
            wv_, wvn, _, _ = load_w([W[:, base + 768 + j * 128:base + 768 + (j + 1) * 128]], 8)
            proj_to_F(Fv, 'Fv', w, wn, 0)
            shift_mix(Fv, 0, j, xr, 'xr', 'Fv')
            proj_to_F(Fv, 'Fv', w, wn, 128)
            shift_mix(Fv, 1, j, xk, 'xk', 'Fv')
            proj_to_F(Fv, 'Fv', wv_, wvn, 0)
            shift_mix(Fv, 2, j, xv, 'xv', 'Fv')
            okk = ppo[f'kk_{l}'] + j
            ork = ppo[f'rk_{l}'] + j
            for b in range(4):
                tk = slice(b * 512, (b + 1) * 512)
                TS('dve', t1, xk[:, tk], pp[:, okk:okk + 1], None, ALU.mult, None, ['xk'], ['t1'])
                ACT(t2, t1, AF.Square, ['t1'], ['t2'])
                MM(ps[0][:], bdm[:], t2, True, True, ['t2'], ['ps0'])
                ACT(t2, ps[0][:], AF.Sqrt, ['ps0'], ['t2'])
                TS('dve', t2, t2, 1e-12, None, ALU.max, None, ['t2'], ['t2'])
                RCP(t2, t2, ['t2'], ['t2'])
                TT('dve', kkn[:, tk], t1, t2, ALU.mult, ['t1', 't2'], ['kkn'])
                P.dma(as0, as_d[0, cj, tk], reads=[f'as_0_{j}_{b}'], writes=['as0'])
                P.dma(as1, as_d[1, cj, tk], reads=[f'as_1_{j}_{b}'], writes=['as1'])
                TT('pool', t1, as0, as1, ALU.add, ['as0', 'as1'], ['t1'])
                TS('dve', t1, t1, pp[:, oka + j:oka + j + 1], ka2[:, j:j + 1], ALU.mult, ALU.add, ['t1', 'ka2'], ['t1'])
                TT('dve', t1, t1, xk[:, tk], ALU.mult, ['t1', 'xk'], ['t1'])
                TT('dve', t1, t1, xr[:, tk], ALU.mult, ['t1', 'xr'], ['t1'])
                TS('dve', t2, t1, pp[:, ork:ork + 1], None, ALU.mult, None, ['t1'], ['t2'])
                MM(ps[1][:], bdm[:], t2, True, True, ['t2'], ['ps1'])
                TT('dve', bonus[:, tk], ps[1][:], xv[:, tk], ALU.mult, ['ps1', 'xv'], ['bonus'])
            for t in range(16):
                TRN(psb[:, (t % 8) * 128:(t % 8 + 1) * 128], xv[:, t * 128:(t + 1) * 128], identb, ['xv', 'identb'], ['ps7'])
                if t % 8 == 7:
                    g8 = t // 8
                    CP('act', Vtm[:, g8 * 8:(g8 + 1) * 8, :], psb.rearrange("p (a b) -> p a b", b=128), ['ps7'], ['Vtm'])
            MSET('pool', ysum, 0.0, ['ysum'])
            P.barrier()
            A.off = mark

            def unit_alloc():
                u = {}
                for nm in ('lwt', 'cl', 'gi', 'gv', 't1', 't2'):
                    u[nm] = A.alloc([128, 512], F32)
                for nm in ('as0', 'bb', 'Bt', 'Bbar', 'kd', 'Kt', 'Kbar'):
                    u[nm] = A.alloc([128, 512], BF16)
                u['AR'] = A.alloc([128, 2, 512], BF16)
                u['BKz'] = A.alloc([128, 16, 128], BF16).rearrange("p (a b c) d -> p a b c d", b=2, c=2)
                u['G'] = [A.alloc([128, 512], BF16) for _ in range(2)]
                u['Nm'] = A.alloc([128, 256], BF16)
                u['MM2'] = [A.alloc([128, 512], BF16) for _ in range(2)]
                u['Mb'] = [m_[:, 0:256] for m_ in u['MM2']]
                u['MTb'] = [m_[:, 256:512] for m_ in u['MM2']]
                u['PTb'] = [A.alloc([128, 256], BF16) for _ in range(2)]
                u['RHSb'] = A.alloc([128, 128], BF16)
                u['Ub'] = A.alloc([128, 128], BF16)
                u['Hf'] = A.alloc([128, 64], F32)
                u['Hs'] = A.alloc([128, 64], BF16)
                return u

            def unit(d, u):
                n = lambda s_: f'{s_}_{d}'
                lwt_, cl, gi, gv, t1, t2 = u['lwt'], u['cl'], u['gi'], u['gv'], u['t1'], u['t2']
                as0, bb, Bt, Bbar, kd, Kt, Kbar = u['as0'], u['bb'], u['Bt'], u['Bbar'], u['kd'], u['Kt'], u['Kbar']
                AR, BKz, G, Nm, Mb, MTb, PTb = u['AR'], u['BKz'], u['G'], u['Nm'], u['Mb'], u['MTb'], u['PTb']
                RHSb, Ub, Hf, Hs = u['RHSb'], u['Ub'], u['Hf'], u['Hs']
                Xb, Yb, Zb = ps[3 * d], ps[3 * d + 1], ps[3 * d + 2]
                pA = [Xb[:, 0:256], Yb[:, 0:256]]
                pB = [Xb[:, 256:512], Yb[:, 256:512]]
                pAn = [f'ps{3 * d}', f'ps{3 * d + 1}']
                pBn = pAn
                pD, pE = Zb[:, 0:256], Zb[:, 256:512]
                pDn, pEn = f'ps{3 * d + 2}', f'ps{3 * d + 2}'
                pFh = [Xb[:, 384:512], Yb[:, 384:512]]
                pU, pS = Xb[:, 256:384], Yb[:, 256:320]
                pUn, pSn = pAn[0], pAn[1]
                pT = psb
                pTn = 'ps7'
                MSET('pool', Hf, 0.0, [n('Hf')])
                MSET('pool', Hs, 0.0, [n('Hs')])
                MSET('pool', BKz, 0.0, [n('BKz')])
                for bi in range(4):
                    b = bi if d == 0 else 3 - bi
                    tk = slice(b * 512, (b + 1) * 512)
                    P.dma(lwt_, lw_d[d, cj, tk], reads=[f'lw_{d}_{j}_{b}'], writes=[n('lwt')])
                    P.dma(as0, as_d[d, cj, tk], reads=[f'as_{d}_{j}_{b}'], writes=[n('as0')])
                    yield
                    if d == 0:
                        P.op('dve', lambda e, cl=cl, lwt_=lwt_: e.tensor_tensor_scan(cl, rst[:], lwt_, 0.0, ALU.mult, ALU.add),
                             [n('lwt')], [n('cl')])
                    else:
                        P.op('dve', lambda e, cl=cl, lwt_=lwt_: e.tensor_tensor_scan(cl[:, ::-1], rst[:], lwt_[:, ::-1], 0.0,
                                                                                    ALU.mult, ALU.add), [n('lwt')], [n('cl')])
                    cl3 = cl.rearrange("p (a b) -> p a b", b=128)
                    tot = cl3[:, :, 127:128] if d == 0 else cl3[:, :, 0:1]
                    ACT(gi, cl, AF.Exp, [n('cl')], [n('gi')])
                    ACT(gv, cl, AF.Exp, [n('cl')], [n('gv')], scale=-1.0)
                    TT('pool', t1, cl, lwt_, ALU.subtract, [n('cl'), n('lwt')], [n('t1')])
                    ACT(t1, t1, AF.Exp, [n('t1')], [n('t1')])
                    TT('dve', t2.rearrange("p (a b) -> p a b", b=128), cl3, tot.to_broadcast([128, 4, 128]), ALU.subtract,
                       [n('cl')], [n('t2')])
                    ACT(t2, t2, AF.Exp, [n('t2')], [n('t2')], scale=-1.0)
                    yield
                    STT(AR[:, 0, :], kkn[:, tk], -1.0, t1, ALU.mult, ALU.mult, ['kkn', n('t1')], [n('AR')])
                    TT('pool', AR[:, 1, :], xr[:, tk], gi, ALU.mult, ['xr', n('gi')], [n('AR')])
                    TT('pool', bb, kkn[:, tk], as0, ALU.mult, ['kkn', n('as0')], [n('bb')])
                    TT('dve', Bt, bb, gv, ALU.mult, [n('bb'), n('gv')], [n('Bt')])
                    TT('pool', Bbar, bb, t2, ALU.mult, [n('bb'), n('t2')], [n('Bbar')])
                    TS('dve', kd, as0, pp[:, oka + j:oka + j + 1], ka1[:, j:j + 1], ALU.mult, ALU.add, [n('as0'), 'ka1'], [n('kd')])
                    TT('pool', kd, kd, xk[:, tk], ALU.mult, [n('kd'), 'xk'], [n('kd')])
                    TT('dve', Kt, kd, gv, ALU.mult, [n('kd'), n('gv')], [n('Kt')])
                    TT('pool', Kbar, kd, t2, ALU.mult, [n('kd'), n('t2')], [n('Kbar')])
                    yield
                    for w_, src_, sn_ in ((0, Bbar, n('Bbar')), (1, Kbar, n('Kbar'))):
                        for ci in range(4):
                            TRN(pT[:, (w_ * 4 + ci) * 128:(w_ * 4 + ci + 1) * 128], src_[:, ci * 128:(ci + 1) * 128], identb,
                                [sn_, 'identb'], [pTn])
                    for w_ in range(2):
                        for ci in range(4):
                            for hh in range(2):
                                c0_ = (w_ * 4 + ci) * 128 + hh * 64
                                CP('act', BKz[:, ci, w_, hh, hh * 64:(hh + 1) * 64], pT[:, c0_:c0_ + 64],
                                   [pTn], [n('BKz')])
                    yield
                    for cii in range(4):
                        ci = cii if d == 0 else 3 - cii
                        cc = slice(ci * 128, (ci + 1) * 128)
                        cg = b * 4 + ci
                        for hh in range(2):
                            hb = hh * 64
                            MM(pA[hh], Bt[hb:hb + 64, cc], AR[hb:hb + 64, :, cc], True, True, [n('Bt'), n('AR')], [pAn[hh]])
                            MM(pB[hh], Kt[hb:hb + 64, cc], AR[hb:hb + 64, :, cc], True, True, [n('Kt'), n('AR')], [pBn[hh]])
                        yield
                        for hh in range(2):
                            TT('dve', G[hh][:, 0:256], pA[hh], mtr[:, d, 0:256], ALU.mult, [pAn[hh]], [n(f'G{hh}')])
                            TT('dve', G[hh][:, 256:512], pB[hh], mtr[:, d, 0:256], ALU.mult, [pBn[hh]], [n(f'G{hh}')])
                        for hh in range(2):
                            hb = hh * 64
                            MM(pA[hh][:, 0:128], AR[hb:hb + 64, 0, cc], Bt[hb:hb + 64, cc], True, True,
                               [n('Bt'), n('AR')], [pAn[hh]])
                        yield
                        for hh in range(2):
                            TT('dve', Nm[:, hh * 128:(hh + 1) * 128], pA[hh][:, 0:128], mlm[:, d, 0:128], ALU.mult,
                               [pAn[hh]], [n('Nm')])
                        NTh = [G[hh][:, 0:128] for hh in range(2)]
                        ARBh = [G[hh][:, 128:256] for hh in range(2)]
                        AKh = [G[hh][:, 256:384] for hh in range(2)]
                        ARKh = [G[hh][:, 384:512] for hh in range(2)]
                        for hh in range(2):
                            TT('dve', PTb[0][:, hh * 128:(hh + 1) * 128], NTh[hh], ident[:], ALU.add, [n(f'G{hh}')], [n('PTb0')])
                        curM = [Nm[:, hh * 128:(hh + 1) * 128] for hh in range(2)]
                        curMn = [n('Nm')] * 2
                        curMT = NTh
                        curMTn = [n('G0'), n('G1')]
                        pcur = 0
                        yield
                        for lev in range(1, 7):
                            sl_ = lev % 2
                            for hh in range(2):
                                MM(pD[:, hh * 128:(hh + 1) * 128], curMT[hh], curM[hh], True, True,
                                   [curMn[hh], curMTn[hh]], [pDn])
                            if lev < 6:
                                for hh in range(2):
                                    MM(pE[:, hh * 128:(hh + 1) * 128], curM[hh], curMT[hh], True, True,
                                       [curMn[hh], curMTn[hh]], [pEn])
                            yield
                            if lev < 6:
                                CP('act', u['MM2'][sl_], Zb[:], [pDn], [n(f'Mb{sl_}'), n(f'MTb{sl_}')])
                            else:
                                CP('act', Mb[sl_], pD, [pDn], [n(f'Mb{sl_}')])
                            curM = [Mb[sl_][:, hh * 128:(hh + 1) * 128] for hh in range(2)]
                            curMn = [n(f'Mb{sl_}')] * 2
                            curMT = [MTb[sl_][:, hh * 128:(hh + 1) * 128] for hh in range(2)]
                            curMTn = [n(f'MTb{sl_}')] * 2
                            for hh in range(2):
                                MM(pFh[hh], curM[hh], PTb[pcur][:, hh * 128:(hh + 1) * 128], True, True,
                                   [curMn[hh], n(f'PTb{pcur}')], [pAn[hh]])
                            yield
                            for hh in range(2):
                                TT('dve', PTb[1 - pcur][:, hh * 128:(hh + 1) * 128], pFh[hh], PTb[pcur][:, hh * 128:(hh + 1) * 128],
                                   ALU.add, [pAn[hh], n(f'PTb{pcur}')], [n(f'PTb{1 - pcur}')])
                            pcur = 1 - pcur
                        PTc = PTb[pcur]
                        PTn = n(f'PTb{pcur}')
                        for hh in range(2):
                            hb = hh * 64
                            vs = slice(hh * 64, (hh + 1) * 64)
                            o = pA[hh][:, 128:192]
                            MM(o, AR[hb:hb + 64, 0, cc], Hs[hb:hb + 64, :], True, False, [n('AR'), n('Hs')], [pAn[hh]])
                            MM(o, AKh[hh], Vtm[:, cg, vs], False, True, [n(f'G{hh}'), 'Vtm'], [pAn[hh]])
                        yield
                        CP('act', RHSb[:, 0:64], pA[0][:, 128:192], [pAn[0]], [n('RHSb')])
                        CP('dve', RHSb[:, 64:128], pA[1][:, 128:192], [pAn[1]], [n('RHSb')])
                        for hh in range(2):
                            vs = slice(hh * 64, (hh + 1) * 64)
                            MM(pU[:, hh * 64:(hh + 1) * 64], PTc[:, hh * 128:(hh + 1) * 128], RHSb[:, vs], True, True,
                               [PTn, n('RHSb')], [pUn])
                        yield
                        CP('dve', Ub, pU, [pUn], [n('Ub')])
                        for hh in range(2):
                            hb = hh * 64
                            vs = slice(hh * 64, (hh + 1) * 64)
                            o = pA[hh][:, 192:256]
                            MM(o, AR[hb:hb + 64, 1, cc], Hs[hb:hb + 64, :], True, False, [n('AR'), n('Hs')], [pAn[hh]])
                            MM(o, ARBh[hh], Ub[:, vs], False, False, [n(f'G{hh}'), n('Ub')], [pAn[hh]])
                            MM(o, ARKh[hh], Vtm[:, cg, vs], False, True, [n(f'G{hh}'), 'Vtm'], [pAn[hh]])
                        o = pS
                        for hh in range(2):
                            vs = slice(hh * 64, (hh + 1) * 64)
                            MM(o, BKz[:, ci, 0, hh, :], Ub[:, vs], hh == 0, False, [n('BKz'), n('Ub')], [pSn])
                            MM(o, BKz[:, ci, 1, hh, :], Vtm[:, cg, vs], False, hh == 1, [n('BKz'), 'Vtm'], [pSn])
                        yield
                        for hh in range(2):
                            vs = slice(hh * 64, (hh + 1) * 64)
                            TT('dve', ysum[:, cg, vs], pA[hh][:, 192:256], ysum[:, cg, vs], ALU.add,
                               [pAn[hh], f'ysum{cg}'], [f'ysum{cg}'])
                        gcol = gi[:, ci * 128 + 127:ci * 128 + 128] if d == 0 else gi[:, ci * 128:ci * 128 + 1]
                        STT(Hf, Hf, gcol, pS, ALU.mult, ALU.add, [n('Hf'), n('gi'), pSn], [n('Hf')])
                        CP('act', Hs, Hf, [n('Hf')], [n('Hs')])
                        yield

            units = [unit(d, unit_alloc()) for d in range(2)]
            active = list(units)
            first = True
            import os
            if os.environ.get('RW_SERIAL'):
                for g_ in units:
                    for _ in g_:
                        pass
                active = []
            NDUM = int(os.environ.get('RW_DUMMY', '0'))
            while active:
                for g_ in list(active):
                    try:
                        next(g_)
                    except StopIteration:
                        active.remove(g_)
                    for _ in range(NDUM):
                        MM(ps[6][:], zt[:, 0:128], zt[:, 128:640], True, True, ['zt'], ['ps6'])
            P.barrier()
            for g4 in range(4):
                tk = slice(g4 * 512, (g4 + 1) * 512)
                y8 = ysum[:, g4 * 4:(g4 + 1) * 4, :].rearrange("p a (h n) -> p (a h) n", n=64)
                gn_stats(y8, 'ysum', 8, 64e-5, True, gsc)
                for qi in range(4):
                    TRN(ps[3][:, qi * 128:(qi + 1) * 128], ysum[:, g4 * 4 + qi, :], ident[:], ['ysum'], ['ps3'])
                olw = ppo[f'lnw_{l}'] + j
                olb = ppo[f'lnb_{l}'] + j
                TS('dve', te1, ps[3][:], pp[:, olw:olw + 1], pp[:, olb:olb + 1], ALU.mult, ALU.add, ['ps3'], ['te1'])
                TT('pool', te1, te1, bonus[:, tk], ALU.add, ['te1', 'bonus'], ['te1'])
                P.dma(gtile, gate_d[cj, tk], reads=[f'gate_{j}_{g4}'], writes=['gtile'])
                TT('dve', mT[:, 3 + j, tk], te1, gtile, ALU.mult, ['te1', 'gtile'], ['mT'])
            P.barrier()

    diff_setup()
    for s in range(nseq):
        for l in range(depth):
            src = xview(xT_in, s) if l == 0 else xview(xres, s)
            dst = xview(xres, s)
            rmsnorm(src, f'g1_{l}', s)
            if 'ret' in mixers:
                retention_phase(l, s)
            else:
                MSET('pool', mT[:, 0:3, :], 0.0, ['mT'])
            if 'rwkv' in mixers:
                rwkv_phase(l, s)
            else:
                MSET('pool', mT[:, 3:6, :], 0.0, ['mT'])
            if 'diff' in mixers:
                diff_phase(l, s)
            else:
                MSET('pool', mT[:, 6:8, :], 0.0, ['mT'])
            P.barrier()
            out_proj(l, s, src, dst)
            if ffn:
                rmsnorm(dst, f'g2_{l}', s)
                ffn_phase(l, s, dst)
        rmsnorm(xview(xres, s), 'gf', s, out_dram=xview(outT, s))
    P.wait_all_outputs(out_tokens)
    P.finish()
    return nc, P


_CACHE = {}


def kernel(**inputs):
    x = np.asarray(inputs['x'], dtype=np.float32)
    consts = host_consts()
    if 'nc' not in _CACHE:
        _CACHE['nc'] = build()[0]
    nc = _CACHE['nc']
    params = {k: np.ascontiguousarray(np.asarray(inputs[k], dtype=np.float32)) for k in PARAM_SHAPES}
    in_maps = []
    for c in range(8):
        xs = np.ascontiguousarray(x[2 * c:2 * c + 2].reshape(2 * T, D).T)
        m = {'xT': xs}
        m.update(params)
        m.update(consts)
        in_maps.append(m)
    res = run_bass_kernel_spmd(nc, in_maps, core_ids=list(range(8)))
    out = np.empty((16, T, D), np.float32)
    for c in range(8):
        o = np.asarray(res.results[c]['outT'])
        out[2 * c:2 * c + 2] = o.T.reshape(2, T, D)
    return out
```

```python
import contextlib
import numpy as np
import concourse.bass as bass
import concourse.mybir as mybir

F32 = mybir.dt.float32
BF16 = mybir.dt.bfloat16
I32 = mybir.dt.int32
ALU = mybir.AluOpType
AF = mybir.ActivationFunctionType
AX = mybir.AxisListType

EPOCH = 16000
N_DMA_SEMS = 40


class Prog:
    def __init__(self, nc, same_engine_sync=True):
        self.nc = nc
        self.stack = contextlib.ExitStack()
        self.engs = ['pe', 'act', 'dve', 'pool', 'sp']
        self.ops = {e: [] for e in self.engs}
        self.count = {e: 0 for e in self.engs}
        self.csems = {e: [] for e in self.engs}
        self.seen = {e: {} for e in self.engs}
        self.last_write = {}
        self.readers = {}
        self.dma_sems = []
        self.dma_uses = []
        self.dma_rr = 0
        self.same_engine_sync = same_engine_sync
        self.n_inst = 0
        self.out_tokens = []
        self.last_tok = {}
        self.dma_last = {}

    def sem(self, name):
        return self.stack.enter_context(self.nc.semaphore(name))

    def sbuf(self, name, shape, dtype):
        return self.stack.enter_context(self.nc.sbuf_tensor(name, list(shape), dtype))

    def psum(self, name, shape, dtype):
        return self.stack.enter_context(self.nc.psum_tensor(name, list(shape), dtype))

    def _csem(self, e, epoch):
        while len(self.csems[e]) <= epoch:
            self.csems[e].append(self.sem(f"c_{e}_{len(self.csems[e])}"))
        return self.csems[e][epoch]

    def _deps(self, reads, writes):
        deps = []
        for r in reads:
            t = self.last_write.get(r)
            if t is not None:
                deps.append(t)
        for w in writes:
            t = self.last_write.get(w)
            if t is not None:
                deps.append(t)
            for t in self.readers.get(w, {}).values():
                deps.append(t)
        return deps

    def _commit(self, token, reads, writes):
        for r in reads:
            self.readers.setdefault(r, {})[token[0]] = token
        for w in writes:
            self.last_write[w] = token
            self.readers[w] = {}

    def _waits(self, e, deps):
        need = {}
        for (src, sem_id, sem, val) in deps:
            if src == e and (e == 'pe' or not self.same_engine_sync):
                continue
            if self.seen[e].get(sem_id, 0) >= val:
                continue
            if need.get(sem_id, (None, 0))[1] < val:
                need[sem_id] = (sem, val)
        out = []
        for sem_id, (sem, val) in need.items():
            self.seen[e][sem_id] = val
            out.append((sem, val))
        return out

    def op(self, e, fn, reads=(), writes=()):
        reads = list(reads)
        writes = list(writes)
        deps = self._deps(reads, writes)
        waits = self._waits(e, deps)
        k = self.count[e]
        self.count[e] += 1
        epoch, idx = divmod(k, EPOCH)
        sem = self._csem(e, epoch)
        token = (e, (e, epoch), sem, idx + 1)
        self.last_tok[e] = token
        self._commit(token, reads, writes)

        def emit(eng, fn=fn, waits=waits, sem=sem):
            for (s, v) in waits:
                eng.wait_ge(s, v)
            fn(eng).then_inc(sem, 1)
        self.ops[e].append(emit)
        self.n_inst += 1 + len(waits)
        return token

    def dma(self, out_ap, in_ap, reads=(), writes=(), queue='sp', **kw):
        reads = list(reads)
        writes = list(writes)
        if not self.dma_sems:
            for i in range(N_DMA_SEMS):
                self.dma_sems.append(self.sem(f"dma_{i}"))
                self.dma_uses.append(0)
        si = self.dma_rr
        self.dma_rr = (self.dma_rr + 1) % N_DMA_SEMS
        sem = self.dma_sems[si]
        prev = self.dma_uses[si] * 16
        self.dma_uses[si] += 1
        target = prev + 16
        deps = self._deps(reads, writes)
        if prev > 0:
            deps.append(('dmaprev', ('dma', si), sem, prev))
        waits = self._waits(queue, deps)
        token = (f'dma{si}_{target}', ('dma', si), sem, target)
        self.dma_last[si] = token
        self._commit(token, reads, writes)

        def emit(eng, waits=waits, sem=sem, out_ap=out_ap, in_ap=in_ap, kw=kw):
            for (s, v) in waits:
                eng.wait_ge(s, v)
            eng.dma_start(out=out_ap, in_=in_ap, **kw).then_inc(sem, 16)
        self.ops[queue].append(emit)
        self.n_inst += 1 + len(waits)
        return token

    def barrier(self):
        toks = list(self.last_tok.values()) + list(self.dma_last.values())
        for e in self.engs:
            waits = self._waits(e, [t for t in toks if t[0] != e])

            def emit(eng, waits=waits):
                for (s, v) in waits:
                    eng.wait_ge(s, v)
            self.ops[e].append(emit)
            self.n_inst += len(waits)
        self.last_write = {}
        self.readers = {}

    def wait_all_outputs(self, tokens, e='sp'):
        waits = self._waits(e, tokens)

        def emit(eng, waits=waits):
            for (s, v) in waits:
                eng.wait_ge(s, v)
        self.ops[e].append(emit)

    def finish(self):
        nc = self.nc
        ops = self.ops
        with nc.Block() as block:
            @block.tensor
            def _(eng):
                for f in ops['pe']:
                    f(eng)

            @block.scalar
            def _(eng):
                for f in ops['act']:
                    f(eng)

            @block.vector
            def _(eng):
                for f in ops['dve']:
                    f(eng)

            @block.gpsimd
            def _(eng):
                for f in ops['pool']:
                    f(eng)

            @block.sync
            def _(eng):
                for f in ops['sp']:
                    f(eng)
        self.stack.close()

from concourse.bass_utils import run_bass_kernel_spmd

T = 2048
D = 1024
DIN = 3840
DFF = 2816
NFC = 22
EPS = 1e-6
LOG_E05 = 0.6065306597126334


class Arena:
    def __init__(self, ap_f32, nwords):
        self.ap = ap_f32
        self.n = nwords
        self.off = 0

    def reset(self):
        self.off = 0

    def alloc(self, shape, dtype):
        nel = 1
        for s in shape[1:]:
            nel *= s
        if dtype == BF16:
            nw = (nel + 1) // 2
        else:
            nw = nel
        nw = (nw + 1) // 2 * 2
        assert self.off + nw <= self.n, f"arena overflow {self.off}+{nw}>{self.n}"
        v = self.ap[0:shape[0], self.off:self.off + nw]
        self.off += nw
        if dtype != F32:
            v = v.bitcast(dtype)
        v = v[:, 0:nel]
        if len(shape) == 3:
            v = v.rearrange("p (a b) -> p a b", b=shape[2])
        elif len(shape) == 4:
            v = v.rearrange("p (a b c) -> p a b c", b=shape[2], c=shape[3])
        return v


def host_consts():
    c = {}
    c['c_ident'] = np.eye(128, dtype=np.float32)
    c['c_ones'] = np.ones((128, 128), np.float32)
    bd = np.zeros((128, 128), np.float32)
    bd[:64, :64] = 1
    bd[64:, 64:] = 1
    c['c_bd'] = bd
    half = 32
    freqs = (np.float32(10000.0) ** (-np.arange(half, dtype=np.float32) / np.float32(half))).astype(np.float32)
    pos = np.arange(T, dtype=np.float32)
    ang = (pos[None, :] * freqs[:, None]).astype(np.float32)
    cs = np.cos(ang).astype(np.float32)
    sn = np.sin(ang).astype(np.float32)
    rot = np.zeros((128, 2, T), np.float32)
    for p in range(128):
        rot[p, 0] = cs[p % 32]
        rot[p, 1] = -sn[p % 32] if (p % 64) < 32 else sn[p % 32]
    c['c_rot'] = rot
    d = np.arange(4096, dtype=np.int64) - 2047
    n = np.abs(d)
    nf = np.maximum(n, 1).astype(np.float32)
    lg = (np.log(nf / np.float32(8.0)) / np.float32(np.log(16.0)) * np.float32(8.0)).astype(np.float32)
    large = np.minimum(8 + lg.astype(np.int32), 15)
    bucket = np.where(d > 0, 16, 0) + np.where(n < 8, n, large)
    oh = np.zeros((32, 4096), np.float32)
    oh[bucket, np.arange(4096)] = 1.0
    oh[:, 4095] = 0.0
    c['c_onehot'] = oh
    r = np.arange(128)[:, None]
    q = np.arange(128)[None, :]
    su = (q > r).astype(np.float32)
    ui = (q >= r).astype(np.float32)
    sl = (q < r).astype(np.float32)
    li = (q <= r).astype(np.float32)
    mtr = np.zeros((128, 2, 256), np.float32)
    mtr[:, 0] = np.concatenate([su, ui], axis=1)
    mtr[:, 1] = np.concatenate([sl, li], axis=1)
    c['c_mtr'] = mtr
    ml = np.zeros((128, 2, 128), np.float32)
    ml[:, 0] = sl
    ml[:, 1] = su
    c['c_ml'] = ml
    rst = np.ones((128, 512), np.float32)
    rst[:, ::128] = 0.0
    c['c_rst'] = rst
    lgam = np.log1p(-np.exp2(-5.0 - np.arange(6, dtype=np.float32))).astype(np.float32)
    c['c_lgam'] = np.tile(lgam[None, :], (128, 1)).astype(np.float32)
    return c


CONST_SHAPES = {
    'c_ident': [128, 128], 'c_ones': [128, 128], 'c_bd': [128, 128], 'c_rot': [128, 2, T],
    'c_onehot': [32, 4096], 'c_mtr': [128, 2, 256], 'c_ml': [128, 2, 128], 'c_rst': [128, 512],
    'c_lgam': [128, 6],
}

PARAM_SHAPES = {
    'mix_norm_g': [2, 1024], 'w_in': [2, 1024, 3840], 'w_out': [2, 1024, 1024],
    'ret_gn_w': [2, 384], 'ret_gn_b': [2, 384], 'rwkv_mu': [2, 6, 2, 384], 'rwkv_w0': [2, 2, 384],
    'rwkv_w1': [2, 2, 384, 64], 'rwkv_w2': [2, 2, 64, 384], 'rwkv_a0': [2, 2, 384],
    'rwkv_a1': [2, 2, 384, 64], 'rwkv_a2': [2, 2, 64, 384], 'rwkv_g1': [2, 384, 128],
    'rwkv_g2': [2, 128, 384], 'rwkv_k_k': [2, 384], 'rwkv_k_a': [2, 384], 'rwkv_r_k': [2, 6, 64],
    'rwkv_ln_w': [2, 384], 'rwkv_ln_b': [2, 384], 'diff_lambda': [2, 4, 32], 'diff_subln_w': [2, 64],
    'rel_bias': [32, 4], 'ffn_norm_g': [2, 1024], 'w_gate': [2, 1024, 2816], 'w_up': [2, 1024, 2816],
    'w_down': [2, 2816, 1024], 'final_norm_g': [1024],
}


def build(nseq=2, depth=2, mixers=('ret', 'rwkv', 'diff'), ffn=True):
    nc = bass.Bass("TRN2", target_bir_lowering=False)
    NTOK = nseq * T
    xT_in = nc.dram_tensor("xT", [D, NTOK], F32, kind="ExternalInput").ap()
    prm = {k: nc.dram_tensor(k, shp, F32, kind="ExternalInput").ap() for k, shp in PARAM_SHAPES.items()}
    cst = {k: nc.dram_tensor(k, shp, F32, kind="ExternalInput").ap() for k, shp in CONST_SHAPES.items()}
    outT = nc.dram_tensor("outT", [D, NTOK], F32, kind="ExternalOutput").ap()
    xres = nc.dram_tensor("xres", [D, NTOK], F32).ap()
    bvec_d = nc.dram_tensor("bvec_d", [4, 4096], F32).ap()
    lw_d = nc.dram_tensor("rw_lw", [2, 384, T], F32).ap()
    as_d = nc.dram_tensor("rw_as", [2, 384, T], BF16).ap()
    gate_d = nc.dram_tensor("rw_gate", [384, T], BF16).ap()

    P = Prog(nc)
    hT = P.sbuf("hT", [128, 8, T], BF16)
    mT = P.sbuf("mT", [128, 8, T], BF16)
    stg = [P.sbuf(f"stg{i}", [128, 8, 256], F32) for i in range(2)]
    wbs = [P.sbuf(f"wb{i}", [128, 8, 256], BF16) for i in range(4)]
    ident = P.sbuf("ident", [128, 128], F32)
    ones = P.sbuf("ones", [128, 128], F32)
    bdm = P.sbuf("bdm", [128, 128], F32)
    mtr = P.sbuf("mtr", [128, 2, 256], F32)
    mlm = P.sbuf("mlm", [128, 2, 128], F32)
    rst = P.sbuf("rst", [128, 512], F32)
    lgam = P.sbuf("lgam", [128, 6], F32)
    zt = P.sbuf("zt", [128, 640], BF16)
    NPC = 256
    pp = P.sbuf("pp", [128, NPC], F32)
    ARW = 26300
    arena_t = P.sbuf("arena", [128, ARW], F32)
    A = Arena(arena_t[:], ARW)
    ps = [P.psum(f"ps{i}", [128, 512], F32) for i in range(7)]
    psb_t = P.psum("psb", [128, 1024], BF16)
    psb = psb_t[:]

    def MM(out, lhsT, rhs, start, stop, R, W):
        return P.op('pe', lambda e: e.matmul(out, lhsT, rhs, start=start, stop=stop, skip_group_check=True), R, W)

    def TRN(out, in_, idn, R, W):
        return P.op('pe', lambda e: e.transpose(out, in_, idn), R, W)

    def TT(eng, out, in0, in1, op, R, W):
        return P.op(eng, lambda e: e.tensor_tensor(out, in0, in1, op), R, W)

    def TS(eng, out, in0, s1, s2, op0, op1, R, W):
        if s2 is None:
            return P.op(eng, lambda e: e.tensor_scalar(out, in0, s1, None, op0), R, W)
        return P.op(eng, lambda e: e.tensor_scalar(out, in0, s1, s2, op0, op1), R, W)

    def STT(out, in0, scalar, in1, op0, op1, R, W):
        return P.op('dve', lambda e: e.scalar_tensor_tensor(out, in0, scalar, in1, op0, op1), R, W)

    def ACT(out, in_, func, R, W, bias=0.0, scale=1.0):
        return P.op('act', lambda e: e.activation(out, in_, func, bias=bias, scale=scale), R, W)

    def CP(eng, out, in_, R, W):
        if eng == 'act':
            return P.op('act', lambda e: e.copy(out, in_), R, W)
        return P.op(eng, lambda e: e.tensor_copy(out, in_), R, W)

    def RSUM(out, in_, R, W):
        return P.op('dve', lambda e: e.reduce_sum(out, in_, AX.X), R, W)

    def RCP(out, in_, R, W):
        return P.op('dve', lambda e: e.reciprocal(out, in_), R, W)

    def MSET(eng, ap, val, W):
        return P.op(eng, lambda e: e.memset(ap, val), (), W)

    def fm(ap2d):
        return ap2d.rearrange("(c p) n -> p c n", p=128)

    P.dma(ident[:], cst['c_ident'], writes=['ident'])
    P.dma(ones[:], cst['c_ones'], writes=['ones'])
    P.dma(bdm[:], cst['c_bd'], writes=['bdm'])
    P.dma(mtr[:], cst['c_mtr'], writes=['mtr'])
    P.dma(mlm[:], cst['c_ml'], writes=['mlm'])
    P.dma(rst[:], cst['c_rst'], writes=['rst'])
    P.dma(lgam[:], cst['c_lgam'], writes=['lgam'])
    MSET('pool', zt[:], 0.0, ['zt'])
    ppo = {}
    ppn = [0]

    def pcol(name, vec_ap, n):
        off = ppn[0]
        ppn[0] += n
        assert ppn[0] <= NPC
        P.dma(pp[:, off:off + n], vec_ap.rearrange("(c p) -> p c", p=128), writes=[f'pp_{name}'],
              allow_slow_non_contiguous=True)
        ppo[name] = off
        return off

    for l in range(depth):
        pcol(f"g1_{l}", prm['mix_norm_g'][l], 8)
        pcol(f"g2_{l}", prm['ffn_norm_g'][l], 8)
        pcol(f"rgw_{l}", prm['ret_gn_w'][l], 3)
        pcol(f"rgb_{l}", prm['ret_gn_b'][l], 3)
        for f in range(6):
            for s in range(2):
                pcol(f"mu_{l}_{f}_{s}", prm['rwkv_mu'][l, f, s], 3)
        for d in range(2):
            pcol(f"w0_{l}_{d}", prm['rwkv_w0'][l, d], 3)
            pcol(f"a0_{l}_{d}", prm['rwkv_a0'][l, d], 3)
        pcol(f"kk_{l}", prm['rwkv_k_k'][l], 3)
        pcol(f"ka_{l}", prm['rwkv_k_a'][l], 3)
        pcol(f"rk_{l}", prm['rwkv_r_k'][l].rearrange("h n -> (h n)"), 3)
        pcol(f"lnw_{l}", prm['rwkv_ln_w'][l], 3)
        pcol(f"lnb_{l}", prm['rwkv_ln_b'][l], 3)
        off = ppn[0]
        ppn[0] += 1
        sw = prm['diff_subln_w'][l].rearrange("(c p) -> p c", p=64)
        P.dma(pp[0:64, off:off + 1], sw, writes=[f'ppsa{l}'], allow_slow_non_contiguous=True)
        P.dma(pp[64:128, off:off + 1], sw, writes=[f'ppsb{l}'], allow_slow_non_contiguous=True)
        ppo[f"sub_{l}"] = off
    pcol("gf", prm['final_norm_g'], 8)
    P.barrier()

    def pc(name, c=0):
        o = ppo[name] + c
        return pp[:, o:o + 1]

    wstate = {'s': 0, 'b': 0}

    def load_w(pieces, nrc, swap_from=None):
        si = wstate['s']
        wstate['s'] = (si + 1) % 2
        bi = wstate['b']
        wstate['b'] = (bi + 1) % 4
        off = 0
        names = []
        for i, ap in enumerate(pieces):
            n = ap.shape[1]
            P.dma(stg[si][:, 0:nrc, off:off + n], fm(ap), writes=[f'stg{si}_{i}'])
            names.append(f'stg{si}_{i}')
            off += n
        CP('pool', wbs[bi][:, 0:nrc, 0:off], stg[si][:, 0:nrc, 0:off], names, [f'wb{bi}'])
        return wbs[bi], f'wb{bi}', stg[si], names

    def xview(ap2d, s):
        return fm(ap2d)[:, :, s * T:(s + 1) * T]

    def rmsnorm(src, gname, s, out_dram=None):
        A.reset()
        sq = [A.alloc([128, 512], BF16) for _ in range(2)]
        rs = A.alloc([128, 512], F32)
        onesb = A.alloc([128, 128], BF16)
        xbs = [A.alloc([128, 8, 512], F32) for _ in range(2)]
        obs = [A.alloc([128, 8, 512], F32) for _ in range(2)] if out_dram is not None else None
        CP('pool', onesb, ones[:], [], ['onesb'])
        for b in range(4):
            tk = slice(b * 512, (b + 1) * 512)
            xb = xbs[b % 2]
            xbn = f'xb{b % 2}'
            P.dma(xb, src[:, :, tk], reads=[f'xr_{s}_{dc}_{b}' for dc in range(8)], writes=[xbn])
            for c in range(8):
                ACT(sq[c % 2], xb[:, c, :], AF.Square, [xbn], [f'sq{c % 2}'])
                MM(ps[b % 2][:], onesb, sq[c % 2], c == 0, c == 7, [f'sq{c % 2}', 'onesb'], [f'ps{b % 2}'])
            ACT(rs, ps[b % 2][:], AF.Sqrt, [f'ps{b % 2}'], ['rs'], bias=EPS, scale=1.0 / D)
            RCP(rs, rs, ['rs'], ['rs'])
            for c in range(8):
                if out_dram is None:
                    STT(hT[:, c, tk], xb[:, c, :], pc(gname, c), rs, ALU.mult, ALU.mult, [xbn, 'rs', 'pp'], ['hT'])
                else:
                    STT(obs[b % 2][:, c, :], xb[:, c, :], pc(gname, c), rs, ALU.mult, ALU.mult, [xbn, 'rs', 'pp'], [f'ob{b % 2}'])
            if out_dram is not None:
                tok = P.dma(out_dram[:, :, tk], obs[b % 2], reads=[f'ob{b % 2}'], queue='act')
                out_tokens.append(tok)
        P.barrier()

    out_tokens = []

    def proj_fm(psum_ap, psname, w, wname, c0, m, tk):
        for c in range(8):
            MM(psum_ap, w[:, c, c0:c0 + m], hT[:, c, tk], c == 0, c == 7, [wname, 'hT'], [psname])

    def proj_tm(psum_ap, psname, w, wname, c0, n, t):
        for c in range(8):
            MM(psum_ap, hT[:, c, t * 128:(t + 1) * 128], w[:, c, c0:c0 + n], c == 0, c == 7, [wname, 'hT'], [psname])

    def resid_add(psum_ap, psname, src, dst, s, dc, b, rt, rtname):
        tk = slice(b * 512, (b + 1) * 512)
        xn = f'xr_{s}_{dc}_{b}'
        P.dma(rt, src[:, dc, tk], reads=[xn], writes=[rtname])
        TT('dve', rt, psum_ap, rt, ALU.add, [psname, rtname], [rtname])
        P.dma(dst[:, dc, tk], rt, reads=[rtname], writes=[xn], queue='act')


    def xnames(s, b):
        return [f'xr_{s}_{dc}_{b}' for dc in range(8)]

    def out_proj(l, s, src, dst):
        A.reset()
        rts = [A.alloc([128, 512], F32) for _ in range(4)]
        k = 0
        for g in range(4):
            w, wn, _, _ = load_w([prm['w_out'][l][:, g * 256:(g + 1) * 256]], 8)
            for dci in range(2):
                dc = g * 2 + dci
                for b in range(4):
                    tk = slice(b * 512, (b + 1) * 512)
                    pb = ps[k % 4]
                    pn = f'ps{k % 4}'
                    for c in range(8):
                        MM(pb[:], w[:, c, dci * 128:(dci + 1) * 128], mT[:, c, tk], c == 0, c == 7, [wn, 'mT'], [pn])
                    resid_add(pb[:], pn, src, dst, s, dc, b, rts[k % 4], f'rt{k % 4}')
                    k += 1
        P.barrier()

    def ffn_phase(l, s, xr):
        TB = 2048
        for tb in range(T // TB):
            A.reset()
            hid = A.alloc([128, NFC, TB], BF16)
            sg = [A.alloc([128, 512], F32) for _ in range(2)]
            rts = [A.alloc([128, 512], F32) for _ in range(3)]
            k = 0
            for fc in range(NFC):
                if fc % 2 == 0:
                    wg_, wgn, _, _ = load_w([prm['w_gate'][l][:, fc * 128:(fc + 2) * 128]], 8)
                    wu_, wun, _, _ = load_w([prm['w_up'][l][:, fc * 128:(fc + 2) * 128]], 8)
                fo = (fc % 2) * 128
                for sb in range(TB // 512):
                    tk = slice(tb * TB + sb * 512, tb * TB + (sb + 1) * 512)
                    hk = slice(sb * 512, (sb + 1) * 512)
                    pg, pgn = ps[(2 * k) % 4], f'ps{(2 * k) % 4}'
                    pu, pun = ps[(2 * k + 1) % 4], f'ps{(2 * k + 1) % 4}'
                    proj_fm(pg[:], pgn, wg_, wgn, fo, 128, tk)
                    proj_fm(pu[:], pun, wu_, wun, fo, 128, tk)
                    ACT(sg[k % 2], pg[:], AF.Silu, [pgn], [f'sg{k % 2}'])
                    TT('dve', hid[:, fc, hk], pu[:], sg[k % 2], ALU.mult, [pun, f'sg{k % 2}'], ['hid'])
                    k += 1
            k = 0
            for g in range(4):
                ws = []
                for (r0, nrc) in ((0, 8), (8, 8), (16, 6)):
                    w, wn, _, _ = load_w([prm['w_down'][l][r0 * 128:(r0 + nrc) * 128, g * 256:(g + 1) * 256]], nrc)
                    ws.append((w, wn, r0, nrc))
                for dci in range(2):
                    dc = g * 2 + dci
                    for sb in range(TB // 512):
                        b = tb * (TB // 512) + sb
                        hk = slice(sb * 512, (sb + 1) * 512)
                        pb, pn = ps[4 + k % 3], f'ps{4 + k % 3}'
                        for (w, wn, r0, nrc) in ws:
                            for c in range(nrc):
                                fcx = r0 + c
                                MM(pb[:], w[:, c, dci * 128:(dci + 1) * 128], hid[:, fcx, hk], fcx == 0, fcx == NFC - 1,
                                   [wn, 'hid'], [pn])
                        resid_add(pb[:], pn, xr, xr, s, dc, b, rts[k % 3], f'rt{k % 3}')
                        k += 1
            P.barrier()

    def gn_alloc(nq):
        return {'sq': A.alloc([128, nq, 64], F32), 's1': A.alloc([128, nq], F32), 's2': A.alloc([128, nq], F32),
                'm2': A.alloc([128, nq], F32)}

    def gn_stats(y, yname, nq, eps, center, g):
        sq, s1, s2, m2 = g['sq'], g['s1'], g['s2'], g['m2']
        if center:
            RSUM(s1, y, [yname], ['gs1'])
            TS('dve', s1, s1, 1.0 / 64, None, ALU.mult, None, ['gs1'], ['gs1'])
            TT('dve', y, y, s1.unsqueeze(2).to_broadcast([128, nq, 64]), ALU.subtract, [yname, 'gs1'], [yname])
        TT('dve', sq, y, y, ALU.mult, [yname], ['gsq'])
        RSUM(s2, sq, ['gsq'], ['gs2'])
        ACT(m2, s2, AF.Sqrt, ['gs2'], ['gm2'], bias=eps, scale=1.0 / 64)
        RCP(m2, m2, ['gm2'], ['gm2'])
        TT('dve', y, y, m2.unsqueeze(2).to_broadcast([128, nq, 64]), ALU.mult, [yname, 'gm2'], [yname])

    def retention_phase(l, s):
        W = prm['w_in'][l]
        for hp in range(3):
            A.reset()
            rot = A.alloc([128, 2, T], F32)
            qT = A.alloc([128, T], BF16)
            kT = A.alloc([128, T], BF16)
            gT = A.alloc([128, T], BF16)
            vtm = A.alloc([128, 16, 128], BF16)
            t1 = A.alloc([128, 512], F32)
            t2 = A.alloc([128, 512], F32)
            wsw = A.alloc([128, 8, 256], BF16)
            if hp == 0:
                P.dma(rot, cst['c_rot'], writes=['rot'])
            c0 = hp * 128
            w, wn, sg_, sgn = load_w([W[:, c0:c0 + 128], W[:, 384 + c0:384 + c0 + 128]], 8)
            for j in range(8):
                src0 = j * 32
                dst0 = (j ^ 1) * 32
                CP('pool', wsw[:, :, dst0:dst0 + 32], sg_[:, :, src0:src0 + 32], sgn, ['wsw'])
            for which, dstT in ((0, qT), (1, kT)):
                for b in range(4):
                    tk = slice(b * 512, (b + 1) * 512)
                    proj_fm(ps[0][:], 'ps0', w, wn, which * 128, 128, tk)
                    proj_fm(ps[1][:], 'ps1', wsw, 'wsw', which * 128, 128, tk)
                    TT('dve', t1, ps[0][:], rot[:, 0, tk], ALU.mult, ['ps0', 'rot'], ['t1'])
                    TT('dve', t2, ps[1][:], rot[:, 1, tk], ALU.mult, ['ps1', 'rot'], ['t2'])
                    TT('pool', dstT[:, tk], t1, t2, ALU.add, ['t1', 't2'], ['qkT'])
            w2, wn2, _, _ = load_w([W[:, 768 + c0:768 + c0 + 128], W[:, 1152 + c0:1152 + c0 + 128]], 8)
            for t in range(16):
                pb, pn = ps[t % 2], f'ps{t % 2}'
                proj_tm(pb[:, 0:128], pn, w2, wn2, 0, 128, t)
                CP('act', vtm[:, t, :], pb[:, 0:128], [pn], ['vtm'])
            for b in range(4):
                tk = slice(b * 512, (b + 1) * 512)
                pb, pn = ps[2 + b % 2], f'ps{2 + b % 2}'
                proj_fm(pb[:], pn, w2, wn2, 128, 128, tk)
                ACT(gT[:, tk], pb[:], AF.Silu, [pn], ['gT'])
            mk = A.alloc([128, 3968], F32)
            pTs = [A.alloc([128, 512], BF16) for _ in range(4)]
            ypair = A.alloc([128, 16, 128], F32)
            tmp = A.alloc([128, 512], F32)
            gsc = gn_alloc(4)
            for hh in range(2):
                h = hp * 2 + hh
                hb = hh * 64
                P.op('pool', lambda e: e.iota(mk, [[1, 3968]], base=-1920, channel_multiplier=-1,
                                              allow_small_or_imprecise_dtypes=True), (), ['mk'])
                ACT(mk, mk, AF.Abs, ['mk'], ['mk'])
                ACT(mk, mk, AF.Exp, ['mk', 'lgam'], ['mk'], bias=float(np.log(0.125)), scale=lgam[:, h:h + 1])
                for qb in range(4):
                    q0 = qb * 512
                    MM(ps[4][:], zt[:, 0:128], zt[:, 128:640], True, True, ['zt'], ['ps4'])
                    NK = 16

                    def score(kt):
                        sb_, sn = ps[kt % 4], f'ps{kt % 4}'
                        MM(sb_[:], kT[hb:hb + 64, kt * 128:(kt + 1) * 128], qT[hb:hb + 64, q0:q0 + 512], True, True,
                           ['qkT'], [sn])
                        off = q0 - kt * 128 + 1920
                        TT('dve', pTs[kt % 4], sb_[:], mk[:, off:off + 512], ALU.mult, [sn, 'mk'], [f'pT{kt % 4}'])

                    def pv(kt):
                        for qi in range(4):
                            MM(ps[4][:, qi * 64:(qi + 1) * 64], pTs[kt % 4][:, qi * 128:(qi + 1) * 128],
                               vtm[:, kt, hb:hb + 64], False, kt == NK - 1 and qi == 3, [f'pT{kt % 4}', 'vtm'], ['ps4'])
                    score(0)
                    score(1)
                    score(2)
                    for kt in range(NK):
                        if kt + 3 < NK:
                            score(kt + 3)
                        pv(kt)
                    yv = ypair[:, qb * 4:(qb + 1) * 4, hb:hb + 64]
                    CP('act', yv, ps[4][:, 0:256].rearrange("p (a b) -> p a b", b=64), ['ps4'], ['ypair'])
                    gn_stats(yv, 'ypair', 4, 1e-5, True, gsc)
            for qb in range(4):
                q0 = qb * 512
                for qi in range(4):
                    TRN(ps[5][:, qi * 128:(qi + 1) * 128], ypair[:, qb * 4 + qi, :], ident[:], ['ypair'], ['ps5'])
                TS('dve', tmp, ps[5][:], pp[:, ppo[f'rgw_{l}'] + hp:ppo[f'rgw_{l}'] + hp + 1],
                   pp[:, ppo[f'rgb_{l}'] + hp:ppo[f'rgb_{l}'] + hp + 1], ALU.mult, ALU.add, ['ps5'], ['tmp'])
                TT('pool', mT[:, hp, q0:q0 + 512], tmp, gT[:, q0:q0 + 512], ALU.mult, ['tmp', 'gT'], ['mT'])
            P.barrier()

    def diff_setup():
        A.reset()
        oh = A.alloc([32, 4096], F32)
        rb = A.alloc([32, 4], F32)
        bv = A.alloc([4, 4096], F32)
        P.dma(oh, cst['c_onehot'], writes=['oh'])
        P.dma(rb, prm['rel_bias'], writes=['rb'])
        for j in range(8):
            MM(ps[0][0:4, :], rb, oh[:, j * 512:(j + 1) * 512], True, True, ['oh', 'rb'], ['ps0'])
            CP('dve', bv[:, j * 512:(j + 1) * 512], ps[0][0:4, :], ['ps0'], ['bv'])
        P.dma(bvec_d, bv, reads=['bv'], writes=['bvec_d'])
        P.barrier()

    def diff_phase(l, s):
        W = prm['w_in'][l]
        lam_init = 0.8 - 0.6 * float(np.exp(-0.3 * l))
        A.reset()
        lvt = A.alloc([128, 4, 32], F32)
        lp = A.alloc([128, 2, 32], F32)
        ls = A.alloc([128, 2], F32)
        nlam = A.alloc([128, 1], F32)
        P.dma(lvt, bass.AP(prm['diff_lambda'].tensor, l * 128, [[0, 128], [32, 4], [1, 32]]), writes=['lvt'])
        TT('dve', lp[:, 0, :], lvt[:, 0, :], lvt[:, 1, :], ALU.mult, ['lvt'], ['lp'])
        TT('dve', lp[:, 1, :], lvt[:, 2, :], lvt[:, 3, :], ALU.mult, ['lvt'], ['lp'])
        RSUM(ls, lp, ['lp'], ['ls'])
        ACT(ls, ls, AF.Exp, ['ls'], ['ls'])
        TT('dve', nlam, ls[:, 1:2], ls[:, 0:1], ALU.subtract, ['ls'], ['nlam'])
        TS('dve', nlam, nlam, -lam_init, None, ALU.add, None, ['nlam'], ['nlam'])
        qc = [A.alloc([128, T], BF16) for _ in range(2)]
        kT = A.alloc([128, T], BF16)
        vaug = A.alloc([128, 16, 65], BF16)
        MSET('pool', qc[0], 0.0, ['qkT'])
        MSET('pool', qc[1], 0.0, ['qkT'])
        MSET('pool', kT[64:128, :], 0.0, ['qkT'])
        bms = [A.alloc([128, 3968], F32) for _ in range(2)]
        tmps = [A.alloc([128, 512], F32) for _ in range(4)]
        pTs = [A.alloc([128, 512], BF16) for _ in range(4)]
        opair = A.alloc([128, 16, 128], F32)
        o1 = A.alloc([128, 4, 64], F32)
        rr = A.alloc([128, 2, 4], F32)
        gsc = gn_alloc(4)
        MSET('pool', vaug[:, :, 64:65], 1.0, ['vaug'])
        for h in range(4):
            hb = (h % 2) * 64
            mc = 6 + h // 2
            def dload(hx):
                wx = load_w([W[:, 3072 + hx * 64:3072 + (hx + 1) * 64], W[:, 3328 + hx * 64:3328 + (hx + 1) * 64],
                             W[:, 3584 + hx * 64:3584 + (hx + 1) * 64]], 8)
                P.dma(bms[hx % 2], bass.AP(bvec_d.tensor, hx * 4096, [[1, 128], [1, 3968]]), reads=['bvec_d'],
                      writes=[f'bm{hx % 2}'])
                return wx
            if h == 0:
                nxt = dload(0)
            w, wn, _, _ = nxt
            bm = bms[h % 2]
            bmn = f'bm{h % 2}'
            for which in (0, 1):
                for b in range(4):
                    tk = slice(b * 512, (b + 1) * 512)
                    pb, pn = ps[b % 2], f'ps{b % 2}'
                    proj_fm(pb[0:64, :], pn, w, wn, which * 64, 64, tk)
                    if which == 1:
                        CP('act', kT[0:64, tk], pb[0:64, :], [pn], ['qkT'])
                    else:
                        CP('act', qc[0][0:32, tk], pb[0:32, :], [pn], ['qkT'])
                        CP('act', qc[1][32:64, tk], pb[32:64, :], [pn], ['qkT'])
            for t in range(16):
                pb, pn = ps[t % 2], f'ps{t % 2}'
                proj_tm(pb[:, 0:64], pn, w, wn, 128, 64, t)
                CP('act', vaug[:, t, 0:64], pb[:, 0:64], [pn], ['vaug'])
            if h + 1 < 4:
                nxt = dload(h + 1)
            for qb in range(4):
                q0 = qb * 512
                MM(ps[4][:], zt[:, 0:128], zt[:, 128:640], True, True, ['zt'], ['ps4'])
                MM(ps[5][:], zt[:, 0:128], zt[:, 128:640], True, True, ['zt'], ['ps5'])
                NS = 32

                def score(i):
                    kt, c = divmod(i, 2)
                    sb_, sn = ps[i % 4], f'ps{i % 4}'
                    MM(sb_[:], kT[:, kt * 128:(kt + 1) * 128], qc[c][:, q0:q0 + 512], True, True, ['qkT'], [sn])
                    j0 = kt * 128 - q0 + 2047
                    bview = bm[:, j0 - 511:j0 + 1][:, ::-1]
                    STT(tmps[i % 4], sb_[:], float(32 ** -0.5), bview, ALU.mult, ALU.add, [sn, bmn], [f'tm{i % 4}'])
                    ACT(pTs[i % 4], tmps[i % 4], AF.Exp, [f'tm{i % 4}'], [f'pT{i % 4}'])

                def pv(i):
                    kt, c = divmod(i, 2)
                    acc, an = ps[4 + c], f'ps{4 + c}'
                    for qi in range(4):
                        MM(acc[:, qi * 65:(qi + 1) * 65], pTs[i % 4][:, qi * 128:(qi + 1) * 128], vaug[:, kt, :],
                           False, i >= NS - 2 and qi == 3, [f'pT{i % 4}', 'vaug'], [an])
                score(0)
                score(1)
                score(2)
                for i in range(NS):
                    if i + 3 < NS:
                        score(i + 3)
                    pv(i)
                a0 = ps[4][:, 0:260].rearrange("p (a b) -> p a b", b=65)
                a1 = ps[5][:, 0:260].rearrange("p (a b) -> p a b", b=65)
                RCP(rr[:, 0, :], a0[:, :, 64], ['ps4'], ['rr'])
                RCP(rr[:, 1, :], a1[:, :, 64], ['ps5'], ['rr'])
                TS('dve', rr[:, 1, :], rr[:, 1, :], nlam, None, ALU.mult, None, ['rr', 'nlam'], ['rr'])
                o0 = opair[:, qb * 4:(qb + 1) * 4, hb:hb + 64]
                TT('dve', o0, a0[:, :, 0:64], rr[:, 0, :].unsqueeze(2).to_broadcast([128, 4, 64]), ALU.mult, ['ps4', 'rr'], ['opair'])
                TT('dve', o1, a1[:, :, 0:64], rr[:, 1, :].unsqueeze(2).to_broadcast([128, 4, 64]), ALU.mult, ['ps5', 'rr'], ['o1'])
                TT('dve', o0, o0, o1, ALU.add, ['opair', 'o1'], ['opair'])
                gn_stats(o0, 'opair', 4, EPS, False, gsc)
            if h % 2 == 1:
                so = ppo[f'sub_{l}']
                for qb in range(4):
                    q0 = qb * 512
                    for qi in range(4):
                        TRN(ps[6][:, qi * 128:(qi + 1) * 128], opair[:, qb * 4 + qi, :], ident[:], ['opair'], ['ps6'])
                    TS('dve', mT[:, mc, q0:q0 + 512], ps[6][:], pp[:, so:so + 1], 1.0 - lam_init,
                       ALU.mult, ALU.mult, ['ps6'], ['mT'])
        P.barrier()

    def rwkv_phase(l, s):
        W = prm['w_in'][l]
        base = 1536
        A.reset()
        Fz = A.alloc([128, 3, T + 2], F32)
        xs = A.alloc([128, 3, T], BF16)
        hw = [A.alloc([128, T], BF16) for _ in range(2)]
        ha = [A.alloc([128, T], BF16) for _ in range(2)]
        hg = A.alloc([128, T], BF16)
        w1b = A.alloc([128, 2, 3, 64], BF16)
        a1b = A.alloc([128, 2, 3, 64], BF16)
        g1b = A.alloc([128, 3, 128], BF16)
        w2b = A.alloc([64, 2, 384], BF16)
        a2b = A.alloc([64, 2, 384], BF16)
        g2b = A.alloc([128, 384], BF16)
        lst = A.alloc([128, 1536], F32)
        cm = A.alloc([128, 6, 3], F32)
        ka1 = A.alloc([128, 3], F32)
        ka2 = A.alloc([128, 3], F32)
        t1 = A.alloc([128, 512], F32)
        t2 = A.alloc([128, 512], F32)
        tsh = [A.alloc([128, 512], F32) for _ in range(2)]
        lwt = [A.alloc([128, 512], F32) for _ in range(2)]
        ast = [A.alloc([128, 512], BF16) for _ in range(2)]
        gtt = [A.alloc([128, 512], BF16) for _ in range(2)]
        for d in range(2):
            P.dma(lst[:, 0:192].rearrange("p (c r) -> p c r", r=64), fm(prm['rwkv_w1'][l, d]), writes=['lst'])
            CP('pool', w1b[:, d], lst[:, 0:192].rearrange("p (c r) -> p c r", r=64), ['lst'], ['lw8'])
            P.dma(lst[:, 192:384].rearrange("p (c r) -> p c r", r=64), fm(prm['rwkv_a1'][l, d]), writes=['lst2'])
            CP('pool', a1b[:, d], lst[:, 192:384].rearrange("p (c r) -> p c r", r=64), ['lst2'], ['lw8'])
            P.dma(lst[0:64, 384:768], prm['rwkv_w2'][l, d], writes=['lst3'])
            CP('pool', w2b[:, d, :], lst[0:64, 384:768], ['lst3'], ['lw8'])
            P.dma(lst[0:64, 768:1152], prm['rwkv_a2'][l, d], writes=['lst4'])
            CP('pool', a2b[:, d, :], lst[0:64, 768:1152], ['lst4'], ['lw8'])
        P.dma(lst[:, 0:384].rearrange("p (c r) -> p c r", r=128), fm(prm['rwkv_g1'][l]), writes=['lst', 'lst2'])
        CP('pool', g1b, lst[:, 0:384].rearrange("p (c r) -> p c r", r=128), ['lst', 'lst2'], ['lw8'])
        P.dma(lst[:, 1152:1536], prm['rwkv_g2'][l], writes=['lst5'])
        CP('pool', g2b, lst[:, 1152:1536], ['lst5'], ['lw8'])
        for f in range(6):
            o0 = ppo[f'mu_{l}_{f}_0']
            o1 = ppo[f'mu_{l}_{f}_1']
            TT('dve', cm[:, f, :], pp[:, o0:o0 + 3], pp[:, o1:o1 + 3], ALU.add, [], ['cm'])
        TS('dve', cm, cm, -1.0, 1.0, ALU.mult, ALU.add, ['cm'], ['cm'])
        oka = ppo[f'ka_{l}']
        TS('dve', ka1, pp[:, oka:oka + 3], -1.0, 1.0, ALU.mult, ALU.add, [], ['ka1'])
        TS('dve', ka2, pp[:, oka:oka + 3], -2.0, 2.0, ALU.mult, ALU.add, [], ['ka2'])

        def shift_mix(Fv, f, j, dst, dname, fname):
            o0 = ppo[f'mu_{l}_{f}_0'] + j
            o1 = ppo[f'mu_{l}_{f}_1'] + j
            for b in range(4):
                b0 = b * 512
                TS('dve', t1, Fv[:, 1 + b0:1 + b0 + 512], cm[:, f, j:j + 1], None, ALU.mult, None, [fname, 'cm'], ['t1'])
                STT(t2, Fv[:, b0:b0 + 512], pp[:, o0:o0 + 1], t1, ALU.mult, ALU.add, [fname, 't1'], ['t2'])
                STT(dst[:, b0:b0 + 512], Fv[:, 2 + b0:2 + b0 + 512], pp[:, o1:o1 + 1], t2, ALU.mult, ALU.add,
                    [fname, 't2'], [dname])

        def proj_to_F(Fv, fname, w, wn, c0):
            for b in range(4):
                tk = slice(b * 512, (b + 1) * 512)
                pb, pn = ps[b % 2], f'ps{b % 2}'
                proj_fm(pb[:], pn, w, wn, c0, 128, tk)
                CP('act', Fv[:, 1 + b * 512:1 + (b + 1) * 512], pb[:], [pn], [fname])

        MSET('pool', Fz[:, :, 0:1], 0.0, ['Fz'])
        MSET('pool', Fz[:, :, T + 1:T + 2], 0.0, ['Fz'])
        for j in range(3):
            w, wn, _, _ = load_w([W[:, base + 1152 + j * 128:base + 1152 + (j + 1) * 128]], 8)
            proj_to_F(Fz[:, j, :], 'Fz', w, wn, 0)
        for (f, kind) in ((3, 'w'), (4, 'a'), (5, 'g')):
            for j in range(3):
                shift_mix(Fz[:, j, :], f, j, xs[:, j, :], 'xs', 'Fz')
            for b in range(4):
                tk = slice(b * 512, (b + 1) * 512)
                if kind == 'g':
                    for j in range(3):
                        MM(ps[2][:], g1b[:, j, :], xs[:, j, tk], j == 0, j == 2, ['xs', 'lw8'], ['ps2'])
                    ACT(hg[:, tk], ps[2][:], AF.Sigmoid, ['ps2'], ['hg'])
                else:
                    for d in range(2):
                        pb, pn = ps[2 + d], f'ps{2 + d}'
                        wl = w1b if kind == 'w' else a1b
                        for j in range(3):
                            MM(pb[0:64, :], wl[:, d, j, :], xs[:, j, tk], j == 0, j == 2, ['xs', 'lw8'], [pn])
                        if kind == 'w':
                            ACT(hw[d][0:64, tk], pb[0:64, :], AF.Tanh, [pn], ['hw'])
                        else:
                            CP('act', ha[d][0:64, tk], pb[0:64, :], [pn], ['ha'])
        k = 0
        for j in range(3):
            cj = slice(j * 128, (j + 1) * 128)
            for b in range(4):
                tk = slice(b * 512, (b + 1) * 512)
                for d in range(2):
                    pb, pn = ps[k % 4], f'ps{k % 4}'
                    MM(pb[:], w2b[:, d, cj], hw[d][0:64, tk], True, True, ['hw', 'lw8'], [pn])
                    ACT(lwt[k % 2], pb[:], AF.Sigmoid, [pn], [f'lwt{k % 2}'], bias=pc(f'w0_{l}_{d}', j))
                    TS('dve', lwt[k % 2], lwt[k % 2], -LOG_E05, None, ALU.mult, None, [f'lwt{k % 2}'], [f'lwt{k % 2}'])
                    P.dma(lw_d[d, cj, tk], lwt[k % 2], reads=[f'lwt{k % 2}'], writes=[f'lw_{d}_{j}_{b}'])
                    k += 1
                    pb, pn = ps[k % 4], f'ps{k % 4}'
                    MM(pb[:], a2b[:, d, cj], ha[d][0:64, tk], True, True, ['ha', 'lw8'], [pn])
                    ACT(ast[k % 2], pb[:], AF.Sigmoid, [pn], [f'ast{k % 2}'], bias=pc(f'a0_{l}_{d}', j))
                    P.dma(as_d[d, cj, tk], ast[k % 2], reads=[f'ast{k % 2}'], writes=[f'as_{d}_{j}_{b}'])
                    k += 1
                pb, pn = ps[k % 4], f'ps{k % 4}'
                MM(pb[:], g2b[:, cj], hg[:, tk], True, True, ['hg', 'lw8'], [pn])
                CP('act', gtt[k % 2], pb[:], [pn], [f'gtt{k % 2}'])
                P.dma(gate_d[cj, tk], gtt[k % 2], reads=[f'gtt{k % 2}'], writes=[f'gate_{j}_{b}'])
                k += 1
        P.barrier()

        for j in range(3):
            cj = slice(j * 128, (j + 1) * 128)
            A.reset()
            xr = A.alloc([128, T], BF16)
            xk = A.alloc([128, T], BF16)
            xv = A.alloc([128, T], BF16)
            Vtm = A.alloc([128, 16, 128], BF16)
            ysum = A.alloc([128, 16, 128], F32)
            kkn = A.alloc([128, T], BF16)
            bonus = A.alloc([128, T], BF16)
            ka1 = A.alloc([128, 3], F32)
            identb = A.alloc([128, 128], BF16)
            gsc = gn_alloc(8)
            gtile = A.alloc([128, 512], BF16)
            te1 = A.alloc([128, 512], F32)
            mark = A.off
            Fv = A.alloc([128, T + 2], F32)
            cm = A.alloc([128, 6, 3], F32)
            ka2 = A.alloc([128, 3], F32)
            t1 = A.alloc([128, 512], F32)
            t2 = A.alloc([128, 512], F32)
            tsh = [A.alloc([128, 512], F32) for _ in range(2)]
            as0 = A.alloc([128, 512], BF16)
            as1 = A.alloc([128, 512], BF16)
            CP('pool', identb, ident[:], [], ['identb'])
            for f in range(6):
                o0 = ppo[f'mu_{l}_{f}_0']
                o1 = ppo[f'mu_{l}_{f}_1']
                TT('dve', cm[:, f, :], pp[:, o0:o0 + 3], pp[:, o1:o1 + 3], ALU.add, [], ['cm'])
            TS('dve', cm, cm, -1.0, 1.0, ALU.mult, ALU.add, ['cm'], ['cm'])
            TS('dve', ka1, pp[:, oka:oka + 3], -1.0, 1.0, ALU.mult, ALU.add, [], ['ka1'])
            TS('dve', ka2, pp[:, oka:oka + 3], -2.0, 2.0, ALU.mult, ALU.add, [], ['ka2'])
            MSET('pool', Fv[:, 0:1], 0.0, ['Fv'])
            MSET('pool', Fv[:, T + 1:T + 2], 0.0, ['Fv'])
            w, wn, _, _ = load_w([W[:, base + j * 128:base + (j + 1) * 128],
                                  W[:, base + 384 + j * 128:base + 384 + (j + 1) * 128]], 8)
            wv_, wvn, _, _ = load_w([W[:, base + 768 + j * 128:base + 768 + (j + 1) * 128]], 8)
            proj_to_F(Fv, 'Fv', w, wn, 0)
            shift_mix(Fv, 0, j, xr, 'xr', 'Fv')
            proj_to_F(Fv, 'Fv', w, wn, 128)
            shift_mix(Fv, 1, j, xk, 'xk', 'Fv')
            proj_to_F(Fv, 'Fv', wv_, wvn, 0)
            shift_mix(Fv, 2, j, xv, 'xv', 'Fv')
            okk = ppo[f'kk_{l}'] + j
            ork = ppo[f'rk_{l}'] + j
            for b in range(4):
                tk = slice(b * 512, (b + 1) * 512)
                TS('dve', t1, xk[:, tk], pp[:, okk:okk + 1], None, ALU.mult, None, ['xk'], ['t1'])
                ACT(t2, t1, AF.Square, ['t1'], ['t2'])
                MM(ps[0][:], bdm[:], t2, True, True, ['t2'], ['ps0'])
                ACT(t2, ps[0][:], AF.Sqrt, ['ps0'], ['t2'])
                TS('dve', t2, t2, 1e-12, None, ALU.max, None, ['t2'], ['t2'])
                RCP(t2, t2, ['t2'], ['t2'])
                TT('dve', kkn[:, tk], t1, t2, ALU.mult, ['t1', 't2'], ['kkn'])
                P.dma(as0, as_d[0, cj, tk], reads=[f'as_0_{j}_{b}'], writes=['as0'])
                P.dma(as1, as_d[1, cj, tk], reads=[f'as_1_{j}_{b}'], writes=['as1'])
                TT('pool', t1, as0, as1, ALU.add, ['as0', 'as1'], ['t1'])
                TS('dve', t1, t1, pp[:, oka + j:oka + j + 1], ka2[:, j:j + 1], ALU.mult, ALU.add, ['t1', 'ka2'], ['t1'])
                TT('dve', t1, t1, xk[:, tk], ALU.mult, ['t1', 'xk'], ['t1'])
                TT('dve', t1, t1, xr[:, tk], ALU.mult, ['t1', 'xr'], ['t1'])
                TS('dve', t2, t1, pp[:, ork:ork + 1], None, ALU.mult, None, ['t1'], ['t2'])
                MM(ps[1][:], bdm[:], t2, True, True, ['t2'], ['ps1'])
                TT('dve', bonus[:, tk], ps[1][:], xv[:, tk], ALU.mult, ['ps1', 'xv'], ['bonus'])
            for t in range(16):
                TRN(psb[:, (t % 8) * 128:(t % 8 + 1) * 128], xv[:, t * 128:(t + 1) * 128], identb, ['xv', 'identb'], ['ps7'])
                if t % 8 == 7:
                    g8 = t // 8
                    CP('act', Vtm[:, g8 * 8:(g8 + 1) * 8, :], psb.rearrange("p (a b) -> p a b", b=128), ['ps7'], ['Vtm'])
            MSET('pool', ysum, 0.0, ['ysum'])
            P.barrier()
            A.off = mark

            def unit_alloc():
                u = {}
                for nm in ('lwt', 'cl', 'gi', 'gv', 't1', 't2'):
                    u[nm] = A.alloc([128, 512], F32)
                for nm in ('as0', 'bb', 'Bt', 'Bbar', 'kd', 'Kt', 'Kbar'):
                    u[nm] = A.alloc([128, 512], BF16)
                u['AR'] = A.alloc([128, 2, 512], BF16)
                u['BKz'] = A.alloc([128, 16, 128], BF16).rearrange("p (a b c) d -> p a b c d", b=2, c=2)
                u['G'] = [A.alloc([128, 512], BF16) for _ in range(2)]
                u['Nm'] = A.alloc([128, 256], BF16)
                u['MM2'] = [A.alloc([128, 512], BF16) for _ in range(2)]
                u['Mb'] = [m_[:, 0:256] for m_ in u['MM2']]
                u['MTb'] = [m_[:, 256:512] for m_ in u['MM2']]
                u['PTb'] = [A.alloc([128, 256], BF16) for _ in range(2)]
                u['RHSb'] = A.alloc([128, 128], BF16)
                u['Ub'] = A.alloc([128, 128], BF16)
                u['Hf'] = A.alloc([128, 64], F32)
                u['Hs'] = A.alloc([128, 64], BF16)
                return u

            def unit(d, u):
                n = lambda s_: f'{s_}_{d}'
                lwt_, cl, gi, gv, t1, t2 = u['lwt'], u['cl'], u['gi'], u['gv'], u['t1'], u['t2']
                as0, bb, Bt, Bbar, kd, Kt, Kbar = u['as0'], u['bb'], u['Bt'], u['Bbar'], u['kd'], u['Kt'], u['Kbar']
                AR, BKz, G, Nm, Mb, MTb, PTb = u['AR'], u['BKz'], u['G'], u['Nm'], u['Mb'], u['MTb'], u['PTb']
                RHSb, Ub, Hf, Hs = u['RHSb'], u['Ub'], u['Hf'], u['Hs']
                Xb, Yb, Zb = ps[3 * d], ps[3 * d + 1], ps[3 * d + 2]
                pA = [Xb[:, 0:256], Yb[:, 0:256]]
                pB = [Xb[:, 256:512], Yb[:, 256:512]]
                pAn = [f'ps{3 * d}', f'ps{3 * d + 1}']
                pBn = pAn
                pD, pE = Zb[:, 0:256], Zb[:, 256:512]
                pDn, pEn = f'ps{3 * d + 2}', f'ps{3 * d + 2}'
                pFh = [Xb[:, 384:512], Yb[:, 384:512]]
                pU, pS = Xb[:, 256:384], Yb[:, 256:320]
                pUn, pSn = pAn[0], pAn[1]
                pT = psb
                pTn = 'ps7'
                MSET('pool', Hf, 0.0, [n('Hf')])
                MSET('pool', Hs, 0.0, [n('Hs')])
                MSET('pool', BKz, 0.0, [n('BKz')])
                for bi in range(4):
                    b = bi if d == 0 else 3 - bi
                    tk = slice(b * 512, (b + 1) * 512)
                    P.dma(lwt_, lw_d[d, cj, tk], reads=[f'lw_{d}_{j}_{b}'], writes=[n('lwt')])
                    P.dma(as0, as_d[d, cj, tk], reads=[f'as_{d}_{j}_{b}'], writes=[n('as0')])
                    yield
                    if d == 0:
                        P.op('dve', lambda e, cl=cl, lwt_=lwt_: e.tensor_tensor_scan(cl, rst[:], lwt_, 0.0, ALU.mult, ALU.add),
                             [n('lwt')], [n('cl')])
                    else:
                        P.op('dve', lambda e, cl=cl, lwt_=lwt_: e.tensor_tensor_scan(cl[:, ::-1], rst[:], lwt_[:, ::-1], 0.0,
                                                                                    ALU.mult, ALU.add), [n('lwt')], [n('cl')])
                    cl3 = cl.rearrange("p (a b) -> p a b", b=128)
                    tot = cl3[:, :, 127:128] if d == 0 else cl3[:, :, 0:1]
                    ACT(gi, cl, AF.Exp, [n('cl')], [n('gi')])
                    ACT(gv, cl, AF.Exp, [n('cl')], [n('gv')], scale=-1.0)
                    TT('pool', t1, cl, lwt_, ALU.subtract, [n('cl'), n('lwt')], [n('t1')])
                    ACT(t1, t1, AF.Exp, [n('t1')], [n('t1')])
                    TT('dve', t2.rearrange("p (a b) -> p a b", b=128), cl3, tot.to_broadcast([128, 4, 128]), ALU.subtract,
                       [n('cl')], [n('t2')])
                    ACT(t2, t2, AF.Exp, [n('t2')], [n('t2')], scale=-1.0)
                    yield
                    STT(AR[:, 0, :], kkn[:, tk], -1.0, t1, ALU.mult, ALU.mult, ['kkn', n('t1')], [n('AR')])
                    TT('pool', AR[:, 1, :], xr[:, tk], gi, ALU.mult, ['xr', n('gi')], [n('AR')])
                    TT('pool', bb, kkn[:, tk], as0, ALU.mult, ['kkn', n('as0')], [n('bb')])
                    TT('dve', Bt, bb, gv, ALU.mult, [n('bb'), n('gv')], [n('Bt')])
                    TT('pool', Bbar, bb, t2, ALU.mult, [n('bb'), n('t2')], [n('Bbar')])
                    TS('dve', kd, as0, pp[:, oka + j:oka + j + 1], ka1[:, j:j + 1], ALU.mult, ALU.add, [n('as0'), 'ka1'], [n('kd')])
                    TT('pool', kd, kd, xk[:, tk], ALU.mult, [n('kd'), 'xk'], [n('kd')])
                    TT('dve', Kt, kd, gv, ALU.mult, [n('kd'), n('gv')], [n('Kt')])
                    TT('pool', Kbar, kd, t2, ALU.mult, [n('kd'), n('t2')], [n('Kbar')])
                    yield
                    for w_, src_, sn_ in ((0, Bbar, n('Bbar')), (1, Kbar, n('Kbar'))):
                        for ci in range(4):
                            TRN(pT[:, (w_ * 4 + ci) * 128:(w_ * 4 + ci + 1) * 128], src_[:, ci * 128:(ci + 1) * 128], identb,
                                [sn_, 'identb'], [pTn])
                    for w_ in range(2):
                        for ci in range(4):
                            for hh in range(2):
                                c0_ = (w_ * 4 + ci) * 128 + hh * 64
                                CP('act', BKz[:, ci, w_, hh, hh * 64:(hh + 1) * 64], pT[:, c0_:c0_ + 64],
                                   [pTn], [n('BKz')])
                    yield
                    for cii in range(4):
                        ci = cii if d == 0 else 3 - cii
                        cc = slice(ci * 128, (ci + 1) * 128)
                        cg = b * 4 + ci
                        for hh in range(2):
                            hb = hh * 64
                            MM(pA[hh], Bt[hb:hb + 64, cc], AR[hb:hb + 64, :, cc], True, True, [n('Bt'), n('AR')], [pAn[hh]])
                            MM(pB[hh], Kt[hb:hb + 64, cc], AR[hb:hb + 64, :, cc], True, True, [n('Kt'), n('AR')], [pBn[hh]])
                        yield
                        for hh in range(2):
                            TT('dve', G[hh][:, 0:256], pA[hh], mtr[:, d, 0:256], ALU.mult, [pAn[hh]], [n(f'G{hh}')])
                            TT('dve', G[hh][:, 256:512], pB[hh], mtr[:, d, 0:256], ALU.mult, [pBn[hh]], [n(f'G{hh}')])
                        for hh in range(2):
                            hb = hh * 64
                            MM(pA[hh][:, 0:128], AR[hb:hb + 64, 0, cc], Bt[hb:hb + 64, cc], True, True,
                               [n('Bt'), n('AR')], [pAn[hh]])
                        yield
                        for hh in range(2):
                            TT('dve', Nm[:, hh * 128:(hh + 1) * 128], pA[hh][:, 0:128], mlm[:, d, 0:128], ALU.mult,
                               [pAn[hh]], [n('Nm')])
                        NTh = [G[hh][:, 0:128] for hh in range(2)]
                        ARBh = [G[hh][:, 128:256] for hh in range(2)]
                        AKh = [G[hh][:, 256:384] for hh in range(2)]
                        ARKh = [G[hh][:, 384:512] for hh in range(2)]
                        for hh in range(2):
                            TT('dve', PTb[0][:, hh * 128:(hh + 1) * 128], NTh[hh], ident[:], ALU.add, [n(f'G{hh}')], [n('PTb0')])
                        curM = [Nm[:, hh * 128:(hh + 1) * 128] for hh in range(2)]
                        curMn = [n('Nm')] * 2
                        curMT = NTh
                        curMTn = [n('G0'), n('G1')]
                        pcur = 0
                        yield
                        for lev in range(1, 7):
                            sl_ = lev % 2
                            for hh in range(2):
                                MM(pD[:, hh * 128:(hh + 1) * 128], curMT[hh], curM[hh], True, True,
                                   [curMn[hh], curMTn[hh]], [pDn])
                            if lev < 6:
                                for hh in range(2):
                                    MM(pE[:, hh * 128:(hh + 1) * 128], curM[hh], curMT[hh], True, True,
                                       [curMn[hh], curMTn[hh]], [pEn])
                            yield
                            if lev < 6:
                                CP('act', u['MM2'][sl_], Zb[:], [pDn], [n(f'Mb{sl_}'), n(f'MTb{sl_}')])
                            else:
                                CP('act', Mb[sl_], pD, [pDn], [n(f'Mb{sl_}')])
                            curM = [Mb[sl_][:, hh * 128:(hh + 1) * 128] for hh in range(2)]
                            curMn = [n(f'Mb{sl_}')] * 2
                            curMT = [MTb[sl_][:, hh * 128:(hh + 1) * 128] for hh in range(2)]
                            curMTn = [n(f'MTb{sl_}')] * 2
                            for hh in range(2):
                                MM(pFh[hh], curM[hh], PTb[pcur][:, hh * 128:(hh + 1) * 128], True, True,
                                   [curMn[hh], n(f'PTb{pcur}')], [pAn[hh]])
                            yield
                            for hh in range(2):
                                TT('dve', PTb[1 - pcur][:, hh * 128:(hh + 1) * 128], pFh[hh], PTb[pcur][:, hh * 128:(hh + 1) * 128],
                                   ALU.add, [pAn[hh], n(f'PTb{pcur}')], [n(f'PTb{1 - pcur}')])
                            pcur = 1 - pcur
                        PTc = PTb[pcur]
                        PTn = n(f'PTb{pcur}')
                        for hh in range(2):
                            hb = hh * 64
                            vs = slice(hh * 64, (hh + 1) * 64)
                            o = pA[hh][:, 128:192]
                            MM(o, AR[hb:hb + 64, 0, cc], Hs[hb:hb + 64, :], True, False, [n('AR'), n('Hs')], [pAn[hh]])
                            MM(o, AKh[hh], Vtm[:, cg, vs], False, True, [n(f'G{hh}'), 'Vtm'], [pAn[hh]])
                        yield
                        CP('act', RHSb[:, 0:64], pA[0][:, 128:192], [pAn[0]], [n('RHSb')])
                        CP('dve', RHSb[:, 64:128], pA[1][:, 128:192], [pAn[1]], [n('RHSb')])
                        for hh in range(2):
                            vs = slice(hh * 64, (hh + 1) * 64)
                            MM(pU[:, hh * 64:(hh + 1) * 64], PTc[:, hh * 128:(hh + 1) * 128], RHSb[:, vs], True, True,
                               [PTn, n('RHSb')], [pUn])
                        yield
                        CP('dve', Ub, pU, [pUn], [n('Ub')])
                        for hh in range(2):
                            hb = hh * 64
                            vs = slice(hh * 64, (hh + 1) * 64)
                            o = pA[hh][:, 192:256]
                            MM(o, AR[hb:hb + 64, 1, cc], Hs[hb:hb + 64, :], True, False, [n('AR'), n('Hs')], [pAn[hh]])
                            MM(o, ARBh[hh], Ub[:, vs], False, False, [n(f'G{hh}'), n('Ub')], [pAn[hh]])
                            MM(o, ARKh[hh], Vtm[:, cg, vs], False, True, [n(f'G{hh}'), 'Vtm'], [pAn[hh]])
                        o = pS
                        for hh in range(2):
                            vs = slice(hh * 64, (hh + 1) * 64)
                            MM(o, BKz[:, ci, 0, hh, :], Ub[:, vs], hh == 0, False, [n('BKz'), n('Ub')], [pSn])
                            MM(o, BKz[:, ci, 1, hh, :], Vtm[:, cg, vs], False, hh == 1, [n('BKz'), 'Vtm'], [pSn])
                        yield
                        for hh in range(2):
                            vs = slice(hh * 64, (hh + 1) * 64)
                            TT('dve', ysum[:, cg, vs], pA[hh][:, 192:256], ysum[:, cg, vs], ALU.add,
                               [pAn[hh], f'ysum{cg}'], [f'ysum{cg}'])
                        gcol = gi[:, ci * 128 + 127:ci * 128 + 128] if d == 0 else gi[:, ci * 128:ci * 128 + 1]
                        STT(Hf, Hf, gcol, pS, ALU.mult, ALU.add, [n('Hf'), n('gi'), pSn], [n('Hf')])
                        CP('act', Hs, Hf, [n('Hf')], [n('Hs')])
                        yield

            units = [unit(d, unit_alloc()) for d in range(2)]
            active = list(units)
            first = True
            import os
            if os.environ.get('RW_SERIAL'):
                for g_ in units:
                    for _ in g_:
                        pass
                active = []
            NDUM = int(os.environ.get('RW_DUMMY', '0'))
            while active:
                for g_ in list(active):
                    try:
                        next(g_)
                    except StopIteration:
                        active.remove(g_)
                    for _ in range(NDUM):
                        MM(ps[6][:], zt[:, 0:128], zt[:, 128:640], True, True, ['zt'], ['ps6'])
            P.barrier()
            for g4 in range(4):
                tk = slice(g4 * 512, (g4 + 1) * 512)
                y8 = ysum[:, g4 * 4:(g4 + 1) * 4, :].rearrange("p a (h n) -> p (a h) n", n=64)
                gn_stats(y8, 'ysum', 8, 64e-5, True, gsc)
                for qi in range(4):
                    TRN(ps[3][:, qi * 128:(qi + 1) * 128], ysum[:, g4 * 4 + qi, :], ident[:], ['ysum'], ['ps3'])
                olw = ppo[f'lnw_{l}'] + j
                olb = ppo[f'lnb_{l}'] + j
                TS('dve', te1, ps[3][:], pp[:, olw:olw + 1], pp[:, olb:olb + 1], ALU.mult, ALU.add, ['ps3'], ['te1'])
                TT('pool', te1, te1, bonus[:, tk], ALU.add, ['te1', 'bonus'], ['te1'])
                P.dma(gtile, gate_d[cj, tk], reads=[f'gate_{j}_{g4}'], writes=['gtile'])
                TT('dve', mT[:, 3 + j, tk], te1, gtile, ALU.mult, ['te1', 'gtile'], ['mT'])
            P.barrier()

    diff_setup()
    for s in range(nseq):
        for l in range(depth):
            src = xview(xT_in, s) if l == 0 else xview(xres, s)
            dst = xview(xres, s)
            rmsnorm(src, f'g1_{l}', s)
            if 'ret' in mixers:
                retention_phase(l, s)
            else:
                MSET('pool', mT[:, 0:3, :], 0.0, ['mT'])
            if 'rwkv' in mixers:
                rwkv_phase(l, s)
            else:
                MSET('pool', mT[:, 3:6, :], 0.0, ['mT'])
            if 'diff' in mixers:
                diff_phase(l, s)
            else:
                MSET('pool', mT[:, 6:8, :], 0.0, ['mT'])
            P.barrier()
            out_proj(l, s, src, dst)
            if ffn:
                rmsnorm(dst, f'g2_{l}', s)
                ffn_phase(l, s, dst)
        rmsnorm(xview(xres, s), 'gf', s, out_dram=xview(outT, s))
    P.wait_all_outputs(out_tokens)
    P.finish()
    return nc, P


_CACHE = {}


def kernel(**inputs):
    x = np.asarray(inputs['x'], dtype=np.float32)
    consts = host_consts()
    if 'nc' not in _CACHE:
        _CACHE['nc'] = build()[0]
    nc = _CACHE['nc']
    params = {k: np.ascontiguousarray(np.asarray(inputs[k], dtype=np.float32)) for k in PARAM_SHAPES}
    in_maps = []
    for c in range(8):
        xs = np.ascontiguousarray(x[2 * c:2 * c + 2].reshape(2 * T, D).T)
        m = {'xT': xs}
        m.update(params)
        m.update(consts)
        in_maps.append(m)
    res = run_bass_kernel_spmd(nc, in_maps, core_ids=list(range(8)))
    out = np.empty((16, T, D), np.float32)
    for c in range(8):
        o = np.asarray(res.results[c]['outT'])
        out[2 * c:2 * c + 2] = o.T.reshape(2, T, D)
    return out
```
